# Optimizing a Trainium2 kernel written in Bass

```python
import math
import jax, jax.numpy as jnp
from jax import lax
import numpy as np

D_MODEL = 1024
BATCH = 8
SEQ = 4096
DEPTH = 2

SC_GROUPS = 8
SC_WIDTH = 512
SC_WIDTH_CONV = 3
CF_GROUPS = 8
CF_WIDTH = 512
CF_WIDTH_CONV = 31
EV_IN = 3 * SC_WIDTH + 2 * CF_WIDTH
EV_OUT_IN = SC_WIDTH + CF_WIDTH

LRU_HEADS = 8
LRU_WIDTH = 512
LRU_BLOCK = LRU_WIDTH // LRU_HEADS
LRU_CONV = 4
LRU_C = 8.0
DIL_PATTERNS = ((128, 1), (512, 4), (2048, 16))
N_DIL_GROUPS = len(DIL_PATTERNS)
DIL_HEADS = 4
DIL_HEAD_DIM = 64
DIL_OUT = DIL_HEADS * DIL_HEAD_DIM
DIL_QKV = N_DIL_GROUPS * DIL_OUT
ATTN_BLOCK = 128
OD_IN = 2 * LRU_WIDTH + 3 * DIL_QKV
OD_OUT_IN = LRU_WIDTH + DIL_OUT

D_FF = 2816
FFN_CONV = 3

N_EVEN = (DEPTH + 1) // 2
N_ODD = DEPTH // 2
EPS = 1e-6

kernel_name = "hybrid_conv_lru_dilated_trunk"


def rmsnorm(x, g):
    xf = x.astype(jnp.float32)
    y = xf * lax.rsqrt(jnp.mean(xf * xf, axis=-1, keepdims=True) + EPS)
    return (y * g.astype(jnp.float32)).astype(x.dtype)


def layernorm(x, g, b):
    xf = x.astype(jnp.float32)
    mu = jnp.mean(xf, axis=-1, keepdims=True)
    var = jnp.mean(jnp.square(xf - mu), axis=-1, keepdims=True)
    y = (xf - mu) * lax.rsqrt(var + EPS) * g.astype(jnp.float32) + b.astype(jnp.float32)
    return y.astype(x.dtype)


def causal_dwconv(x, w):
    K, C = w.shape
    return lax.conv_general_dilated(
        x, w[:, None, :].astype(x.dtype), window_strides=(1,), padding=[(K - 1, 0)],
        dimension_numbers=("NWC", "WIO", "NWC"), feature_group_count=C)


def even_mixer(h, w_in, sc_conv_w, cf_conv_w, cf_conv_b, cf_ln_g, cf_ln_b, w_out):
    z = h @ w_in
    sc_b, sc_c, sc_x, cf_a, cf_g = jnp.split(
        z, [SC_WIDTH, 2 * SC_WIDTH, 3 * SC_WIDTH, 3 * SC_WIDTH + CF_WIDTH], axis=-1)
    y_sc = sc_b * causal_dwconv(sc_c * sc_x, sc_conv_w)
    u = cf_a * jax.nn.sigmoid(cf_g)
    u = causal_dwconv(u, cf_conv_w) + cf_conv_b
    u = jax.nn.silu(layernorm(u, cf_ln_g, cf_ln_b))
    return jnp.concatenate([y_sc, u], axis=-1) @ w_out


def rg_lru(xc, w_a, b_a, w_x, b_x, lam):
    B, S, W = xc.shape
    xb = xc.reshape(B, S, LRU_HEADS, LRU_BLOCK)
    r = jax.nn.sigmoid(jnp.einsum("bshi,hij->bshj", xb, w_a).reshape(B, S, W) + b_a)
    i = jax.nn.sigmoid(jnp.einsum("bshi,hij->bshj", xb, w_x).reshape(B, S, W) + b_x)
    log_a = -LRU_C * r.astype(jnp.float32) * jax.nn.softplus(-lam.astype(jnp.float32))
    a = jnp.exp(log_a)
    mult = jnp.sqrt(-jnp.expm1(2.0 * log_a))
    bt = mult * (i * xc).astype(jnp.float32)

    def combine(left, right):
        a1, b1 = left
        a2, b2 = right
        return a1 * a2, a2 * b1 + b2

    _, hs = lax.associative_scan(combine, (a, bt), axis=1)
    return hs.astype(xc.dtype)


def dilated_group_attention(q, k, v, window, dilation):
    B, S, H, Dh = q.shape
    L = S // dilation
    nb = -(-L // ATTN_BLOCK)
    Lp = nb * ATTN_BLOCK
    span = window // dilation

    def strided(t):
        t = t.reshape(B, L, dilation, H, Dh).transpose(0, 2, 1, 3, 4)
        t = jnp.pad(t, ((0, 0), (0, 0), (0, Lp - L), (0, 0), (0, 0)))
        return t.reshape(B, dilation, nb, ATTN_BLOCK, H, Dh)

    def with_prev(t):
        prev = jnp.pad(t[:, :, :-1], ((0, 0), (0, 0), (1, 0), (0, 0), (0, 0), (0, 0)))
        return jnp.concatenate([prev, t], axis=3)

    qb = strided(q)
    kk = with_prev(strided(k))
    vv = with_prev(strided(v))
    s = jnp.einsum("brnqhd,brnkhd->brnhqk", qb, kk).astype(jnp.float32)
    qi = jnp.arange(ATTN_BLOCK)[:, None]
    kj = jnp.arange(2 * ATTN_BLOCK)[None, :]
    steps = ATTN_BLOCK + qi - kj
    band = (steps >= 0) & (steps <= span)
    blk_idx = jnp.arange(nb)[:, None, None]
    mask = band[None] & ((blk_idx > 0) | (kj >= ATTN_BLOCK)[None])
    s = jnp.where(mask[None, None, :, None], s, -jnp.inf)
    m = jnp.max(s, axis=-1, keepdims=True)
    p = jnp.exp(s - m)
    den = jnp.sum(p, axis=-1, keepdims=True)
    o = jnp.einsum("brnhqk,brnkhd->brnqhd", p, vv.astype(jnp.float32))
    o = o / jnp.swapaxes(den, 3, 4)
    lse = jnp.swapaxes((m + jnp.log(den))[..., 0], 3, 4)
    o = o.reshape(B, dilation, Lp, H, Dh)[:, :, :L].transpose(0, 2, 1, 3, 4).reshape(B, S, H, Dh)
    lse = lse.reshape(B, dilation, Lp, H)[:, :, :L].transpose(0, 2, 1, 3).reshape(B, S, H)
    return o, lse


def dilated_attention(q, k, v, q_norm_g, k_norm_g):
    B, S, _ = q.shape
    shp = (B, S, N_DIL_GROUPS, DIL_HEADS, DIL_HEAD_DIM)
    qn = rmsnorm(q.reshape(shp), q_norm_g) * (DIL_HEAD_DIM ** -0.5)
    kn = rmsnorm(k.reshape(shp), k_norm_g)
    vr = v.reshape(shp)
    outs, lses = [], []
    for g, (window, dilation) in enumerate(DIL_PATTERNS):
        o, lse = dilated_group_attention(qn[:, :, g], kn[:, :, g], vr[:, :, g], window, dilation)
        outs.append(o)
        lses.append(lse)
    o = jnp.stack(outs)
    wts = jax.nn.softmax(jnp.stack(lses), axis=0)
    out = jnp.einsum("gbsh,gbshd->bshd", wts, o)
    return out.reshape(B, S, DIL_OUT).astype(q.dtype)


def odd_mixer(h, w_in, lru_conv_w, lru_conv_b, lru_wa, lru_ba, lru_wx, lru_bx, lru_lam,
              q_norm_g, k_norm_g, w_out):
    z = h @ w_in
    lru_x, lru_gate, q, k, v = jnp.split(
        z, [LRU_WIDTH, 2 * LRU_WIDTH, 2 * LRU_WIDTH + DIL_QKV, 2 * LRU_WIDTH + 2 * DIL_QKV], axis=-1)
    xc = causal_dwconv(lru_x, lru_conv_w) + lru_conv_b
    y_lru = rg_lru(xc, lru_wa, lru_ba, lru_wx, lru_bx, lru_lam) * jax.nn.gelu(lru_gate)
    y_att = dilated_attention(q, k, v, q_norm_g, k_norm_g)
    return jnp.concatenate([y_lru, y_att], axis=-1) @ w_out


def conv_glu_ffn(h, w_gate, w_up, conv_w, conv_b, w_down):
    g = causal_dwconv(h @ w_gate, conv_w) + conv_b
    return (jax.nn.silu(g) * (h @ w_up)) @ w_down


def setup_inputs(seed: int = 0) -> dict:
    key = jax.random.key(seed)
    ks = iter(jax.random.split(key, 40))
    f32 = jnp.float32

    def nrm(shape, fan_in):
        return jax.random.normal(next(ks), shape, f32) * (fan_in ** -0.5)

    def gain(shape):
        return 1.0 + 0.05 * jax.random.normal(next(ks), shape, f32)

    def bias(shape):
        return 0.01 * jax.random.normal(next(ks), shape, f32)

    u = jax.random.uniform(next(ks), (N_ODD, LRU_WIDTH), f32, minval=0.9, maxval=0.999)
    a0 = u ** (1.0 / LRU_C)
    lam = jnp.log(a0) - jnp.log1p(-a0)

    return {
        "x": jax.random.normal(next(ks), (BATCH, SEQ, D_MODEL), f32),
        "mix_norm_g": gain((DEPTH, D_MODEL)),
        "ffn_norm_g": gain((DEPTH, D_MODEL)),
        "ev_w_in": nrm((N_EVEN, D_MODEL, EV_IN), D_MODEL),
        "ev_sc_conv_w": nrm((N_EVEN, SC_WIDTH_CONV, SC_WIDTH), SC_WIDTH_CONV),
        "ev_cf_conv_w": nrm((N_EVEN, CF_WIDTH_CONV, CF_WIDTH), CF_WIDTH_CONV),
        "ev_cf_conv_b": bias((N_EVEN, CF_WIDTH)),
        "ev_cf_ln_g": gain((N_EVEN, CF_WIDTH)),
        "ev_cf_ln_b": bias((N_EVEN, CF_WIDTH)),
        "ev_w_out": nrm((N_EVEN, EV_OUT_IN, D_MODEL), EV_OUT_IN),
        "od_w_in": nrm((N_ODD, D_MODEL, OD_IN), D_MODEL),
        "od_lru_conv_w": nrm((N_ODD, LRU_CONV, LRU_WIDTH), LRU_CONV),
        "od_lru_conv_b": bias((N_ODD, LRU_WIDTH)),
        "od_lru_wa": nrm((N_ODD, LRU_HEADS, LRU_BLOCK, LRU_BLOCK), LRU_BLOCK),
        "od_lru_ba": bias((N_ODD, LRU_WIDTH)),
        "od_lru_wx": nrm((N_ODD, LRU_HEADS, LRU_BLOCK, LRU_BLOCK), LRU_BLOCK),
        "od_lru_bx": bias((N_ODD, LRU_WIDTH)),
        "od_lru_lam": lam,
        "od_q_norm_g": gain((N_ODD, DIL_HEAD_DIM)),
        "od_k_norm_g": gain((N_ODD, DIL_HEAD_DIM)),
        "od_w_out": nrm((N_ODD, OD_OUT_IN, D_MODEL), OD_OUT_IN),
        "ffn_w_gate": nrm((DEPTH, D_MODEL, D_FF), D_MODEL),
        "ffn_w_up": nrm((DEPTH, D_MODEL, D_FF), D_MODEL),
        "ffn_conv_w": nrm((DEPTH, FFN_CONV, D_FF), FFN_CONV),
        "ffn_conv_b": bias((DEPTH, D_FF)),
        "ffn_w_down": nrm((DEPTH, D_FF, D_MODEL), D_FF),
    }


def reference(x, mix_norm_g, ffn_norm_g,
              ev_w_in, ev_sc_conv_w, ev_cf_conv_w, ev_cf_conv_b, ev_cf_ln_g, ev_cf_ln_b, ev_w_out,
              od_w_in, od_lru_conv_w, od_lru_conv_b, od_lru_wa, od_lru_ba, od_lru_wx, od_lru_bx,
              od_lru_lam, od_q_norm_g, od_k_norm_g, od_w_out,
              ffn_w_gate, ffn_w_up, ffn_conv_w, ffn_conv_b, ffn_w_down):
    for layer in range(DEPTH):
        j = layer // 2
        h = rmsnorm(x, mix_norm_g[layer])
        if layer % 2 == 0:
            x = x + even_mixer(h, ev_w_in[j], ev_sc_conv_w[j], ev_cf_conv_w[j], ev_cf_conv_b[j],
                               ev_cf_ln_g[j], ev_cf_ln_b[j], ev_w_out[j])
        else:
            x = x + odd_mixer(h, od_w_in[j], od_lru_conv_w[j], od_lru_conv_b[j], od_lru_wa[j],
                              od_lru_ba[j], od_lru_wx[j], od_lru_bx[j], od_lru_lam[j],
                              od_q_norm_g[j], od_k_norm_g[j], od_w_out[j])
        h = rmsnorm(x, ffn_norm_g[layer])
        x = x + conv_glu_ffn(h, ffn_w_gate[layer], ffn_w_up[layer], ffn_conv_w[layer],
                             ffn_conv_b[layer], ffn_w_down[layer])
    return x
```

```python
import os
import numpy as np
import concourse.bass as bass
import concourse.mybir as mybir
from concourse.bass_utils import run_bass_kernel_spmd

F32 = mybir.dt.float32
BF16 = mybir.dt.bfloat16
AF = mybir.ActivationFunctionType
ALU = mybir.AluOpType

D = 1024
DFF = 2816
NFF = DFF // 128
EPS = 1e-6
TT = 512
DILS = (1, 4, 16)
N_PE_TAPS = 8
POOL_TAPS = 9


class KB:
    def __init__(self, nc):
        self.nc = nc
        self.eng = {}
        for name, h in (("pe", nc.tensor), ("act", nc.scalar), ("dve", nc.vector), ("pool", nc.gpsimd), ("sp", nc.sync)):
            sem = nc.alloc_semaphore("sem_" + name)
            self.eng[name] = dict(h=h, sem=sem, cnt=0, seen={})
        self.lastw = {}
        self.readers = {}
        self.dma_sems = {}

    def _deps(self, reads, writes):
        deps = []
        for r in reads:
            if r in self.lastw:
                deps.append(self.lastw[r])
        for w in writes:
            if w in self.lastw:
                deps.append(self.lastw[w])
            deps.extend(self.readers.get(w, ()))
        return deps

    def _wait(self, me, deps):
        e = self.eng[me]
        best = {}
        for (sem, val, key) in deps:
            if best.get(key, (None, 0))[1] < val:
                best[key] = (sem, val)
        for key, (sem, val) in best.items():
            if e["seen"].get(key, 0) < val:
                e["h"].wait_ge(sem, val)
                e["seen"][key] = val

    def _commit(self, ticket, reads, writes):
        for r in reads:
            self.readers.setdefault(r, []).append(ticket)
        for w in writes:
            self.lastw[w] = ticket
            self.readers[w] = []

    def op(self, me, fn, reads=(), writes=()):
        e = self.eng[me]
        self._wait(me, self._deps(reads, writes))
        inst = fn(e["h"])
        e["cnt"] += 1
        inst.then_inc(e["sem"], 1)
        t = (e["sem"], e["cnt"], me)
        self._commit(t, reads, writes)
        return t

    def mm(self, fns, reads=(), writes=()):
        e = self.eng["pe"]
        self._wait("pe", self._deps(reads, writes))
        inst = None
        for fn in fns:
            inst = fn(e["h"])
        e["cnt"] += 1
        inst.then_inc(e["sem"], 1)
        t = (e["sem"], e["cnt"], "pe")
        self._commit(t, reads, writes)
        return t

    def dma(self, semkey, out, in_, reads=(), writes=(), q="sp"):
        e = self.eng[q]
        if q == "pool":
            semkey = semkey + "_sw"
        if semkey not in self.dma_sems:
            self.dma_sems[semkey] = [self.nc.alloc_semaphore("dsem_" + semkey), 0]
        s = self.dma_sems[semkey]
        self._wait(q, self._deps(reads, writes))
        inst = e["h"].dma_start(out=out, in_=in_)
        s[1] += 16
        inst.then_inc(s[0], 16)
        t = (s[0], s[1], "dma_" + semkey)
        self._commit(t, reads, writes)
        return t

    def barrier(self):
        for me, e in self.eng.items():
            for other, o in self.eng.items():
                if other != me and o["cnt"] > e["seen"].get(other, 0):
                    e["h"].wait_ge(o["sem"], o["cnt"])
                    e["seen"][other] = o["cnt"]
            for k, (sem, val) in self.dma_sems.items():
                key = "dma_" + k
                if val > e["seen"].get(key, 0):
                    e["h"].wait_ge(sem, val)
                    e["seen"][key] = val
        self.lastw = {}
        self.readers = {}


class Arena:
    def __init__(self, ap_f32):
        self.ap = ap_f32
        self.n = ap_f32.shape[1]
        self.off = 0

    def reset(self):
        self.off = 0

    def f32(self, n):
        assert self.off + n <= self.n, ("arena overflow", self.off, n, self.n)
        v = self.ap[:, self.off:self.off + n]
        self.off += n
        return v

    def bf16(self, n):
        m = (n + 1) // 2
        return self.f32(m).bitcast(BF16)[:, 0:n]


def _col(v):
    v = np.asarray(v, np.float32).reshape(-1)
    return np.ascontiguousarray(v.reshape(-1, 128).T)


def pack_vecs(inp):
    cols = []
    idx = {}

    def add(name, arr2d):
        idx[name] = sum(c.shape[1] for c in cols)
        cols.append(np.ascontiguousarray(arr2d, dtype=np.float32))

    for l in range(2):
        add("mixg%d" % l, _col(inp["mix_norm_g"][l]))
    for l in range(2):
        add("ffng%d" % l, _col(inp["ffn_norm_g"][l]))
    for k in range(3):
        add("scw%d" % k, _col(inp["ev_sc_conv_w"][0, k]))
    for k in range(31):
        add("cfw%d" % k, _col(inp["ev_cf_conv_w"][0, k]))
    add("cfb", _col(inp["ev_cf_conv_b"][0]))
    add("lng", _col(inp["ev_cf_ln_g"][0]))
    add("lnb", _col(inp["ev_cf_ln_b"][0]))
    for k in range(4):
        add("lruw%d" % k, _col(inp["od_lru_conv_w"][0, k]))
    add("lrub", _col(inp["od_lru_conv_b"][0]))
    add("ba", _col(inp["od_lru_ba"][0]))
    add("bx", _col(inp["od_lru_bx"][0]))
    add("lam", _col(inp["od_lru_lam"][0]))
    add("qg", np.tile(np.asarray(inp["od_q_norm_g"][0], np.float32), 2).reshape(128, 1))
    add("kg", np.tile(np.asarray(inp["od_k_norm_g"][0], np.float32), 2).reshape(128, 1))
    for l in range(2):
        for k in range(3):
            add("fcw%d_%d" % (l, k), _col(inp["ffn_conv_w"][l, k]))
        add("fcb%d" % l, _col(inp["ffn_conv_b"][l]))
    return np.ascontiguousarray(np.concatenate(cols, axis=1)), idx


def blockdiag(w):
    w = np.asarray(w, np.float32)
    bd = np.zeros((128, 4, 128), np.float32)
    for c in range(4):
        for hl in range(2):
            bd[hl * 64:(hl + 1) * 64, c, hl * 64:(hl + 1) * 64] = w[2 * c + hl]
    return np.ascontiguousarray(bd.reshape(128, 512))


_VIDX = None


def vec_index():
    global _VIDX
    if _VIDX is None:
        fake = {
            "mix_norm_g": np.zeros((2, 1024)), "ffn_norm_g": np.zeros((2, 1024)),
            "ev_sc_conv_w": np.zeros((1, 3, 512)), "ev_cf_conv_w": np.zeros((1, 31, 512)),
            "ev_cf_conv_b": np.zeros((1, 512)), "ev_cf_ln_g": np.zeros((1, 512)), "ev_cf_ln_b": np.zeros((1, 512)),
            "od_lru_conv_w": np.zeros((1, 4, 512)), "od_lru_conv_b": np.zeros((1, 512)),
            "od_lru_ba": np.zeros((1, 512)), "od_lru_bx": np.zeros((1, 512)), "od_lru_lam": np.zeros((1, 512)),
            "od_q_norm_g": np.zeros((1, 64)), "od_k_norm_g": np.zeros((1, 64)),
            "ffn_conv_w": np.zeros((2, 3, 2816)), "ffn_conv_b": np.zeros((2, 2816)),
        }
        v, idx = pack_vecs(fake)
        _VIDX = (idx, v.shape[1])
    return _VIDX


def build(S, n_layers=2, debug=False, stop_after=None):
    NT = S // TT
    NBLK = S // 128
    VI, NV = vec_index()
    nc = bass.Bass("TRN2", target_bir_lowering=False)
    kb = KB(nc)

    def din(name, shape, dt=F32):
        return nc.dram_tensor(name, shape, dt, kind="ExternalInput").ap()

    skind = "ExternalOutput" if debug else "Internal"

    def dscr(name, shape, dt):
        return nc.dram_tensor(name, shape, dt, kind=skind).ap()

    x_in = din("x", [S, D])
    vecs_d = din("vecs", [128, NV])
    bdA_d = din("bdA", [128, 512])
    bdX_d = din("bdX", [128, 512])
    W = {
        "ev_in": din("ev_w_in", [D, 2560]), "ev_out": din("ev_w_out", [1024, D]),
        "od_in": din("od_w_in", [D, 3328]), "od_out": din("od_w_out", [768, D]),
        "g0": din("ffn_w_gate0", [D, DFF]), "u0": din("ffn_w_up0", [D, DFF]), "d0": din("ffn_w_down0", [DFF, D]),
        "g1": din("ffn_w_gate1", [D, DFF]), "u1": din("ffn_w_up1", [D, DFF]), "d1": din("ffn_w_down1", [DFF, D]),
    }
    out_d = nc.dram_tensor("out", [S, D], F32, kind="ExternalOutput").ap()

    xT_s = dscr("xT_s", [D, S], F32)
    u0_s = dscr("u0_s", [512, S], BF16)
    cf_s = dscr("cf_s", [512, S], F32)
    u1_s = dscr("u1_s", [768, S], BF16)
    act_s = dscr("act_s", [DFF, S], BF16)
    qn_s = dscr("qn_s", [768, S], BF16)
    kn_s = dscr("kn_s", [768, S], BF16)
    v_s = dscr("v_s", [768, S], BF16)

    def fm(ap, t=None):
        v = ap.rearrange("(c p) s -> p c s", p=128)
        if t is not None:
            v = v[:, :, t * TT:(t + 1) * TT]
        return v

    ident_f = nc.alloc_sbuf_tensor("ident_f", [128, 128], F32)
    ident_b = nc.alloc_sbuf_tensor("ident_b", [128, 128], BF16)
    ones_b = nc.alloc_sbuf_tensor("ones_b", [128, 128], BF16)
    onesLN = nc.alloc_sbuf_tensor("onesLN", [128, 128], F32)
    blk_b = nc.alloc_sbuf_tensor("blk_b", [128, 128], BF16)
    mask_b = nc.alloc_sbuf_tensor("mask_b", [128, 2, 256], BF16)
    vecs = nc.alloc_sbuf_tensor("vecs_sb", [128, NV], F32)
    dv = nc.alloc_sbuf_tensor("dv", [128, 16], F32)
    bdA = nc.alloc_sbuf_tensor("bdA_sb", [128, 4, 128], BF16)
    bdX = nc.alloc_sbuf_tensor("bdX_sb", [128, 4, 128], BF16)
    hTraw = nc.alloc_sbuf_tensor("hTraw", [128, 4 * S], F32)
    hT = hTraw[:, :].bitcast(BF16).rearrange("p (c s) -> p c s", c=8)
    arenaH = Arena(hTraw[:, :])
    wres = nc.alloc_sbuf_tensor("wres", [128, NFF * 1024], BF16)
    AR = (nc.sbuf_bytes_remaining - 2048) // 4
    arena_t = nc.alloc_sbuf_tensor("arena", [128, AR], F32)
    arena = Arena(arena_t[:, :])
    banks = [nc.alloc_psum_tensor("bank%d" % i, [128, 512], F32) for i in range(8)]

    def V(name, i=0):
        o = VI[name] + i
        return vecs[:, o:o + 1]

    EPSC = dv[:, 0:1]
    QG8 = dv[:, 1:2]

    def SP8(i):
        return dv[:, 2 + i:3 + i]

    def SP16(i):
        return dv[:, 6 + i:7 + i]

    class RR:
        def __init__(self, ids):
            self.ids = ids
            self.i = 0

        def next(self):
            b = self.ids[self.i % len(self.ids)]
            self.i += 1
            return b

    class Slots:
        def __init__(self, name, aps):
            self.name = name
            self.aps = aps
            self.i = -1

        def next(self):
            self.i += 1
            k = self.i % len(self.aps)
            return self.aps[k], "%s%d" % (self.name, k)

    kb.dma("vecs", vecs[:, :], vecs_d, writes=["vecs"])
    kb.op("pool", lambda e: e.memset(ident_f[:, :], 1.0), writes=["ident_f"])
    kb.op("pool", lambda e: e.affine_select(out=ident_f[:, :], in_=ident_f[:, :], pattern=[[-1, 128]],
                                             compare_op=ALU.is_equal, fill=0.0, base=0, channel_multiplier=1),
          reads=["ident_f"], writes=["ident_f"])
    kb.op("pool", lambda e: e.tensor_copy(out=ident_b[:, :], in_=ident_f[:, :]), reads=["ident_f"], writes=["ident_b"])
    kb.op("pool", lambda e: e.memset(ones_b[:, :], 1.0), writes=["ones_b"])
    kb.op("pool", lambda e: e.memset(onesLN[:, :], 1.0 / 512.0), writes=["onesLN"])
    kb.op("pool", lambda e: e.memset(blk_b[:, :], 0.0), writes=["blk_b"])
    kb.op("pool", lambda e: e.memset(blk_b[0:64, 0:64], 1.0), reads=["blk_b"], writes=["blk_b"])
    kb.op("pool", lambda e: e.memset(blk_b[64:128, 64:128], 1.0), reads=["blk_b"], writes=["blk_b"])
    kb.op("pool", lambda e: e.memset(mask_b[:, :, :], 1.0), writes=["mask_b"])
    for hh in range(2):
        kb.op("pool", lambda e: e.affine_select(out=mask_b[:, hh, 0:128], in_=mask_b[:, hh, 0:128], pattern=[[1, 128]],
                                                 compare_op=ALU.is_ge, fill=0.0, base=0, channel_multiplier=-1),
              reads=["mask_b"], writes=["mask_b"])
        kb.op("pool", lambda e: e.affine_select(out=mask_b[:, hh, 128:256], in_=mask_b[:, hh, 128:256], pattern=[[-1, 128]],
                                                 compare_op=ALU.is_ge, fill=0.0, base=0, channel_multiplier=1),
              reads=["mask_b"], writes=["mask_b"])
    kb.op("dve", lambda e: e.memset(dv[:, :], EPS), writes=["dv"])
    kb.op("dve", lambda e: e.tensor_scalar(out=QG8, in0=V("qg"), scalar1=0.125, scalar2=None, op0=ALU.mult),
          reads=["vecs", "dv"], writes=["dv"])
    lam4 = vecs[:, VI["lam"]:VI["lam"] + 4]
    kb.op("act", lambda e: e.activation(out=dv[:, 10:14], in_=lam4, func=AF.Exp, scale=-1.0), reads=["vecs", "dv"], writes=["dv"])
    kb.op("act", lambda e: e.activation(out=dv[:, 10:14], in_=dv[:, 10:14], func=AF.Ln, bias=1.0, scale=1.0), reads=["dv"], writes=["dv"])
    kb.op("dve", lambda e: e.tensor_scalar(out=dv[:, 2:6], in0=dv[:, 10:14], scalar1=-8.0, scalar2=None, op0=ALU.mult),
          reads=["dv"], writes=["dv"])
    kb.op("dve", lambda e: e.tensor_scalar(out=dv[:, 6:10], in0=dv[:, 10:14], scalar1=-16.0, scalar2=None, op0=ALU.mult),
          reads=["dv"], writes=["dv"])
    arena.reset()
    st0 = arena.f32(1024)
    kb.dma("st0", st0[:, 0:512], bdA_d, writes=["st0"])
    kb.op("pool", lambda e: e.tensor_copy(out=bdA[:, :, :], in_=st0[:, 0:512].rearrange("p (c m) -> p c m", c=4)), reads=["st0"], writes=["bdA"])
    kb.dma("st0b", st0[:, 512:1024], bdX_d, writes=["st0b"])
    kb.op("pool", lambda e: e.tensor_copy(out=bdX[:, :, :], in_=st0[:, 512:1024].rearrange("p (c m) -> p c m", c=4)), reads=["st0b"], writes=["bdX"])
    kb.barrier()

    deferred_casts = []

    def flush_casts():
        while deferred_casts:
            deferred_casts.pop(0)()

    def load_w_chunk(Wap, col0, dst, dkey, stg):
        if len(deferred_casts) >= len(stg.aps):
            flush_casts()
        sap, skey = stg.next()
        src = Wap.rearrange("(kc p) m -> p kc m", p=128)[:, :, col0:col0 + 128]
        kb.dma(skey, sap[:, 0:1024].rearrange("p (kc m) -> p kc m", kc=8), src, writes=[skey])
        deferred_casts.append(lambda: kb.op("pool", lambda e: e.tensor_copy(out=dst, in_=sap[:, 0:1024].rearrange("p (kc m) -> p kc m", kc=8)),
                                            reads=[skey], writes=[dkey]))

    def load_wres_one(Wap, kc, stg):
        if len(deferred_casts) >= len(stg.aps):
            flush_casts()
        sap, skey = stg.next()
        kb.dma(skey, sap[:, 0:1024], Wap[kc * 128:(kc + 1) * 128, :], writes=[skey])
        deferred_casts.append(lambda: kb.op("pool", lambda e: e.tensor_copy(out=wres[:, kc * 1024:(kc + 1) * 1024], in_=sap[:, 0:1024]),
                                            reads=[skey], writes=["wres"]))

    def proj(bank_id, wt, wkey, t, extra_reads=()):
        b = banks[bank_id]
        fns = []
        for kc in range(8):
            fns.append(lambda e, kc=kc: e.matmul(b[:, :], lhsT=wt[:, kc, :], rhs=hT[:, kc, t * TT:(t + 1) * TT],
                                                 start=(kc == 0), stop=(kc == 7)))
        kb.mm(fns, reads=[wkey, "hT%d" % t] + list(extra_reads), writes=["B%d" % bank_id])

    def norm_tile(xt, xkey, t, gname, sq, rs, rinv, statrr):
        kb.op("act", lambda e: e.activation(out=sq[:, :, :], in_=xt[:, :, :], func=AF.Square), reads=[xkey], writes=["sq"])
        bid = statrr.next()
        b = banks[bid]
        kb.mm([lambda e, c=c: e.matmul(b[:, :], lhsT=ones_b[:, :], rhs=sq[:, c, :], start=(c == 0), stop=(c == 7)) for c in range(8)],
              reads=["sq"], writes=["B%d" % bid])
        kb.op("act", lambda e: e.activation(out=rinv, in_=b[:, :], func=AF.Ln, bias=EPSC, scale=1.0 / D), reads=["B%d" % bid], writes=["rinv"])
        kb.op("act", lambda e: e.activation(out=rinv, in_=rinv, func=AF.Exp, scale=-0.5), reads=["rinv"], writes=["rinv"])
        for c in range(8):
            kb.op("dve", lambda e: e.scalar_tensor_tensor(out=hT[:, c, t * TT:(t + 1) * TT], in0=xt[:, c, :], scalar=V(gname, c),
                                                          in1=rinv, op0=ALU.mult, op1=ALU.mult),
                  reads=[xkey, "rinv"], writes=["hT%d" % t])

    def phase_p0():
        arena.reset()
        xin = Slots("xin", [arena.f32(4 * D).rearrange("p (s c) -> p s c", s=4) for _ in range(2)])
        xts = Slots("xt", [arena.f32(8 * TT).rearrange("p (c s) -> p c s", c=8) for _ in range(2)])
        sq = arena.bf16(8 * TT).rearrange("p (c s) -> p c s", c=8)
        rs = arena.f32(TT)
        rinv = arena.f32(TT)
        trr = RR([0, 1, 2, 3])
        srr = RR([4, 5])
        for t in range(NT):
            xi, xik = xin.next()
            kb.dma(xik, xi, x_in[t * TT:(t + 1) * TT, :].rearrange("(s p) c -> p s c", p=128), writes=[xik])
            xt, xk = xts.next()
            for c in range(8):
                bid = trr.next()
                b = banks[bid]
                kb.mm([lambda e, s=s: e.transpose(b[:, s * 128:(s + 1) * 128], xi[:, s, c * 128:(c + 1) * 128], ident_f[:, :]) for s in range(4)],
                      reads=[xik], writes=["B%d" % bid])
                kb.op("act", lambda e: e.activation(out=xt[:, c, :], in_=b[:, :], func=AF.Copy), reads=["B%d" % bid], writes=[xk])
            kb.dma(xk, fm(xT_s, t), xt, reads=[xk], writes=["xTs%d" % t], q="pool")
            norm_tile(xt, xk, t, "mixg0", sq, rs, rinv, srr)
        flush_casts()
        kb.barrier()

    def phase_norm(gname):
        arena.reset()
        xts = Slots("xt", [arena.f32(8 * TT).rearrange("p (c s) -> p c s", c=8) for _ in range(2)])
        sq = arena.bf16(8 * TT).rearrange("p (c s) -> p c s", c=8)
        rs = arena.f32(TT)
        rinv = arena.f32(TT)
        srr = RR([4, 5])
        for t in range(NT):
            xt, xk = xts.next()
            kb.dma(xk, xt, fm(xT_s, t), reads=["xTs%d" % t], writes=[xk])
            norm_tile(xt, xk, t, gname, sq, rs, rinv, srr)
        flush_casts()
        kb.barrier()

    def phase_p2():
        arena.reset()
        stg = Slots("stg", [arena.f32(1024) for _ in range(4)])
        ws = Slots("ws", [arena.bf16(1024).rearrange("p (kc m) -> p kc m", kc=8) for _ in range(9)])
        wres_todo = list(range(8))
        pbuf = Slots("pb", [arena.f32(2 + TT) for _ in range(2)])
        ubuf = Slots("ub", [arena.f32(30 + TT) for _ in range(2)])
        tmpa = Slots("ta", [arena.f32(TT) for _ in range(3)])
        tmpb = Slots("tb", [arena.f32(TT) for _ in range(3)])
        tmpc = Slots("tc", [arena.f32(TT) for _ in range(2)])
        tmpd = Slots("td", [arena.f32(TT) for _ in range(4)])
        yo = Slots("yo", [arena.bf16(TT) for _ in range(3)])
        co = Slots("co", [arena.f32(TT) for _ in range(3)])
        prr = RR([0, 1, 2, 3])
        crr = RR([4, 5, 6, 7])
        dgs = Slots("dg", [arena.f32(N_PE_TAPS * 128).rearrange("p (n m) -> p n m", n=N_PE_TAPS) for _ in range(2)])
        Wi = W["ev_in"]

        def getw(col0):
            wt, wk = ws.next()
            load_w_chunk(Wi, col0, wt, wk, stg)
            return wt, wk

        prr_sc = RR([0, 1, 2, 3, 4, 5, 6, 7])
        nxt = [getw(512 + 0), getw(1024 + 0), getw(0)]
        flush_casts()
        for i in range(4):
            (wc, wck), (wx, wxk), (wb, wbk) = nxt
            load_wres_one(W["ev_out"], wres_todo.pop(0), stg)
            if i < 3:
                nxt = [getw(512 + (i + 1) * 128), getw(1024 + (i + 1) * 128), getw((i + 1) * 128)]
            else:
                nxt = [getw(1536), getw(2048)]
            prev = None
            for t in range(NT):
                if t == min(3, NT - 1):
                    flush_casts()
                bc, bx, bb = prr_sc.next(), prr_sc.next(), prr_sc.next()
                proj(bc, wc, wck, t)
                proj(bx, wx, wxk, t)
                proj(bb, wb, wbk, t)
                csb, ck = tmpa.next()
                kb.op("act", lambda e: e.activation(out=csb, in_=banks[bc][:, :], func=AF.Copy), reads=["B%d" % bc], writes=[ck])
                pb, pk = pbuf.next()
                if prev is None:
                    kb.op("dve", lambda e: e.memset(pb[:, 0:2], 0.0), writes=[pk + "h"])
                else:
                    kb.op("dve", lambda e: e.tensor_copy(out=pb[:, 0:2], in_=prev[0][:, TT:TT + 2]), reads=[prev[1]], writes=[pk + "h"])
                kb.op("dve", lambda e: e.tensor_tensor(out=pb[:, 2:2 + TT], in0=banks[bx][:, :], in1=csb, op=ALU.mult),
                      reads=["B%d" % bx, ck], writes=[pk])
                acc, ak = tmpb.next()
                kb.op("dve", lambda e: e.tensor_scalar(out=acc, in0=pb[:, 2:2 + TT], scalar1=V("scw2", i), scalar2=None, op0=ALU.mult),
                      reads=[pk], writes=[ak])
                kb.op("dve", lambda e: e.scalar_tensor_tensor(out=acc, in0=pb[:, 1:1 + TT], scalar=V("scw1", i), in1=acc, op0=ALU.mult, op1=ALU.add),
                      reads=[pk, pk + "h", ak], writes=[ak])
                kb.op("dve", lambda e: e.scalar_tensor_tensor(out=acc, in0=pb[:, 0:TT], scalar=V("scw0", i), in1=acc, op0=ALU.mult, op1=ALU.add),
                      reads=[pk, pk + "h", ak], writes=[ak])
                y, yk = yo.next()
                kb.op("dve", lambda e: e.tensor_tensor(out=y, in0=banks[bb][:, :], in1=acc, op=ALU.mult), reads=["B%d" % bb, ak], writes=[yk])
                kb.dma(yk, u0_s[i * 128:(i + 1) * 128, t * TT:(t + 1) * TT], y, reads=[yk], writes=["u0_%d_%d" % (i, t)])
                prev = (pb, pk)
        cf_pend = [None]
        PE_TAPS = list(range(0, N_PE_TAPS))
        DVE_TAPS = list(range(N_PE_TAPS, 30 - POOL_TAPS))
        POOL_T = list(range(30 - POOL_TAPS, 30))

        def cf_b(acc, ak, acc2, a2k, pacc, pak, i, t):
            c_, ck_ = co.next()
            kb.op("dve", lambda e: e.scalar_tensor_tensor(out=c_, in0=acc2, scalar=V("cfb", i), in1=pacc, op0=ALU.add, op1=ALU.add),
                  reads=[a2k, pak], writes=[ck_])
            kb.op("dve", lambda e: e.tensor_tensor(out=c_, in0=acc, in1=c_, op=ALU.add), reads=[ak, ck_], writes=[ck_])
            kb.dma(ck_, cf_s[i * 128:(i + 1) * 128, t * TT:(t + 1) * TT], c_, reads=[ck_], writes=["cf_%d_%d" % (i, t)])

        for i in range(4):
            (wa, wak), (wg, wgk) = nxt
            load_wres_one(W["ev_out"], wres_todo.pop(0), stg)
            if i < 3:
                nxt = [getw(1536 + (i + 1) * 128), getw(2048 + (i + 1) * 128)]
            dg, dgk = dgs.next()
            for n, k in enumerate(PE_TAPS):
                kb.op("pool", lambda e: e.tensor_scalar(out=dg[:, n, :], in0=ident_f[:, :], scalar1=V("cfw%d" % k, i), scalar2=0.0,
                                                        op0=ALU.mult, op1=ALU.add), reads=["ident_f"], writes=[dgk])
            prev = None
            for t in range(NT):
                if t == min(3, NT - 1):
                    flush_casts()
                ba_, bg_ = prr.next(), prr.next()
                proj(ba_, wa, wak, t)
                proj(bg_, wg, wgk, t)
                sg, sk = tmpa.next()
                kb.op("act", lambda e: e.activation(out=sg, in_=banks[bg_][:, :], func=AF.Sigmoid), reads=["B%d" % bg_], writes=[sk])
                ub, uk = ubuf.next()
                if prev is None:
                    kb.op("dve", lambda e: e.memset(ub[:, 0:30], 0.0), writes=[uk + "h"])
                else:
                    kb.op("dve", lambda e: e.tensor_copy(out=ub[:, 0:30], in_=prev[0][:, TT:TT + 30]), reads=[prev[1]], writes=[uk + "h"])
                kb.op("dve", lambda e: e.tensor_tensor(out=ub[:, 30:30 + TT], in0=banks[ba_][:, :], in1=sg, op=ALU.mult),
                      reads=["B%d" % ba_, sk], writes=[uk])
                cb2 = crr.next()
                acc2, a2k = banks[cb2][:, :], "B%d" % cb2
                kb.mm([lambda e, n=n, k=k: e.matmul(acc2, lhsT=dg[:, n, :], rhs=ub[:, k:k + TT], start=(n == 0), stop=(n == len(PE_TAPS) - 1))
                       for n, k in enumerate(PE_TAPS)], reads=[uk, uk + "h", dgk], writes=[a2k])
                cb = crr.next()
                acc, ak = banks[cb][:, :], "B%d" % cb
                kb.op("dve", lambda e: e.tensor_scalar(out=acc, in0=ub[:, 30:30 + TT], scalar1=V("cfw30", i), scalar2=None,
                                                       op0=ALU.mult), reads=[uk], writes=[ak])
                for k in DVE_TAPS:
                    kb.op("dve", lambda e: e.scalar_tensor_tensor(out=acc, in0=ub[:, k:k + TT], scalar=V("cfw%d" % k, i), in1=acc,
                                                                  op0=ALU.mult, op1=ALU.add), reads=[uk, uk + "h", ak], writes=[ak])
                pacc, pak = tmpc.next()
                for n, k in enumerate(POOL_T):
                    if n == 0:
                        dst, dk = pacc, pak
                    else:
                        dst, dk = tmpd.next()
                    kb.op("act", lambda e: e.activation(out=dst, in_=ub[:, k:k + TT], func=AF.Identity, scale=V("cfw%d" % k, i)),
                          reads=[uk, uk + "h"], writes=[dk])
                    if n > 0:
                        kb.op("pool", lambda e: e.tensor_tensor(out=pacc, in0=pacc, in1=dst, op=ALU.add), reads=[pak, dk], writes=[pak])
                if cf_pend[0] is not None:
                    cf_b(*cf_pend[0])
                cf_pend[0] = (acc, ak, acc2, a2k, pacc, pak, i, t)
                prev = (ub, uk)
        cf_b(*cf_pend[0])
        flush_casts()
        kb.barrier()

    def phase_col(KC, load_fn, prep_fn, gname, final=False):
        xts = Slots("xt", [arena.f32(8 * TT).rearrange("p (c s) -> p c s", c=8) for _ in range(2)])
        if gname is not None:
            sq = arena.bf16(8 * TT).rearrange("p (c s) -> p c s", c=8)
            rs = None
            rinv = arena.f32(TT)
        if final:
            arenaH.reset()
            ots = Slots("ot", [arenaH.f32(4 * D).rearrange("p (s c) -> p s c", s=4) for _ in range(2)])
        orr = RR([0, 1, 2, 3])
        srr = RR([4, 5])
        trr = RR([6, 7])
        def loadx(t):
            xt, xk = xts.next()
            kb.dma(xk, xt, fm(xT_s, t), reads=["xTs%d" % t], writes=[xk] + [xk + "_%d" % m for m in range(8)])
            return (xt, xk)

        HU = {}
        HX = {}
        P = {}
        for t0 in range(min(2, NT)):
            HU[t0] = load_fn(t0)
            HX[t0] = loadx(t0)
        P[0] = prep_fn(HU[0])
        for t in range(NT):
            xt, xk = HX[t]
            if t + 1 < NT:
                P[t + 1] = prep_fn(HU[t + 1])
            ut, uk = P[t]
            for m in range(8):
                bid = orr.next()
                b = banks[bid]
                kb.mm([lambda e, kc=kc: e.matmul(b[:, :], lhsT=wres[:, kc * 1024 + m * 128: kc * 1024 + (m + 1) * 128], rhs=ut[:, kc, :],
                                                 start=(kc == 0), stop=(kc == KC - 1)) for kc in range(KC)],
                      reads=["wres"] + (uk if isinstance(uk, list) else [uk]), writes=["B%d" % bid])
                kb.op("dve", lambda e: e.tensor_tensor(out=xt[:, m, :], in0=b[:, :], in1=xt[:, m, :], op=ALU.add), reads=["B%d" % bid, xk], writes=[xk + "_%d" % m])
                if gname is not None:
                    if m == 0:
                        sbid = srr.next()
                    kb.op("act", lambda e: e.activation(out=sq[:, m, :], in_=xt[:, m, :], func=AF.Square), reads=[xk + "_%d" % m], writes=["sq_%d" % m])
                    kb.mm([lambda e: e.matmul(banks[sbid][:, :], lhsT=ones_b[:, :], rhs=sq[:, m, :], start=(m == 0), stop=(m == 7))],
                          reads=["sq_%d" % m], writes=["B%d" % sbid])
            xkeys = [xk] + [xk + "_%d" % m for m in range(8)]
            if t + 2 < NT:
                HU[t + 2] = load_fn(t + 2)
            if not final:
                kb.dma(xk, fm(xT_s, t), xt, reads=xkeys, writes=["xTs%d" % t], q="pool")
                if gname is not None:
                    sb = banks[sbid]
                    kb.op("act", lambda e: e.activation(out=rinv, in_=sb[:, :], func=AF.Ln, bias=EPSC, scale=1.0 / D), reads=["B%d" % sbid], writes=["rinv"])
                    kb.op("act", lambda e: e.activation(out=rinv, in_=rinv, func=AF.Exp, scale=-0.5), reads=["rinv"], writes=["rinv"])
                    for c in range(8):
                        kb.op("dve", lambda e: e.scalar_tensor_tensor(out=hT[:, c, t * TT:(t + 1) * TT], in0=xt[:, c, :], scalar=V(gname, c),
                                                                      in1=rinv, op0=ALU.mult, op1=ALU.mult),
                              reads=[xk + "_%d" % c, "rinv"], writes=["hT%d" % t])
            else:
                ot, ok = ots.next()
                for s in range(4):
                    for half in range(2):
                        bid = trr.next()
                        b = banks[bid]
                        kb.mm([lambda e, q=q: e.transpose(b[:, q * 128:(q + 1) * 128], xt[:, half * 4 + q, s * 128:(s + 1) * 128], ident_f[:, :]) for q in range(4)],
                              reads=xkeys, writes=["B%d" % bid])
                        kb.op("act", lambda e: e.activation(out=ot[:, s, half * 512:(half + 1) * 512], in_=b[:, :], func=AF.Copy),
                              reads=["B%d" % bid], writes=[ok])
                kb.dma(ok, out_d[t * TT:(t + 1) * TT, :].rearrange("(s p) c -> p s c", p=128), ot, reads=[ok], writes=["out%d" % t], q="pool")
            if t + 2 < NT:
                HX[t + 2] = loadx(t + 2)

    def phase_p3():
        arena.reset()
        uts = Slots("ut", [arena.bf16(8 * TT).rearrange("p (c s) -> p c s", c=8) for _ in range(2)])
        cfts = Slots("cft", [arena.f32(4 * TT).rearrange("p (c s) -> p c s", c=4) for _ in range(2)])
        sq4 = arena.f32(4 * TT).rearrange("p (c s) -> p c s", c=4)
        mean_sb = arena.f32(TT)
        m2 = arena.f32(TT)
        var = m2
        sd = m2
        rinv2 = m2
        tmps = Slots("lt", [arena.f32(TT) for _ in range(2)])
        lrr = RR([6, 7])

        def load_fn(t):
            ut, uk = uts.next()
            kb.dma(uk, ut[:, 0:4, :], fm(u0_s, t), reads=["u0_%d_%d" % (i, t) for i in range(4)], writes=[uk + "lo"])
            cft, ck = cfts.next()
            kb.dma(ck, cft, fm(cf_s, t), reads=["cf_%d_%d" % (i, t) for i in range(4)], writes=[ck])
            return (ut, uk, cft, ck)

        def prep_fn(h):
            ut, uk, cft, ck = h
            bm = lrr.next()
            kb.mm([lambda e, i=i: e.matmul(banks[bm][:, :], lhsT=onesLN[:, :], rhs=cft[:, i, :], start=(i == 0), stop=(i == 3)) for i in range(4)],
                  reads=[ck], writes=["B%d" % bm])
            kb.op("act", lambda e: e.activation(out=sq4[:, :, :], in_=cft[:, :, :], func=AF.Square), reads=[ck], writes=["sq4"])
            be = lrr.next()
            kb.mm([lambda e, i=i: e.matmul(banks[be][:, :], lhsT=onesLN[:, :], rhs=sq4[:, i, :], start=(i == 0), stop=(i == 3)) for i in range(4)],
                  reads=["sq4"], writes=["B%d" % be])
            kb.op("act", lambda e: e.activation(out=mean_sb, in_=banks[bm][:, :], func=AF.Copy), reads=["B%d" % bm], writes=["mean"])
            kb.op("dve", lambda e: e.tensor_tensor(out=m2, in0=mean_sb, in1=mean_sb, op=ALU.mult), reads=["mean"], writes=["m2"])
            kb.op("dve", lambda e: e.tensor_tensor(out=var, in0=banks[be][:, :], in1=m2, op=ALU.subtract), reads=["B%d" % be, "m2"], writes=["m2"])
            kb.op("dve", lambda e: e.tensor_scalar(out=var, in0=var, scalar1=0.0, scalar2=None, op0=ALU.max), reads=["m2"], writes=["m2"])
            kb.op("act", lambda e: e.activation(out=sd, in_=var, func=AF.Ln, bias=EPSC, scale=1.0), reads=["m2"], writes=["m2"])
            kb.op("act", lambda e: e.activation(out=rinv2, in_=sd, func=AF.Exp, scale=-0.5), reads=["m2"], writes=["m2"])
            for i in range(4):
                tp, tk = tmps.next()
                kb.op("dve", lambda e: e.tensor_tensor(out=tp, in0=cft[:, i, :], in1=mean_sb, op=ALU.subtract), reads=[ck, "mean"], writes=[tk])
                kb.op("dve", lambda e: e.tensor_tensor(out=tp, in0=tp, in1=rinv2, op=ALU.mult), reads=[tk, "m2"], writes=[tk])
                kb.op("act", lambda e: e.activation(out=ut[:, 4 + i, :], in_=tp, func=AF.Silu, bias=V("lnb", i), scale=V("lng", i)),
                      reads=[tk], writes=[uk])
            return ut, [uk, uk + "lo"]

        phase_col(8, load_fn, prep_fn, "ffng0")
        flush_casts()
        kb.barrier()

    def phase_f1(l):
        arena.reset()
        stg = Slots("stg", [arena.f32(1024) for _ in range(3)])
        ws = Slots("ws", [arena.bf16(1024).rearrange("p (kc m) -> p kc m", kc=8) for _ in range(6)])
        gbuf = Slots("gb", [arena.f32(2 + TT) for _ in range(3)])
        tmpb = Slots("tb", [arena.f32(TT) for _ in range(3)])
        tmps = Slots("tsl", [arena.f32(TT) for _ in range(3)])
        ao = Slots("ao", [arena.bf16(TT) for _ in range(4)])
        prr = RR([0, 1, 2, 3, 4, 5, 6, 7])
        Wg, Wu = W["g%d" % l], W["u%d" % l]

        def getw(Wap, col0):
            wt, wk = ws.next()
            load_w_chunk(Wap, col0, wt, wk, stg)
            return wt, wk

        ups = Slots("up", [arena.bf16(TT) for _ in range(3)])

        def stage_b(acc, ak, up, upk, j, t):
            sl, sk = tmps.next()
            kb.op("act", lambda e: e.activation(out=sl, in_=acc, func=AF.Silu), reads=[ak], writes=[sk])
            a_, ak_ = ao.next()
            kb.op("pool", lambda e: e.tensor_tensor(out=a_, in0=up, in1=sl, op=ALU.mult), reads=[upk, sk], writes=[ak_])
            kb.dma(ak_, act_s[j * 128:(j + 1) * 128, t * TT:(t + 1) * TT], a_, reads=[ak_], writes=["act_%d_%d" % (j, t)])

        pend = None
        nxt = [getw(Wg, 0), getw(Wu, 0)]
        flush_casts()
        for j in range(NFF):
            (wg, wgk), (wu, wuk) = nxt
            if j + 1 < NFF:
                nxt = [getw(Wg, (j + 1) * 128), getw(Wu, (j + 1) * 128)]
            load_wres_one(W["d%d" % l], j, stg)
            prev = None
            for t in range(NT):
                if t == min(3, NT - 1):
                    flush_casts()
                bg_, bu_ = prr.next(), prr.next()
                proj(bg_, wg, wgk, t)
                proj(bu_, wu, wuk, t)
                gb, gk = gbuf.next()
                if prev is None:
                    kb.op("dve", lambda e: e.memset(gb[:, 0:2], 0.0), writes=[gk + "h"])
                else:
                    kb.op("dve", lambda e: e.tensor_copy(out=gb[:, 0:2], in_=prev[0][:, TT:TT + 2]), reads=[prev[1]], writes=[gk + "h"])
                kb.op("act", lambda e: e.activation(out=gb[:, 2:2 + TT], in_=banks[bg_][:, :], func=AF.Copy), reads=["B%d" % bg_], writes=[gk])
                acc, ak = tmpb.next()
                kb.op("act", lambda e: e.activation(out=acc, in_=banks[bg_][:, :], func=AF.Identity, bias=V("fcb%d" % l, j), scale=V("fcw%d_2" % l, j)),
                      reads=["B%d" % bg_], writes=[ak])
                up, upk = ups.next()
                kb.op("act", lambda e: e.activation(out=up, in_=banks[bu_][:, :], func=AF.Copy), reads=["B%d" % bu_], writes=[upk])
                kb.op("dve", lambda e: e.scalar_tensor_tensor(out=acc, in0=gb[:, 1:1 + TT], scalar=V("fcw%d_1" % l, j), in1=acc, op0=ALU.mult, op1=ALU.add),
                      reads=[gk, gk + "h", ak], writes=[ak])
                kb.op("dve", lambda e: e.scalar_tensor_tensor(out=acc, in0=gb[:, 0:TT], scalar=V("fcw%d_0" % l, j), in1=acc, op0=ALU.mult, op1=ALU.add),
                      reads=[gk, gk + "h", ak], writes=[ak])
                if pend is not None:
                    stage_b(*pend)
                pend = (acc, ak, up, upk, j, t)
                prev = (gb, gk)
        stage_b(*pend)
        flush_casts()
        kb.barrier()

    def phase_f2(final):
        arena.reset()
        ats = Slots("at", [arena.bf16(NFF * TT).rearrange("p (c s) -> p c s", c=NFF) for _ in range(2)])

        def load_u(t):
            at, ak = ats.next()
            kb.dma(ak, at, fm(act_s, t), reads=["act_%d_%d" % (j, t) for j in range(NFF)], writes=[ak])
            return at, ak

        phase_col(NFF, load_u, lambda h: h, None if final else "mixg1", final=final)
        flush_casts()
        kb.barrier()

    def phase_q2():
        arena.reset()
        stg = Slots("stg", [arena.f32(1024) for _ in range(3)])
        ws = Slots("ws", [arena.bf16(1024).rearrange("p (kc m) -> p kc m", kc=8) for _ in range(5)])
        wres_todo = list(range(6))
        lru_mark = arena.off
        xbuf = Slots("xb", [arena.f32(3 + TT) for _ in range(3)])
        nsl = {"xc": 4, "a": 4, "bt": 4, "mu": 4, "r": 2, "ig": 2, "a2": 2}
        T = {n: Slots(n, [arena.f32(TT) for _ in range(k)]) for n, k in nsl.items()}
        T["gs"] = Slots("gs", [arena.bf16(TT) for _ in range(6)])
        hb = Slots("hb", [arena.f32(TT) for _ in range(3)])
        xcb = Slots("xcb", [arena.bf16(TT) for _ in range(4)])
        yo = Slots("yo", [arena.bf16(TT) for _ in range(3)])
        prr = RR([0, 1, 2, 3])
        srr = RR([4, 5, 6, 7])
        Wi = W["od_in"]

        def getw(col0):
            wt, wk = ws.next()
            load_w_chunk(Wi, col0, wt, wk, stg)
            return wt, wk

        def lru_a(cx):
            i, t = cx["i"], cx["t"]
            bx_, bg_ = prr.next(), prr.next()
            cx["bg"] = bg_
            proj(bx_, cx["wx"], cx["wxk"], t)
            proj(bg_, cx["wg"], cx["wgk"], t)
            xb, xk = xbuf.next()
            prev = cx["prev"]
            if prev is None:
                kb.op("dve", lambda e: e.memset(xb[:, 0:3], 0.0), writes=[xk + "h"])
            else:
                kb.op("dve", lambda e: e.tensor_copy(out=xb[:, 0:3], in_=prev[0][:, TT:TT + 3]), reads=[prev[1]], writes=[xk + "h"])
            kb.op("act", lambda e: e.activation(out=xb[:, 3:3 + TT], in_=banks[bx_][:, :], func=AF.Copy), reads=["B%d" % bx_], writes=[xk])
            cx["xb"] = (xb, xk)
            gs, gsk = T["gs"].next()
            kb.op("act", lambda e: e.activation(out=gs, in_=banks[bg_][:, :], func=AF.Gelu_apprx_tanh), reads=["B%d" % bg_], writes=[gsk])
            cx["gs"] = (gs, gsk)
            xc, xck = T["xc"].next()
            kb.op("act", lambda e: e.activation(out=xc, in_=banks[bx_][:, :], func=AF.Identity, bias=V("lrub", i), scale=V("lruw3", i)),
                  reads=["B%d" % bx_], writes=[xck])
            for k in (2, 1, 0):
                kb.op("dve", lambda e: e.scalar_tensor_tensor(out=xc, in0=xb[:, k:k + TT], scalar=V("lruw%d" % k, i), in1=xc,
                                                              op0=ALU.mult, op1=ALU.add), reads=[xk, xk + "h", xck], writes=[xck])
            cx["xc"] = (xc, xck)
            xcbf, xcbk = xcb.next()
            kb.op("dve", lambda e: e.tensor_copy(out=xcbf, in_=xc), reads=[xck], writes=[xcbk])
            cx["xcbf"] = (xcbf, xcbk)

        def lru_b(cxs):
            st = []
            for cx in cxs:
                i = cx["i"]
                xcbf, xcbk = cx["xcbf"]
                br_, bi_ = srr.next(), srr.next()
                kb.mm([lambda e, i=i, br_=br_, xcbf=xcbf: e.matmul(banks[br_][:, :], lhsT=bdA[:, i, :], rhs=xcbf, start=True, stop=True)],
                      reads=["bdA", xcbk], writes=["B%d" % br_])
                kb.mm([lambda e, i=i, bi_=bi_, xcbf=xcbf: e.matmul(banks[bi_][:, :], lhsT=bdX[:, i, :], rhs=xcbf, start=True, stop=True)],
                      reads=["bdX", xcbk], writes=["B%d" % bi_])
                st.append(dict(cx=cx, i=i, br=br_, bi=bi_))
            for s_ in st:
                i = s_["i"]
                r, rk = T["r"].next()
                ig, igk = T["ig"].next()
                s_["r"], s_["ig"] = (r, rk), (ig, igk)
                kb.op("act", lambda e: e.activation(out=r, in_=banks[s_["br"]][:, :], func=AF.Sigmoid, bias=V("ba", i)), reads=["B%d" % s_["br"]], writes=[rk])
                kb.op("act", lambda e: e.activation(out=ig, in_=banks[s_["bi"]][:, :], func=AF.Sigmoid, bias=V("bx", i)), reads=["B%d" % s_["bi"]], writes=[igk])
            for s_ in st:
                i = s_["i"]
                r, rk = s_["r"]
                a, ak = T["a"].next()
                a2, a2k = T["a2"].next()
                s_["a2"] = (a2, a2k)
                kb.op("act", lambda e: e.activation(out=a, in_=r, func=AF.Exp, scale=SP8(i)), reads=[rk], writes=[ak])
                kb.op("act", lambda e: e.activation(out=a2, in_=r, func=AF.Exp, scale=SP16(i)), reads=[rk], writes=[a2k])
                s_["cx"]["a"] = (a, ak)
            for s_ in st:
                a2, a2k = s_["a2"]
                ig, igk = s_["ig"]
                xc, xck = s_["cx"]["xc"]
                kb.op("dve", lambda e: e.tensor_scalar(out=a2, in0=a2, scalar1=1.0, scalar2=0.0, op0=ALU.subtract, op1=ALU.min), reads=[a2k], writes=[a2k])
                bt, btk = T["bt"].next()
                kb.op("dve", lambda e: e.tensor_tensor(out=bt, in0=ig, in1=xc, op=ALU.mult), reads=[igk, xck], writes=[btk])
                s_["cx"]["bt"] = (bt, btk)
            for s_ in st:
                a2, a2k = s_["a2"]
                mu, muk = T["mu"].next()
                kb.op("act", lambda e: e.activation(out=mu, in_=a2, func=AF.Sqrt, scale=-1.0), reads=[a2k], writes=[muk])
                s_["cx"]["mu"] = (mu, muk)

        def lru_c(cx):
            i, t = cx["i"], cx["t"]
            a, ak = cx["a"]
            bt, btk = cx["bt"]
            mu, muk = cx["mu"]
            gs, gsk = cx["gs"]
            kb.op("dve", lambda e: e.tensor_tensor(out=bt, in0=bt, in1=mu, op=ALU.mult), reads=[btk, muk], writes=[btk])
            h, hk = hb.next()
            hprev = lru_state["hprev"] if t > 0 else None
            if hprev is None:
                kb.op("dve", lambda e: e.tensor_tensor_scan(out=h, data0=a, data1=bt, initial=0.0, op0=ALU.mult, op1=ALU.add),
                      reads=[ak, btk], writes=[hk])
            else:
                kb.op("dve", lambda e: e.tensor_tensor_scan(out=h, data0=a, data1=bt, initial=hprev[0][:, TT - 1:TT], op0=ALU.mult, op1=ALU.add),
                      reads=[ak, btk, hprev[1]], writes=[hk])
            y, yk = yo.next()
            kb.op("pool", lambda e: e.tensor_tensor(out=y, in0=h, in1=gs, op=ALU.mult), reads=[hk, gsk], writes=[yk])
            kb.dma(yk, u1_s[i * 128:(i + 1) * 128, t * TT:(t + 1) * TT], y, reads=[yk], writes=["u1_%d_%d" % (i, t)])
            lru_state["hprev"] = (h, hk)

        lru_state = {"hprev": None}
        pipe = []
        nxt = [getw(0), getw(512)]
        flush_casts()
        nB = 0
        nC = 0
        for i in range(4):
            (wx, wxk), (wg, wgk) = nxt
            load_wres_one(W["od_out"], wres_todo.pop(0), stg)
            nxt = [getw((i + 1) * 128), getw(512 + (i + 1) * 128)] if i < 3 else [getw(1024), getw(1792)]
            prev = None
            for t in range(NT):
                if t == min(3, NT - 1):
                    flush_casts()
                cx = dict(i=i, t=t, wx=wx, wxk=wxk, wg=wg, wgk=wgk, prev=prev)
                lru_a(cx)
                prev = cx["xb"]
                pipe.append(cx)
                if len(pipe) % 2 == 0:
                    if len(pipe) - nB >= 4:
                        lru_b(pipe[nB:nB + 2])
                        nB += 2
                    while nB - nC > 2:
                        lru_c(pipe[nC])
                        nC += 1
        while nB < len(pipe):
            lru_b(pipe[nB:nB + 2])
            nB += 2
        while nC < len(pipe):
            lru_c(pipe[nC])
            nC += 1
        flush_casts()
        kb.barrier()
        arena.off = lru_mark
        sqb = Slots("sqb", [arena.bf16(TT) for _ in range(2)])
        rqs = Slots("rq", [arena.f32(TT) for _ in range(2)])
        rows = Slots("row", [arena.bf16(S) for _ in range(3)])
        qk_pend = [None]

        def qk_b(bp, bs, row, rk, t, d, gcol):
            rq, rqk = rqs.next()
            kb.op("act", lambda e: e.activation(out=rq, in_=banks[bs][:, :], func=AF.Ln, bias=EPSC, scale=1.0 / 64.0), reads=["B%d" % bs], writes=[rqk])
            kb.op("act", lambda e: e.activation(out=rq, in_=rq, func=AF.Exp, scale=-0.5), reads=[rqk], writes=[rqk])
            if d == 1:
                yv, pin, rin = row[:, t * TT:(t + 1) * TT], banks[bp][:, :], rq
            else:
                yv = row.rearrange("p (r m) -> p m r", r=d)[:, t * (TT // d):(t + 1) * (TT // d), :]
                pin = banks[bp][:, :].rearrange("p (m r) -> p m r", r=d)
                rin = rq.rearrange("p (m r) -> p m r", r=d)
            kb.op("dve", lambda e: e.scalar_tensor_tensor(out=yv, in0=pin, scalar=gcol, in1=rin, op0=ALU.mult, op1=ALU.add if False else ALU.mult),
                  reads=["B%d" % bp, rqk], writes=[rk + "_%d" % t])

        for c in range(6):
            d = DILS[c // 2]
            (wq, wqk), (wk_, wkk) = nxt
            if wres_todo:
                load_wres_one(W["od_out"], wres_todo.pop(0), stg)
            nxt = [getw(1024 + (c + 1) * 128), getw(1792 + (c + 1) * 128)] if c < 5 else [getw(2560)]
            for (wt, wkey, gcol, dst, nm) in ((wq, wqk, QG8, qn_s, "qn"), (wk_, wkk, V("kg"), kn_s, "kn")):
                row, rk = rows.next()
                for t in range(NT):
                    if t == min(3, NT - 1):
                        flush_casts()
                    bp = prr.next()
                    proj(bp, wt, wkey, t)
                    sq, sqk = sqb.next()
                    kb.op("act", lambda e: e.activation(out=sq, in_=banks[bp][:, :], func=AF.Square), reads=["B%d" % bp], writes=[sqk])
                    bs = srr.next()
                    kb.mm([lambda e: e.matmul(banks[bs][:, :], lhsT=blk_b[:, :], rhs=sq, start=True, stop=True)], reads=[sqk], writes=["B%d" % bs])
                    if qk_pend[0] is not None:
                        qk_b(*qk_pend[0])
                    qk_pend[0] = (bp, bs, row, rk, t, d, gcol)
                if qk_pend[0] is not None:
                    qk_b(*qk_pend[0])
                    qk_pend[0] = None
                kb.dma(rk, dst[c * 128:(c + 1) * 128, :], row, reads=[rk + "_%d" % t for t in range(NT)], writes=["%s_%d" % (nm, c)])
        for c in range(6):
            d = DILS[c // 2]
            (wv, wvk), = nxt
            if c < 5:
                nxt = [getw(2560 + (c + 1) * 128)]
            row, rk = rows.next()
            for t in range(NT):
                if t == min(3, NT - 1):
                    flush_casts()
                bp = prr.next()
                proj(bp, wv, wvk, t)
                if d == 1:
                    yv, pin = row[:, t * TT:(t + 1) * TT], banks[bp][:, :]
                else:
                    yv = row.rearrange("p (r m) -> p m r", r=d)[:, t * (TT // d):(t + 1) * (TT // d), :]
                    pin = banks[bp][:, :].rearrange("p (m r) -> p m r", r=d)
                kb.op("act", lambda e: e.activation(out=yv, in_=pin, func=AF.Copy),
                      reads=["B%d" % bp], writes=[rk + "_%d" % t])
            kb.dma(rk, v_s[c * 128:(c + 1) * 128, :], row, reads=[rk + "_%d" % t for t in range(NT)], writes=["v_%d" % c])
        flush_casts()
        kb.barrier()

    def phase_qa():
        arena.reset()
        arenaH.reset()
        qrow = arenaH.bf16(2 * S).rearrange("p (c s) -> p c s", c=2)
        krz = arenaH.bf16(4 * S).rearrange("p (h c s) -> p h c s", h=2, c=2)
        Vt = arenaH.bf16(NBLK * 256).rearrange("p (b c m) -> p b c m", b=NBLK, c=2)
        vrow = arena.bf16(2 * S).rearrange("p (c s) -> p c s", c=2)
        accN = arena.f32(2 * S).rearrange("p (c s) -> p c s", c=2)
        accD = arena.f32(2 * S).rearrange("p (c s) -> p c s", c=2)
        pts = Slots("pt", [arena.bf16(2 * 2 * 256).rearrange("p (c h n) -> p c h n", c=2, h=2) for _ in range(3)])
        yrow = qrow
        for hh in range(2):
            oh = 1 - hh
            kb.op("pool", lambda e: e.memset(krz[oh * 64:(oh + 1) * 64, hh, :, :], 0.0), writes=["krzz%d" % hh])
        srr = RR([0, 1, 2, 3])
        trr = RR([6, 7])
        qa_pend = [None]

        def qa_b(g, d, NB, r, kbk, gblk, pt, ptk):
            def pv(bank_id, col0, first):
                ba = banks[bank_id][:, :].rearrange("p (i n) -> p i n", i=4)
                fns = []
                for hh in range(2):
                    for c in range(2):
                        fns.append(lambda e, hh=hh, c=c: e.matmul(ba[hh * 64:(hh + 1) * 64, c, :], lhsT=Vt[:, gblk, c, hh * 64:(hh + 1) * 64],
                                                                  rhs=pt[:, c, hh, col0:col0 + 128], start=(first and c == 0), stop=False,
                                                                  skip_group_check=True))
                        fns.append(lambda e, hh=hh, c=c: e.matmul(ba[hh * 64:(hh + 1) * 64, 2 + c, :], lhsT=ones_b[:, 0:64],
                                                                  rhs=pt[:, c, hh, col0:col0 + 128], start=False, stop=(not first and c == 1),
                                                                  skip_group_check=True))
                kb.mm(fns, reads=["Vt", ptk + "c0", ptk + "c1"], writes=["B%d" % bank_id])
            bcur = 4 + (kbk % 2)
            pv(bcur, 0, first=(kbk == 0))
            ba = banks[bcur][:, :].rearrange("p (i n) -> p i n", i=4)
            if d == 1:
                dN = accN[:, :, kbk * 128:(kbk + 1) * 128]
                dD = accD[:, :, kbk * 128:(kbk + 1) * 128]
            else:
                dN = accN.rearrange("p c (m r) -> p c r m", r=d)[:, :, r, kbk * 128:(kbk + 1) * 128]
                dD = accD.rearrange("p c (m r) -> p c r m", r=d)[:, :, r, kbk * 128:(kbk + 1) * 128]
            for c in range(2):
                if g == 0:
                    kb.op("dve", lambda e: e.tensor_copy(out=dN[:, c, :], in_=ba[:, c, :]), reads=["B%d" % bcur], writes=["accN"])
                    kb.op("dve", lambda e: e.tensor_copy(out=dD[:, c, :], in_=ba[:, 2 + c, :]), reads=["B%d" % bcur], writes=["accD"])
                else:
                    kb.op("dve", lambda e: e.tensor_tensor(out=dN[:, c, :], in0=ba[:, c, :], in1=dN[:, c, :], op=ALU.add), reads=["B%d" % bcur, "accN"], writes=["accN"])
                    kb.op("dve", lambda e: e.tensor_tensor(out=dD[:, c, :], in0=ba[:, 2 + c, :], in1=dD[:, c, :], op=ALU.add), reads=["B%d" % bcur, "accD"], writes=["accD"])
            if kbk + 1 < NB:
                pv(4 + ((kbk + 1) % 2), 128, first=True)

        for g in range(3):
            d = DILS[g]
            L = S // d
            NB = L // 128
            kb.dma("qrow", qrow, fm(qn_s[g * 256:(g + 1) * 256, :]), reads=["qn_all"], writes=["qrow"])
            kb.dma("vrow", vrow, fm(v_s[g * 256:(g + 1) * 256, :]), reads=["v_all"], writes=["vrow"])
            for hh in range(2):
                kb.dma("krz%d" % hh, krz[hh * 64:(hh + 1) * 64, hh, :, :],
                       fm(kn_s[g * 256:(g + 1) * 256, :])[hh * 64:(hh + 1) * 64, :, :], reads=["kn_all"], writes=["krz"])
            for c in range(2):
                for b4 in range(NBLK // 4):
                    bid = trr.next()
                    bb = banks[bid][:, :].bitcast(BF16)
                    kb.mm([lambda e, q=q: e.transpose(bb[:, q * 128:(q + 1) * 128], vrow[:, c, (b4 * 4 + q) * 128:(b4 * 4 + q + 1) * 128], ident_b[:, :]) for q in range(4)],
                          reads=["vrow"], writes=["B%d" % bid])
                    kb.op("act", lambda e: e.activation(out=Vt[:, b4 * 4:(b4 + 1) * 4, c, :], in_=bb[:, 0:512].rearrange("p (q m) -> p q m", q=4), func=AF.Copy),
                          reads=["B%d" % bid], writes=["Vt"])
            for r in range(d):
                for kbk in range(NB):
                    base = r * L + kbk * 128
                    gblk = r * NB + kbk
                    N = 256 if kbk < NB - 1 else 128
                    pt, ptk = pts.next()
                    for c in range(2):
                        bid = srr.next()
                        bs = banks[bid][:, :].rearrange("p (h n) -> p h n", h=2)
                        kb.mm([lambda e, hh=hh: e.matmul(bs[:, hh, 0:N], lhsT=krz[:, hh, c, base:base + 128], rhs=qrow[:, c, base:base + N], start=True, stop=True)
                               for hh in range(2)], reads=["krz", "krzz0", "krzz1", "qrow"], writes=["B%d" % bid])
                        kb.op("act", lambda e: e.activation(out=pt[:, c, :, 0:N], in_=bs[:, :, 0:N], func=AF.Exp), reads=["B%d" % bid], writes=[ptk + "c%d" % c])
                        kb.op("dve", lambda e: e.tensor_tensor(out=pt[:, c, :, 0:N], in0=pt[:, c, :, 0:N], in1=mask_b[:, :, 0:N], op=ALU.mult),
                              reads=[ptk + "c%d" % c, "mask_b"], writes=[ptk + "c%d" % c])
                    if qa_pend[0] is not None:
                        qa_b(*qa_pend[0])
                    qa_pend[0] = (g, d, NB, r, kbk, gblk, pt, ptk)
            if qa_pend[0] is not None:
                qa_b(*qa_pend[0])
                qa_pend[0] = None
        kb.op("act", lambda e: e.activation(out=accD[:, :, :], in_=accD[:, :, :], func=AF.Ln), reads=["accD"], writes=["accD"])
        kb.op("act", lambda e: e.activation(out=accD[:, :, :], in_=accD[:, :, :], func=AF.Exp, scale=-1.0), reads=["accD"], writes=["accD"])
        kb.op("dve", lambda e: e.tensor_tensor(out=yrow[:, :, :], in0=accN[:, :, :], in1=accD[:, :, :], op=ALU.mult), reads=["accN", "accD", "qrow"], writes=["qrow"])
        kb.dma("qrow", fm(u1_s[512:768, :]), yrow, reads=["qrow"], writes=["u1att"])
        flush_casts()
        kb.barrier()

    def phase_q3():
        arena.reset()
        uts = Slots("ut", [arena.bf16(6 * TT).rearrange("p (c s) -> p c s", c=6) for _ in range(2)])

        def load_u(t):
            ut, uk = uts.next()
            kb.dma(uk, ut, fm(u1_s, t), reads=["u1all"], writes=[uk])
            return ut, uk

        phase_col(6, load_u, lambda h: h, "ffng1")
        flush_casts()
        kb.barrier()

    plist = [("p0", phase_p0), ("p2", phase_p2), ("p3", phase_p3), ("f1_0", lambda: phase_f1(0)),
             ("f2_0", lambda: phase_f2(final=(n_layers == 1)))]
    if n_layers == 2:
        plist += [("q2", phase_q2), ("qa", phase_qa), ("q3", phase_q3),
                  ("f1_1", lambda: phase_f1(1)), ("f2_1", lambda: phase_f2(final=True))]
    for name, fn in plist:
        with nc.named_scope(name):
            fn()
        if stop_after == name:
            break
    kb.barrier()
    return nc


def host_inputs(inp, b):
    vecs, _ = pack_vecs(inp)
    m = {
        "x": np.ascontiguousarray(np.asarray(inp["x"][b], np.float32)),
        "vecs": vecs,
        "bdA": blockdiag(inp["od_lru_wa"][0]),
        "bdX": blockdiag(inp["od_lru_wx"][0]),
        "ev_w_in": np.ascontiguousarray(inp["ev_w_in"][0], dtype=np.float32),
        "ev_w_out": np.ascontiguousarray(inp["ev_w_out"][0], dtype=np.float32),
        "od_w_in": np.ascontiguousarray(inp["od_w_in"][0], dtype=np.float32),
        "od_w_out": np.ascontiguousarray(inp["od_w_out"][0], dtype=np.float32),
    }
    for l in range(2):
        m["ffn_w_gate%d" % l] = np.ascontiguousarray(inp["ffn_w_gate"][l], dtype=np.float32)
        m["ffn_w_up%d" % l] = np.ascontiguousarray(inp["ffn_w_up"][l], dtype=np.float32)
        m["ffn_w_down%d" % l] = np.ascontiguousarray(inp["ffn_w_down"][l], dtype=np.float32)
    return m


def kernel(**inputs):
    inp = {k: np.asarray(v) for k, v in inputs.items()}
    B, S, _ = inp["x"].shape
    nc = build(S)
    in_maps = [host_inputs(inp, b) for b in range(B)]
    res = run_bass_kernel_spmd(nc, in_maps, core_ids=list(range(B)))
    out = np.stack([np.asarray(r["out"], np.float32) for r in res.results], axis=0)
    return out
```

```python
import os
import numpy as np
import concourse.bass as bass
import concourse.mybir as mybir
from concourse.bass_utils import run_bass_kernel_spmd

F32 = mybir.dt.float32
BF16 = mybir.dt.bfloat16
AF = mybir.ActivationFunctionType
ALU = mybir.AluOpType

D = 1024
DFF = 2816
NFF = DFF // 128
EPS = 1e-6
TT = 512
DILS = (1, 4, 16)
N_PE_TAPS = 8
POOL_TAPS = 9


class KB:
    def __init__(self, nc):
        self.nc = nc
        self.eng = {}
        for name, h in (("pe", nc.tensor), ("act", nc.scalar), ("dve", nc.vector), ("pool", nc.gpsimd), ("sp", nc.sync)):
            sem = nc.alloc_semaphore("sem_" + name)
            self.eng[name] = dict(h=h, sem=sem, cnt=0, seen={})
        self.lastw = {}
        self.readers = {}
        self.dma_sems = {}

    def _deps(self, reads, writes):
        deps = []
        for r in reads:
            if r in self.lastw:
                deps.append(self.lastw[r])
        for w in writes:
            if w in self.lastw:
                deps.append(self.lastw[w])
            deps.extend(self.readers.get(w, ()))
        return deps

    def _wait(self, me, deps):
        e = self.eng[me]
        best = {}
        for (sem, val, key) in deps:
            if best.get(key, (None, 0))[1] < val:
                best[key] = (sem, val)
        for key, (sem, val) in best.items():
            if e["seen"].get(key, 0) < val:
                e["h"].wait_ge(sem, val)
                e["seen"][key] = val

    def _commit(self, ticket, reads, writes):
        for r in reads:
            self.readers.setdefault(r, []).append(ticket)
        for w in writes:
            self.lastw[w] = ticket
            self.readers[w] = []

    def op(self, me, fn, reads=(), writes=()):
        e = self.eng[me]
        self._wait(me, self._deps(reads, writes))
        inst = fn(e["h"])
        e["cnt"] += 1
        inst.then_inc(e["sem"], 1)
        t = (e["sem"], e["cnt"], me)
        self._commit(t, reads, writes)
        return t

    def mm(self, fns, reads=(), writes=()):
        e = self.eng["pe"]
        self._wait("pe", self._deps(reads, writes))
        inst = None
        for fn in fns:
            inst = fn(e["h"])
        e["cnt"] += 1
        inst.then_inc(e["sem"], 1)
        t = (e["sem"], e["cnt"], "pe")
        self._commit(t, reads, writes)
        return t

    def dma(self, semkey, out, in_, reads=(), writes=(), q="sp"):
        e = self.eng[q]
        if q == "pool":
            semkey = semkey + "_sw"
        if semkey not in self.dma_sems:
            self.dma_sems[semkey] = [self.nc.alloc_semaphore("dsem_" + semkey), 0]
        s = self.dma_sems[semkey]
        self._wait(q, self._deps(reads, writes))
        inst = e["h"].dma_start(out=out, in_=in_)
        s[1] += 16
        inst.then_inc(s[0], 16)
        t = (s[0], s[1], "dma_" + semkey)
        self._commit(t, reads, writes)
        return t

    def barrier(self):
        for me, e in self.eng.items():
            for other, o in self.eng.items():
                if other != me and o["cnt"] > e["seen"].get(other, 0):
                    e["h"].wait_ge(o["sem"], o["cnt"])
                    e["seen"][other] = o["cnt"]
            for k, (sem, val) in self.dma_sems.items():
                key = "dma_" + k
                if val > e["seen"].get(key, 0):
                    e["h"].wait_ge(sem, val)
                    e["seen"][key] = val
        self.lastw = {}
        self.readers = {}


class Arena:
    def __init__(self, ap_f32):
        self.ap = ap_f32
        self.n = ap_f32.shape[1]
        self.off = 0

    def reset(self):
        self.off = 0

    def f32(self, n):
        assert self.off + n <= self.n, ("arena overflow", self.off, n, self.n)
        v = self.ap[:, self.off:self.off + n]
        self.off += n
        return v

    def bf16(self, n):
        m = (n + 1) // 2
        return self.f32(m).bitcast(BF16)[:, 0:n]


def _col(v):
    v = np.asarray(v, np.float32).reshape(-1)
    return np.ascontiguousarray(v.reshape(-1, 128).T)


def pack_vecs(inp):
    cols = []
    idx = {}

    def add(name, arr2d):
        idx[name] = sum(c.shape[1] for c in cols)
        cols.append(np.ascontiguousarray(arr2d, dtype=np.float32))

    for l in range(2):
        add("mixg%d" % l, _col(inp["mix_norm_g"][l]))
    for l in range(2):
        add("ffng%d" % l, _col(inp["ffn_norm_g"][l]))
    for k in range(3):
        add("scw%d" % k, _col(inp["ev_sc_conv_w"][0, k]))
    for k in range(31):
        add("cfw%d" % k, _col(inp["ev_cf_conv_w"][0, k]))
    add("cfb", _col(inp["ev_cf_conv_b"][0]))
    add("lng", _col(inp["ev_cf_ln_g"][0]))
    add("lnb", _col(inp["ev_cf_ln_b"][0]))
    for k in range(4):
        add("lruw%d" % k, _col(inp["od_lru_conv_w"][0, k]))
    add("lrub", _col(inp["od_lru_conv_b"][0]))
    add("ba", _col(inp["od_lru_ba"][0]))
    add("bx", _col(inp["od_lru_bx"][0]))
    add("lam", _col(inp["od_lru_lam"][0]))
    add("qg", np.tile(np.asarray(inp["od_q_norm_g"][0], np.float32), 2).reshape(128, 1))
    add("kg", np.tile(np.asarray(inp["od_k_norm_g"][0], np.float32), 2).reshape(128, 1))
    for l in range(2):
        for k in range(3):
            add("fcw%d_%d" % (l, k), _col(inp["ffn_conv_w"][l, k]))
        add("fcb%d" % l, _col(inp["ffn_conv_b"][l]))
    return np.ascontiguousarray(np.concatenate(cols, axis=1)), idx


def blockdiag(w):
    w = np.asarray(w, np.float32)
    bd = np.zeros((128, 4, 128), np.float32)
    for c in range(4):
        for hl in range(2):
            bd[hl * 64:(hl + 1) * 64, c, hl * 64:(hl + 1) * 64] = w[2 * c + hl]
    return np.ascontiguousarray(bd.reshape(128, 512))


_VIDX = None


def vec_index():
    global _VIDX
    if _VIDX is None:
        fake = {
            "mix_norm_g": np.zeros((2, 1024)), "ffn_norm_g": np.zeros((2, 1024)),
            "ev_sc_conv_w": np.zeros((1, 3, 512)), "ev_cf_conv_w": np.zeros((1, 31, 512)),
            "ev_cf_conv_b": np.zeros((1, 512)), "ev_cf_ln_g": np.zeros((1, 512)), "ev_cf_ln_b": np.zeros((1, 512)),
            "od_lru_conv_w": np.zeros((1, 4, 512)), "od_lru_conv_b": np.zeros((1, 512)),
            "od_lru_ba": np.zeros((1, 512)), "od_lru_bx": np.zeros((1, 512)), "od_lru_lam": np.zeros((1, 512)),
            "od_q_norm_g": np.zeros((1, 64)), "od_k_norm_g": np.zeros((1, 64)),
            "ffn_conv_w": np.zeros((2, 3, 2816)), "ffn_conv_b": np.zeros((2, 2816)),
        }
        v, idx = pack_vecs(fake)
        _VIDX = (idx, v.shape[1])
    return _VIDX


def build(S, n_layers=2, debug=False, stop_after=None):
    NT = S // TT
    NBLK = S // 128
    VI, NV = vec_index()
    nc = bass.Bass("TRN2", target_bir_lowering=False)
    kb = KB(nc)

    def din(name, shape, dt=F32):
        return nc.dram_tensor(name, shape, dt, kind="ExternalInput").ap()

    skind = "ExternalOutput" if debug else "Internal"

    def dscr(name, shape, dt):
        return nc.dram_tensor(name, shape, dt, kind=skind).ap()

    x_in = din("x", [S, D])
    vecs_d = din("vecs", [128, NV])
    bdA_d = din("bdA", [128, 512])
    bdX_d = din("bdX", [128, 512])
    W = {
        "ev_in": din("ev_w_in", [D, 2560]), "ev_out": din("ev_w_out", [1024, D]),
        "od_in": din("od_w_in", [D, 3328]), "od_out": din("od_w_out", [768, D]),
        "g0": din("ffn_w_gate0", [D, DFF]), "u0": din("ffn_w_up0", [D, DFF]), "d0": din("ffn_w_down0", [DFF, D]),
        "g1": din("ffn_w_gate1", [D, DFF]), "u1": din("ffn_w_up1", [D, DFF]), "d1": din("ffn_w_down1", [DFF, D]),
    }
    out_d = nc.dram_tensor("out", [S, D], F32, kind="ExternalOutput").ap()

    xT_s = dscr("xT_s", [D, S], F32)
    u0_s = dscr("u0_s", [512, S], BF16)
    cf_s = dscr("cf_s", [512, S], F32)
    u1_s = dscr("u1_s", [768, S], BF16)
    act_s = dscr("act_s", [DFF, S], BF16)
    qn_s = dscr("qn_s", [768, S], BF16)
    kn_s = dscr("kn_s", [768, S], BF16)
    v_s = dscr("v_s", [768, S], BF16)

    def fm(ap, t=None):
        v = ap.rearrange("(c p) s -> p c s", p=128)
        if t is not None:
            v = v[:, :, t * TT:(t + 1) * TT]
        return v

    ident_f = nc.alloc_sbuf_tensor("ident_f", [128, 128], F32)
    ident_b = nc.alloc_sbuf_tensor("ident_b", [128, 128], BF16)
    ones_b = nc.alloc_sbuf_tensor("ones_b", [128, 128], BF16)
    onesLN = nc.alloc_sbuf_tensor("onesLN", [128, 128], F32)
    blk_b = nc.alloc_sbuf_tensor("blk_b", [128, 128], BF16)
    mask_b = nc.alloc_sbuf_tensor("mask_b", [128, 2, 256], BF16)
    vecs = nc.alloc_sbuf_tensor("vecs_sb", [128, NV], F32)
    dv = nc.alloc_sbuf_tensor("dv", [128, 16], F32)
    bdA = nc.alloc_sbuf_tensor("bdA_sb", [128, 4, 128], BF16)
    bdX = nc.alloc_sbuf_tensor("bdX_sb", [128, 4, 128], BF16)
    hTraw = nc.alloc_sbuf_tensor("hTraw", [128, 4 * S], F32)
    hT = hTraw[:, :].bitcast(BF16).rearrange("p (c s) -> p c s", c=8)
    arenaH = Arena(hTraw[:, :])
    wres = nc.alloc_sbuf_tensor("wres", [128, NFF * 1024], BF16)
    AR = (nc.sbuf_bytes_remaining - 2048) // 4
    arena_t = nc.alloc_sbuf_tensor("arena", [128, AR], F32)
    arena = Arena(arena_t[:, :])
    banks = [nc.alloc_psum_tensor("bank%d" % i, [128, 512], F32) for i in range(8)]

    def V(name, i=0):
        o = VI[name] + i
        return vecs[:, o:o + 1]

    EPSC = dv[:, 0:1]
    QG8 = dv[:, 1:2]

    def SP8(i):
        return dv[:, 2 + i:3 + i]

    def SP16(i):
        return dv[:, 6 + i:7 + i]

    class RR:
        def __init__(self, ids):
            self.ids = ids
            self.i = 0

        def next(self):
            b = self.ids[self.i % len(self.ids)]
            self.i += 1
            return b

    class Slots:
        def __init__(self, name, aps):
            self.name = name
            self.aps = aps
            self.i = -1

        def next(self):
            self.i += 1
            k = self.i % len(self.aps)
            return self.aps[k], "%s%d" % (self.name, k)

    kb.dma("vecs", vecs[:, :], vecs_d, writes=["vecs"])
    kb.op("pool", lambda e: e.memset(ident_f[:, :], 1.0), writes=["ident_f"])
    kb.op("pool", lambda e: e.affine_select(out=ident_f[:, :], in_=ident_f[:, :], pattern=[[-1, 128]],
                                             compare_op=ALU.is_equal, fill=0.0, base=0, channel_multiplier=1),
          reads=["ident_f"], writes=["ident_f"])
    kb.op("pool", lambda e: e.tensor_copy(out=ident_b[:, :], in_=ident_f[:, :]), reads=["ident_f"], writes=["ident_b"])
    kb.op("pool", lambda e: e.memset(ones_b[:, :], 1.0), writes=["ones_b"])
    kb.op("pool", lambda e: e.memset(onesLN[:, :], 1.0 / 512.0), writes=["onesLN"])
    kb.op("pool", lambda e: e.memset(blk_b[:, :], 0.0), writes=["blk_b"])
    kb.op("pool", lambda e: e.memset(blk_b[0:64, 0:64], 1.0), reads=["blk_b"], writes=["blk_b"])
    kb.op("pool", lambda e: e.memset(blk_b[64:128, 64:128], 1.0), reads=["blk_b"], writes=["blk_b"])
    kb.op("pool", lambda e: e.memset(mask_b[:, :, :], 1.0), writes=["mask_b"])
    for hh in range(2):
        kb.op("pool", lambda e: e.affine_select(out=mask_b[:, hh, 0:128], in_=mask_b[:, hh, 0:128], pattern=[[1, 128]],
                                                 compare_op=ALU.is_ge, fill=0.0, base=0, channel_multiplier=-1),
              reads=["mask_b"], writes=["mask_b"])
        kb.op("pool", lambda e: e.affine_select(out=mask_b[:, hh, 128:256], in_=mask_b[:, hh, 128:256], pattern=[[-1, 128]],
                                                 compare_op=ALU.is_ge, fill=0.0, base=0, channel_multiplier=1),
              reads=["mask_b"], writes=["mask_b"])
    kb.op("dve", lambda e: e.memset(dv[:, :], EPS), writes=["dv"])
    kb.op("dve", lambda e: e.tensor_scalar(out=QG8, in0=V("qg"), scalar1=0.125, scalar2=None, op0=ALU.mult),
          reads=["vecs", "dv"], writes=["dv"])
    lam4 = vecs[:, VI["lam"]:VI["lam"] + 4]
    kb.op("act", lambda e: e.activation(out=dv[:, 10:14], in_=lam4, func=AF.Exp, scale=-1.0), reads=["vecs", "dv"], writes=["dv"])
    kb.op("act", lambda e: e.activation(out=dv[:, 10:14], in_=dv[:, 10:14], func=AF.Ln, bias=1.0, scale=1.0), reads=["dv"], writes=["dv"])
    kb.op("dve", lambda e: e.tensor_scalar(out=dv[:, 2:6], in0=dv[:, 10:14], scalar1=-8.0, scalar2=None, op0=ALU.mult),
          reads=["dv"], writes=["dv"])
    kb.op("dve", lambda e: e.tensor_scalar(out=dv[:, 6:10], in0=dv[:, 10:14], scalar1=-16.0, scalar2=None, op0=ALU.mult),
          reads=["dv"], writes=["dv"])
    arena.reset()
    st0 = arena.f32(1024)
    kb.dma("st0", st0[:, 0:512], bdA_d, writes=["st0"])
    kb.op("pool", lambda e: e.tensor_copy(out=bdA[:, :, :], in_=st0[:, 0:512].rearrange("p (c m) -> p c m", c=4)), reads=["st0"], writes=["bdA"])
    kb.dma("st0b", st0[:, 512:1024], bdX_d, writes=["st0b"])
    kb.op("pool", lambda e: e.tensor_copy(out=bdX[:, :, :], in_=st0[:, 512:1024].rearrange("p (c m) -> p c m", c=4)), reads=["st0b"], writes=["bdX"])
    kb.barrier()

    deferred_casts = []

    def flush_casts():
        while deferred_casts:
            deferred_casts.pop(0)()

    def load_w_chunk(Wap, col0, dst, dkey, stg):
        if len(deferred_casts) >= len(stg.aps):
            flush_casts()
        sap, skey = stg.next()
        src = Wap.rearrange("(kc p) m -> p kc m", p=128)[:, :, col0:col0 + 128]
        kb.dma(skey, sap[:, 0:1024].rearrange("p (kc m) -> p kc m", kc=8), src, writes=[skey])
        deferred_casts.append(lambda: kb.op("pool", lambda e: e.tensor_copy(out=dst, in_=sap[:, 0:1024].rearrange("p (kc m) -> p kc m", kc=8)),
                                            reads=[skey], writes=[dkey]))

    def load_wres_one(Wap, kc, stg):
        if len(deferred_casts) >= len(stg.aps):
            flush_casts()
        sap, skey = stg.next()
        kb.dma(skey, sap[:, 0:1024], Wap[kc * 128:(kc + 1) * 128, :], writes=[skey])
        deferred_casts.append(lambda: kb.op("pool", lambda e: e.tensor_copy(out=wres[:, kc * 1024:(kc + 1) * 1024], in_=sap[:, 0:1024]),
                                            reads=[skey], writes=["wres"]))

    def proj(bank_id, wt, wkey, t, extra_reads=()):
        b = banks[bank_id]
        fns = []
        for kc in range(8):
            fns.append(lambda e, kc=kc: e.matmul(b[:, :], lhsT=wt[:, kc, :], rhs=hT[:, kc, t * TT:(t + 1) * TT],
                                                 start=(kc == 0), stop=(kc == 7)))
        kb.mm(fns, reads=[wkey, "hT%d" % t] + list(extra_reads), writes=["B%d" % bank_id])

    def norm_tile(xt, xkey, t, gname, sq, rs, rinv, statrr):
        kb.op("act", lambda e: e.activation(out=sq[:, :, :], in_=xt[:, :, :], func=AF.Square), reads=[xkey], writes=["sq"])
        bid = statrr.next()
        b = banks[bid]
        kb.mm([lambda e, c=c: e.matmul(b[:, :], lhsT=ones_b[:, :], rhs=sq[:, c, :], start=(c == 0), stop=(c == 7)) for c in range(8)],
              reads=["sq"], writes=["B%d" % bid])
        kb.op("act", lambda e: e.activation(out=rinv, in_=b[:, :], func=AF.Ln, bias=EPSC, scale=1.0 / D), reads=["B%d" % bid], writes=["rinv"])
        kb.op("act", lambda e: e.activation(out=rinv, in_=rinv, func=AF.Exp, scale=-0.5), reads=["rinv"], writes=["rinv"])
        for c in range(8):
            kb.op("dve", lambda e: e.scalar_tensor_tensor(out=hT[:, c, t * TT:(t + 1) * TT], in0=xt[:, c, :], scalar=V(gname, c),
                                                          in1=rinv, op0=ALU.mult, op1=ALU.mult),
                  reads=[xkey, "rinv"], writes=["hT%d" % t])

    def phase_p0():
        arena.reset()
        xin = Slots("xin", [arena.f32(4 * D).rearrange("p (s c) -> p s c", s=4) for _ in range(2)])
        xts = Slots("xt", [arena.f32(8 * TT).rearrange("p (c s) -> p c s", c=8) for _ in range(2)])
        sq = arena.bf16(8 * TT).rearrange("p (c s) -> p c s", c=8)
        rs = arena.f32(TT)
        rinv = arena.f32(TT)
        trr = RR([0, 1, 2, 3])
        srr = RR([4, 5])
        for t in range(NT):
            xi, xik = xin.next()
            kb.dma(xik, xi, x_in[t * TT:(t + 1) * TT, :].rearrange("(s p) c -> p s c", p=128), writes=[xik])
            xt, xk = xts.next()
            for c in range(8):
                bid = trr.next()
                b = banks[bid]
                kb.mm([lambda e, s=s: e.transpose(b[:, s * 128:(s + 1) * 128], xi[:, s, c * 128:(c + 1) * 128], ident_f[:, :]) for s in range(4)],
                      reads=[xik], writes=["B%d" % bid])
                kb.op("act", lambda e: e.activation(out=xt[:, c, :], in_=b[:, :], func=AF.Copy), reads=["B%d" % bid], writes=[xk])
            kb.dma(xk, fm(xT_s, t), xt, reads=[xk], writes=["xTs%d" % t], q="pool")
            norm_tile(xt, xk, t, "mixg0", sq, rs, rinv, srr)
        flush_casts()
        kb.barrier()

    def phase_norm(gname):
        arena.reset()
        xts = Slots("xt", [arena.f32(8 * TT).rearrange("p (c s) -> p c s", c=8) for _ in range(2)])
        sq = arena.bf16(8 * TT).rearrange("p (c s) -> p c s", c=8)
        rs = arena.f32(TT)
        rinv = arena.f32(TT)
        srr = RR([4, 5])
        for t in range(NT):
            xt, xk = xts.next()
            kb.dma(xk, xt, fm(xT_s, t), reads=["xTs%d" % t], writes=[xk])
            norm_tile(xt, xk, t, gname, sq, rs, rinv, srr)
        flush_casts()
        kb.barrier()

    def phase_p2():
        arena.reset()
        stg = Slots("stg", [arena.f32(1024) for _ in range(4)])
        ws = Slots("ws", [arena.bf16(1024).rearrange("p (kc m) -> p kc m", kc=8) for _ in range(9)])
        wres_todo = list(range(8))
        pbuf = Slots("pb", [arena.f32(2 + TT) for _ in range(2)])
        ubuf = Slots("ub", [arena.f32(30 + TT) for _ in range(2)])
        tmpa = Slots("ta", [arena.f32(TT) for _ in range(3)])
        tmpb = Slots("tb", [arena.f32(TT) for _ in range(3)])
        tmpc = Slots("tc", [arena.f32(TT) for _ in range(2)])
        tmpd = Slots("td", [arena.f32(TT) for _ in range(4)])
        yo = Slots("yo", [arena.bf16(TT) for _ in range(3)])
        co = Slots("co", [arena.f32(TT) for _ in range(3)])
        prr = RR([0, 1, 2, 3])
        crr = RR([4, 5, 6, 7])
        dgs = Slots("dg", [arena.f32(N_PE_TAPS * 128).rearrange("p (n m) -> p n m", n=N_PE_TAPS) for _ in range(2)])
        Wi = W["ev_in"]

        def getw(col0):
            wt, wk = ws.next()
            load_w_chunk(Wi, col0, wt, wk, stg)
            return wt, wk

        prr_sc = RR([0, 1, 2, 3, 4, 5, 6, 7])
        nxt = [getw(512 + 0), getw(1024 + 0), getw(0)]
        flush_casts()
        for i in range(4):
            (wc, wck), (wx, wxk), (wb, wbk) = nxt
            load_wres_one(W["ev_out"], wres_todo.pop(0), stg)
            if i < 3:
                nxt = [getw(512 + (i + 1) * 128), getw(1024 + (i + 1) * 128), getw((i + 1) * 128)]
            else:
                nxt = [getw(1536), getw(2048)]
            prev = None
            for t in range(NT):
                if t == min(3, NT - 1):
                    flush_casts()
                bc, bx, bb = prr_sc.next(), prr_sc.next(), prr_sc.next()
                proj(bc, wc, wck, t)
                proj(bx, wx, wxk, t)
                proj(bb, wb, wbk, t)
                csb, ck = tmpa.next()
                kb.op("act", lambda e: e.activation(out=csb, in_=banks[bc][:, :], func=AF.Copy), reads=["B%d" % bc], writes=[ck])
                pb, pk = pbuf.next()
                if prev is None:
                    kb.op("dve", lambda e: e.memset(pb[:, 0:2], 0.0), writes=[pk + "h"])
                else:
                    kb.op("dve", lambda e: e.tensor_copy(out=pb[:, 0:2], in_=prev[0][:, TT:TT + 2]), reads=[prev[1]], writes=[pk + "h"])
                kb.op("dve", lambda e: e.tensor_tensor(out=pb[:, 2:2 + TT], in0=banks[bx][:, :], in1=csb, op=ALU.mult),
                      reads=["B%d" % bx, ck], writes=[pk])
                acc, ak = tmpb.next()
                kb.op("dve", lambda e: e.tensor_scalar(out=acc, in0=pb[:, 2:2 + TT], scalar1=V("scw2", i), scalar2=None, op0=ALU.mult),
                      reads=[pk], writes=[ak])
                kb.op("dve", lambda e: e.scalar_tensor_tensor(out=acc, in0=pb[:, 1:1 + TT], scalar=V("scw1", i), in1=acc, op0=ALU.mult, op1=ALU.add),
                      reads=[pk, pk + "h", ak], writes=[ak])
                kb.op("dve", lambda e: e.scalar_tensor_tensor(out=acc, in0=pb[:, 0:TT], scalar=V("scw0", i), in1=acc, op0=ALU.mult, op1=ALU.add),
                      reads=[pk, pk + "h", ak], writes=[ak])
                y, yk = yo.next()
                kb.op("dve", lambda e: e.tensor_tensor(out=y, in0=banks[bb][:, :], in1=acc, op=ALU.mult), reads=["B%d" % bb, ak], writes=[yk])
                kb.dma(yk, u0_s[i * 128:(i + 1) * 128, t * TT:(t + 1) * TT], y, reads=[yk], writes=["u0_%d_%d" % (i, t)])
                prev = (pb, pk)
        cf_pend = [None]
        PE_TAPS = list(range(0, N_PE_TAPS))
        DVE_TAPS = list(range(N_PE_TAPS, 30 - POOL_TAPS))
        POOL_T = list(range(30 - POOL_TAPS, 30))

        def cf_b(acc, ak, acc2, a2k, pacc, pak, i, t):
            c_, ck_ = co.next()
            kb.op("dve", lambda e: e.scalar_tensor_tensor(out=c_, in0=acc2, scalar=V("cfb", i), in1=pacc, op0=ALU.add, op1=ALU.add),
                  reads=[a2k, pak], writes=[ck_])
            kb.op("dve", lambda e: e.tensor_tensor(out=c_, in0=acc, in1=c_, op=ALU.add), reads=[ak, ck_], writes=[ck_])
            kb.dma(ck_, cf_s[i * 128:(i + 1) * 128, t * TT:(t + 1) * TT], c_, reads=[ck_], writes=["cf_%d_%d" % (i, t)])

        for i in range(4):
            (wa, wak), (wg, wgk) = nxt
            load_wres_one(W["ev_out"], wres_todo.pop(0), stg)
            if i < 3:
                nxt = [getw(1536 + (i + 1) * 128), getw(2048 + (i + 1) * 128)]
            dg, dgk = dgs.next()
            for n, k in enumerate(PE_TAPS):
                kb.op("pool", lambda e: e.tensor_scalar(out=dg[:, n, :], in0=ident_f[:, :], scalar1=V("cfw%d" % k, i), scalar2=0.0,
                                                        op0=ALU.mult, op1=ALU.add), reads=["ident_f"], writes=[dgk])
            prev = None
            for t in range(NT):
                if t == min(3, NT - 1):
                    flush_casts()
                ba_, bg_ = prr.next(), prr.next()
                proj(ba_, wa, wak, t)
                proj(bg_, wg, wgk, t)
                sg, sk = tmpa.next()
                kb.op("act", lambda e: e.activation(out=sg, in_=banks[bg_][:, :], func=AF.Sigmoid), reads=["B%d" % bg_], writes=[sk])
                ub, uk = ubuf.next()
                if prev is None:
                    kb.op("dve", lambda e: e.memset(ub[:, 0:30], 0.0), writes=[uk + "h"])
                else:
                    kb.op("dve", lambda e: e.tensor_copy(out=ub[:, 0:30], in_=prev[0][:, TT:TT + 30]), reads=[prev[1]], writes=[uk + "h"])
                kb.op("dve", lambda e: e.tensor_tensor(out=ub[:, 30:30 + TT], in0=banks[ba_][:, :], in1=sg, op=ALU.mult),
                      reads=["B%d" % ba_, sk], writes=[uk])
                cb2 = crr.next()
                acc2, a2k = banks[cb2][:, :], "B%d" % cb2
                kb.mm([lambda e, n=n, k=k: e.matmul(acc2, lhsT=dg[:, n, :], rhs=ub[:, k:k + TT], start=(n == 0), stop=(n == len(PE_TAPS) - 1))
                       for n, k in enumerate(PE_TAPS)], reads=[uk, uk + "h", dgk], writes=[a2k])
                cb = crr.next()
                acc, ak = banks[cb][:, :], "B%d" % cb
                kb.op("dve", lambda e: e.tensor_scalar(out=acc, in0=ub[:, 30:30 + TT], scalar1=V("cfw30", i), scalar2=None,
                                                       op0=ALU.mult), reads=[uk], writes=[ak])
                for k in DVE_TAPS:
                    kb.op("dve", lambda e: e.scalar_tensor_tensor(out=acc, in0=ub[:, k:k + TT], scalar=V("cfw%d" % k, i), in1=acc,
                                                                  op0=ALU.mult, op1=ALU.add), reads=[uk, uk + "h", ak], writes=[ak])
                pacc, pak = tmpc.next()
                for n, k in enumerate(POOL_T):
                    if n == 0:
                        dst, dk = pacc, pak
                    else:
                        dst, dk = tmpd.next()
                    kb.op("act", lambda e: e.activation(out=dst, in_=ub[:, k:k + TT], func=AF.Identity, scale=V("cfw%d" % k, i)),
                          reads=[uk, uk + "h"], writes=[dk])
                    if n > 0:
                        kb.op("pool", lambda e: e.tensor_tensor(out=pacc, in0=pacc, in1=dst, op=ALU.add), reads=[pak, dk], writes=[pak])
                if cf_pend[0] is not None:
                    cf_b(*cf_pend[0])
                cf_pend[0] = (acc, ak, acc2, a2k, pacc, pak, i, t)
                prev = (ub, uk)
        cf_b(*cf_pend[0])
        flush_casts()
        kb.barrier()

    def phase_col(KC, load_fn, prep_fn, gname, final=False):
        xts = Slots("xt", [arena.f32(8 * TT).rearrange("p (c s) -> p c s", c=8) for _ in range(2)])
        if gname is not None:
            sq = arena.bf16(8 * TT).rearrange("p (c s) -> p c s", c=8)
            rs = None
            rinv = arena.f32(TT)
        if final:
            arenaH.reset()
            ots = Slots("ot", [arenaH.f32(4 * D).rearrange("p (s c) -> p s c", s=4) for _ in range(2)])
        orr = RR([0, 1, 2, 3])
        srr = RR([4, 5])
        trr = RR([6, 7])
        def loadx(t):
            xt, xk = xts.next()
            kb.dma(xk, xt, fm(xT_s, t), reads=["xTs%d" % t], writes=[xk])
            return (xt, xk)

        HU = {}
        HX = {}
        P = {}
        for t0 in range(min(2, NT)):
            HU[t0] = load_fn(t0)
            HX[t0] = loadx(t0)
        P[0] = prep_fn(HU[0])
        for t in range(NT):
            xt, xk = HX[t]
            if t + 1 < NT:
                P[t + 1] = prep_fn(HU[t + 1])
            ut, uk = P[t]
            for m in range(8):
                bid = orr.next()
                b = banks[bid]
                kb.mm([lambda e, kc=kc: e.matmul(b[:, :], lhsT=wres[:, kc * 1024 + m * 128: kc * 1024 + (m + 1) * 128], rhs=ut[:, kc, :],
                                                 start=(kc == 0), stop=(kc == KC - 1)) for kc in range(KC)],
                      reads=["wres"] + (uk if isinstance(uk, list) else [uk]), writes=["B%d" % bid])
                kb.op("dve", lambda e: e.tensor_tensor(out=xt[:, m, :], in0=b[:, :], in1=xt[:, m, :], op=ALU.add), reads=["B%d" % bid, xk], writes=[xk])
            if t + 2 < NT:
                HU[t + 2] = load_fn(t + 2)
            if not final:
                kb.dma(xk, fm(xT_s, t), xt, reads=[xk], writes=["xTs%d" % t], q="pool")
                if gname is not None:
                    norm_tile(xt, xk, t, gname, sq, rs, rinv, srr)
            else:
                ot, ok = ots.next()
                for s in range(4):
                    for half in range(2):
                        bid = trr.next()
                        b = banks[bid]
                        kb.mm([lambda e, q=q: e.transpose(b[:, q * 128:(q + 1) * 128], xt[:, half * 4 + q, s * 128:(s + 1) * 128], ident_f[:, :]) for q in range(4)],
                              reads=[xk], writes=["B%d" % bid])
                        kb.op("act", lambda e: e.activation(out=ot[:, s, half * 512:(half + 1) * 512], in_=b[:, :], func=AF.Copy),
                              reads=["B%d" % bid], writes=[ok])
                kb.dma(ok, out_d[t * TT:(t + 1) * TT, :].rearrange("(s p) c -> p s c", p=128), ot, reads=[ok], writes=["out%d" % t], q="pool")
            if t + 2 < NT:
                HX[t + 2] = loadx(t + 2)

    def phase_p3():
        arena.reset()
        uts = Slots("ut", [arena.bf16(8 * TT).rearrange("p (c s) -> p c s", c=8) for _ in range(2)])
        cfts = Slots("cft", [arena.f32(4 * TT).rearrange("p (c s) -> p c s", c=4) for _ in range(2)])
        sq4 = arena.f32(4 * TT).rearrange("p (c s) -> p c s", c=4)
        mean_sb = arena.f32(TT)
        m2 = arena.f32(TT)
        var = m2
        sd = m2
        rinv2 = m2
        tmps = Slots("lt", [arena.f32(TT) for _ in range(2)])
        lrr = RR([6, 7])

        def load_fn(t):
            ut, uk = uts.next()
            kb.dma(uk, ut[:, 0:4, :], fm(u0_s, t), reads=["u0_%d_%d" % (i, t) for i in range(4)], writes=[uk + "lo"])
            cft, ck = cfts.next()
            kb.dma(ck, cft, fm(cf_s, t), reads=["cf_%d_%d" % (i, t) for i in range(4)], writes=[ck])
            return (ut, uk, cft, ck)

        def prep_fn(h):
            ut, uk, cft, ck = h
            bm = lrr.next()
            kb.mm([lambda e, i=i: e.matmul(banks[bm][:, :], lhsT=onesLN[:, :], rhs=cft[:, i, :], start=(i == 0), stop=(i == 3)) for i in range(4)],
                  reads=[ck], writes=["B%d" % bm])
            kb.op("act", lambda e: e.activation(out=sq4[:, :, :], in_=cft[:, :, :], func=AF.Square), reads=[ck], writes=["sq4"])
            be = lrr.next()
            kb.mm([lambda e, i=i: e.matmul(banks[be][:, :], lhsT=onesLN[:, :], rhs=sq4[:, i, :], start=(i == 0), stop=(i == 3)) for i in range(4)],
                  reads=["sq4"], writes=["B%d" % be])
            kb.op("act", lambda e: e.activation(out=mean_sb, in_=banks[bm][:, :], func=AF.Copy), reads=["B%d" % bm], writes=["mean"])
            kb.op("dve", lambda e: e.tensor_tensor(out=m2, in0=mean_sb, in1=mean_sb, op=ALU.mult), reads=["mean"], writes=["m2"])
            kb.op("dve", lambda e: e.tensor_tensor(out=var, in0=banks[be][:, :], in1=m2, op=ALU.subtract), reads=["B%d" % be, "m2"], writes=["m2"])
            kb.op("dve", lambda e: e.tensor_scalar(out=var, in0=var, scalar1=0.0, scalar2=None, op0=ALU.max), reads=["m2"], writes=["m2"])
            kb.op("act", lambda e: e.activation(out=sd, in_=var, func=AF.Ln, bias=EPSC, scale=1.0), reads=["m2"], writes=["m2"])
            kb.op("act", lambda e: e.activation(out=rinv2, in_=sd, func=AF.Exp, scale=-0.5), reads=["m2"], writes=["m2"])
            for i in range(4):
                tp, tk = tmps.next()
                kb.op("dve", lambda e: e.tensor_tensor(out=tp, in0=cft[:, i, :], in1=mean_sb, op=ALU.subtract), reads=[ck, "mean"], writes=[tk])
                kb.op("dve", lambda e: e.tensor_tensor(out=tp, in0=tp, in1=rinv2, op=ALU.mult), reads=[tk, "m2"], writes=[tk])
                kb.op("act", lambda e: e.activation(out=ut[:, 4 + i, :], in_=tp, func=AF.Silu, bias=V("lnb", i), scale=V("lng", i)),
                      reads=[tk], writes=[uk])
            return ut, [uk, uk + "lo"]

        phase_col(8, load_fn, prep_fn, "ffng0")
        flush_casts()
        kb.barrier()

    def phase_f1(l):
        arena.reset()
        stg = Slots("stg", [arena.f32(1024) for _ in range(3)])
        ws = Slots("ws", [arena.bf16(1024).rearrange("p (kc m) -> p kc m", kc=8) for _ in range(6)])
        gbuf = Slots("gb", [arena.f32(2 + TT) for _ in range(3)])
        tmpb = Slots("tb", [arena.f32(TT) for _ in range(3)])
        tmps = Slots("tsl", [arena.f32(TT) for _ in range(3)])
        ao = Slots("ao", [arena.bf16(TT) for _ in range(4)])
        prr = RR([0, 1, 2, 3, 4, 5, 6, 7])
        Wg, Wu = W["g%d" % l], W["u%d" % l]

        def getw(Wap, col0):
            wt, wk = ws.next()
            load_w_chunk(Wap, col0, wt, wk, stg)
            return wt, wk

        ups = Slots("up", [arena.bf16(TT) for _ in range(3)])

        def stage_b(acc, ak, up, upk, j, t):
            sl, sk = tmps.next()
            kb.op("act", lambda e: e.activation(out=sl, in_=acc, func=AF.Silu), reads=[ak], writes=[sk])
            a_, ak_ = ao.next()
            kb.op("pool", lambda e: e.tensor_tensor(out=a_, in0=up, in1=sl, op=ALU.mult), reads=[upk, sk], writes=[ak_])
            kb.dma(ak_, act_s[j * 128:(j + 1) * 128, t * TT:(t + 1) * TT], a_, reads=[ak_], writes=["act_%d_%d" % (j, t)])

        pend = None
        nxt = [getw(Wg, 0), getw(Wu, 0)]
        flush_casts()
        for j in range(NFF):
            (wg, wgk), (wu, wuk) = nxt
            if j + 1 < NFF:
                nxt = [getw(Wg, (j + 1) * 128), getw(Wu, (j + 1) * 128)]
            load_wres_one(W["d%d" % l], j, stg)
            prev = None
            for t in range(NT):
                if t == min(3, NT - 1):
                    flush_casts()
                bg_, bu_ = prr.next(), prr.next()
                proj(bg_, wg, wgk, t)
                proj(bu_, wu, wuk, t)
                gb, gk = gbuf.next()
                if prev is None:
                    kb.op("dve", lambda e: e.memset(gb[:, 0:2], 0.0), writes=[gk + "h"])
                else:
                    kb.op("dve", lambda e: e.tensor_copy(out=gb[:, 0:2], in_=prev[0][:, TT:TT + 2]), reads=[prev[1]], writes=[gk + "h"])
                kb.op("act", lambda e: e.activation(out=gb[:, 2:2 + TT], in_=banks[bg_][:, :], func=AF.Copy), reads=["B%d" % bg_], writes=[gk])
                acc, ak = tmpb.next()
                kb.op("act", lambda e: e.activation(out=acc, in_=banks[bg_][:, :], func=AF.Identity, bias=V("fcb%d" % l, j), scale=V("fcw%d_2" % l, j)),
                      reads=["B%d" % bg_], writes=[ak])
                up, upk = ups.next()
                kb.op("act", lambda e: e.activation(out=up, in_=banks[bu_][:, :], func=AF.Copy), reads=["B%d" % bu_], writes=[upk])
                kb.op("dve", lambda e: e.scalar_tensor_tensor(out=acc, in0=gb[:, 1:1 + TT], scalar=V("fcw%d_1" % l, j), in1=acc, op0=ALU.mult, op1=ALU.add),
                      reads=[gk, gk + "h", ak], writes=[ak])
                kb.op("dve", lambda e: e.scalar_tensor_tensor(out=acc, in0=gb[:, 0:TT], scalar=V("fcw%d_0" % l, j), in1=acc, op0=ALU.mult, op1=ALU.add),
                      reads=[gk, gk + "h", ak], writes=[ak])
                if pend is not None:
                    stage_b(*pend)
                pend = (acc, ak, up, upk, j, t)
                prev = (gb, gk)
        stage_b(*pend)
        flush_casts()
        kb.barrier()

    def phase_f2(final):
        arena.reset()
        ats = Slots("at", [arena.bf16(NFF * TT).rearrange("p (c s) -> p c s", c=NFF) for _ in range(2)])

        def load_u(t):
            at, ak = ats.next()
            kb.dma(ak, at, fm(act_s, t), reads=["act_%d_%d" % (j, t) for j in range(NFF)], writes=[ak])
            return at, ak

        phase_col(NFF, load_u, lambda h: h, None if final else "mixg1", final=final)
        flush_casts()
        kb.barrier()

    def phase_q2():
        arena.reset()
        stg = Slots("stg", [arena.f32(1024) for _ in range(3)])
        ws = Slots("ws", [arena.bf16(1024).rearrange("p (kc m) -> p kc m", kc=8) for _ in range(5)])
        wres_todo = list(range(6))
        lru_mark = arena.off
        xbuf = Slots("xb", [arena.f32(3 + TT) for _ in range(3)])
        nsl = {"xc": 4, "a": 4, "bt": 4, "mu": 4, "r": 2, "ig": 2, "a2": 2}
        T = {n: Slots(n, [arena.f32(TT) for _ in range(k)]) for n, k in nsl.items()}
        T["gs"] = Slots("gs", [arena.bf16(TT) for _ in range(6)])
        hb = Slots("hb", [arena.f32(TT) for _ in range(3)])
        xcb = Slots("xcb", [arena.bf16(TT) for _ in range(4)])
        yo = Slots("yo", [arena.bf16(TT) for _ in range(3)])
        prr = RR([0, 1, 2, 3])
        srr = RR([4, 5, 6, 7])
        Wi = W["od_in"]

        def getw(col0):
            wt, wk = ws.next()
            load_w_chunk(Wi, col0, wt, wk, stg)
            return wt, wk

        def lru_a(cx):
            i, t = cx["i"], cx["t"]
            bx_, bg_ = prr.next(), prr.next()
            cx["bg"] = bg_
            proj(bx_, cx["wx"], cx["wxk"], t)
            proj(bg_, cx["wg"], cx["wgk"], t)
            xb, xk = xbuf.next()
            prev = cx["prev"]
            if prev is None:
                kb.op("dve", lambda e: e.memset(xb[:, 0:3], 0.0), writes=[xk + "h"])
            else:
                kb.op("dve", lambda e: e.tensor_copy(out=xb[:, 0:3], in_=prev[0][:, TT:TT + 3]), reads=[prev[1]], writes=[xk + "h"])
            kb.op("act", lambda e: e.activation(out=xb[:, 3:3 + TT], in_=banks[bx_][:, :], func=AF.Copy), reads=["B%d" % bx_], writes=[xk])
            cx["xb"] = (xb, xk)
            gs, gsk = T["gs"].next()
            kb.op("act", lambda e: e.activation(out=gs, in_=banks[bg_][:, :], func=AF.Gelu_apprx_tanh), reads=["B%d" % bg_], writes=[gsk])
            cx["gs"] = (gs, gsk)
            xc, xck = T["xc"].next()
            kb.op("act", lambda e: e.activation(out=xc, in_=banks[bx_][:, :], func=AF.Identity, bias=V("lrub", i), scale=V("lruw3", i)),
                  reads=["B%d" % bx_], writes=[xck])
            for k in (2, 1, 0):
                kb.op("dve", lambda e: e.scalar_tensor_tensor(out=xc, in0=xb[:, k:k + TT], scalar=V("lruw%d" % k, i), in1=xc,
                                                              op0=ALU.mult, op1=ALU.add), reads=[xk, xk + "h", xck], writes=[xck])
            cx["xc"] = (xc, xck)
            xcbf, xcbk = xcb.next()
            kb.op("dve", lambda e: e.tensor_copy(out=xcbf, in_=xc), reads=[xck], writes=[xcbk])
            cx["xcbf"] = (xcbf, xcbk)

        def lru_b(cxs):
            st = []
            for cx in cxs:
                i = cx["i"]
                xcbf, xcbk = cx["xcbf"]
                br_, bi_ = srr.next(), srr.next()
                kb.mm([lambda e, i=i, br_=br_, xcbf=xcbf: e.matmul(banks[br_][:, :], lhsT=bdA[:, i, :], rhs=xcbf, start=True, stop=True)],
                      reads=["bdA", xcbk], writes=["B%d" % br_])
                kb.mm([lambda e, i=i, bi_=bi_, xcbf=xcbf: e.matmul(banks[bi_][:, :], lhsT=bdX[:, i, :], rhs=xcbf, start=True, stop=True)],
                      reads=["bdX", xcbk], writes=["B%d" % bi_])
                st.append(dict(cx=cx, i=i, br=br_, bi=bi_))
            for s_ in st:
                i = s_["i"]
                r, rk = T["r"].next()
                ig, igk = T["ig"].next()
                s_["r"], s_["ig"] = (r, rk), (ig, igk)
                kb.op("act", lambda e: e.activation(out=r, in_=banks[s_["br"]][:, :], func=AF.Sigmoid, bias=V("ba", i)), reads=["B%d" % s_["br"]], writes=[rk])
                kb.op("act", lambda e: e.activation(out=ig, in_=banks[s_["bi"]][:, :], func=AF.Sigmoid, bias=V("bx", i)), reads=["B%d" % s_["bi"]], writes=[igk])
            for s_ in st:
                i = s_["i"]
                r, rk = s_["r"]
                a, ak = T["a"].next()
                a2, a2k = T["a2"].next()
                s_["a2"] = (a2, a2k)
                kb.op("act", lambda e: e.activation(out=a, in_=r, func=AF.Exp, scale=SP8(i)), reads=[rk], writes=[ak])
                kb.op("act", lambda e: e.activation(out=a2, in_=r, func=AF.Exp, scale=SP16(i)), reads=[rk], writes=[a2k])
                s_["cx"]["a"] = (a, ak)
            for s_ in st:
                a2, a2k = s_["a2"]
                ig, igk = s_["ig"]
                xc, xck = s_["cx"]["xc"]
                kb.op("dve", lambda e: e.tensor_scalar(out=a2, in0=a2, scalar1=1.0, scalar2=0.0, op0=ALU.subtract, op1=ALU.min), reads=[a2k], writes=[a2k])
                bt, btk = T["bt"].next()
                kb.op("dve", lambda e: e.tensor_tensor(out=bt, in0=ig, in1=xc, op=ALU.mult), reads=[igk, xck], writes=[btk])
                s_["cx"]["bt"] = (bt, btk)
            for s_ in st:
                a2, a2k = s_["a2"]
                mu, muk = T["mu"].next()
                kb.op("act", lambda e: e.activation(out=mu, in_=a2, func=AF.Sqrt, scale=-1.0), reads=[a2k], writes=[muk])
                s_["cx"]["mu"] = (mu, muk)

        def lru_c(cx):
            i, t = cx["i"], cx["t"]
            a, ak = cx["a"]
            bt, btk = cx["bt"]
            mu, muk = cx["mu"]
            gs, gsk = cx["gs"]
            kb.op("dve", lambda e: e.tensor_tensor(out=bt, in0=bt, in1=mu, op=ALU.mult), reads=[btk, muk], writes=[btk])
            h, hk = hb.next()
            hprev = lru_state["hprev"] if t > 0 else None
            if hprev is None:
                kb.op("dve", lambda e: e.tensor_tensor_scan(out=h, data0=a, data1=bt, initial=0.0, op0=ALU.mult, op1=ALU.add),
                      reads=[ak, btk], writes=[hk])
            else:
                kb.op("dve", lambda e: e.tensor_tensor_scan(out=h, data0=a, data1=bt, initial=hprev[0][:, TT - 1:TT], op0=ALU.mult, op1=ALU.add),
                      reads=[ak, btk, hprev[1]], writes=[hk])
            y, yk = yo.next()
            kb.op("pool", lambda e: e.tensor_tensor(out=y, in0=h, in1=gs, op=ALU.mult), reads=[hk, gsk], writes=[yk])
            kb.dma(yk, u1_s[i * 128:(i + 1) * 128, t * TT:(t + 1) * TT], y, reads=[yk], writes=["u1_%d_%d" % (i, t)])
            lru_state["hprev"] = (h, hk)

        lru_state = {"hprev": None}
        pipe = []
        nxt = [getw(0), getw(512)]
        flush_casts()
        nB = 0
        nC = 0
        for i in range(4):
            (wx, wxk), (wg, wgk) = nxt
            load_wres_one(W["od_out"], wres_todo.pop(0), stg)
            nxt = [getw((i + 1) * 128), getw(512 + (i + 1) * 128)] if i < 3 else [getw(1024), getw(1792)]
            prev = None
            for t in range(NT):
                if t == min(3, NT - 1):
                    flush_casts()
                cx = dict(i=i, t=t, wx=wx, wxk=wxk, wg=wg, wgk=wgk, prev=prev)
                lru_a(cx)
                prev = cx["xb"]
                pipe.append(cx)
                if len(pipe) % 2 == 0:
                    if len(pipe) - nB >= 4:
                        lru_b(pipe[nB:nB + 2])
                        nB += 2
                    while nB - nC > 2:
                        lru_c(pipe[nC])
                        nC += 1
        while nB < len(pipe):
            lru_b(pipe[nB:nB + 2])
            nB += 2
        while nC < len(pipe):
            lru_c(pipe[nC])
            nC += 1
        flush_casts()
        kb.barrier()
        arena.off = lru_mark
        sqb = Slots("sqb", [arena.bf16(TT) for _ in range(2)])
        rqs = Slots("rq", [arena.f32(TT) for _ in range(2)])
        rows = Slots("row", [arena.bf16(S) for _ in range(3)])
        qk_pend = [None]

        def qk_b(bp, bs, row, rk, t, d, gcol):
            rq, rqk = rqs.next()
            kb.op("act", lambda e: e.activation(out=rq, in_=banks[bs][:, :], func=AF.Ln, bias=EPSC, scale=1.0 / 64.0), reads=["B%d" % bs], writes=[rqk])
            kb.op("act", lambda e: e.activation(out=rq, in_=rq, func=AF.Exp, scale=-0.5), reads=[rqk], writes=[rqk])
            if d == 1:
                yv, pin, rin = row[:, t * TT:(t + 1) * TT], banks[bp][:, :], rq
            else:
                yv = row.rearrange("p (r m) -> p m r", r=d)[:, t * (TT // d):(t + 1) * (TT // d), :]
                pin = banks[bp][:, :].rearrange("p (m r) -> p m r", r=d)
                rin = rq.rearrange("p (m r) -> p m r", r=d)
            kb.op("dve", lambda e: e.scalar_tensor_tensor(out=yv, in0=pin, scalar=gcol, in1=rin, op0=ALU.mult, op1=ALU.add if False else ALU.mult),
                  reads=["B%d" % bp, rqk], writes=[rk + "_%d" % t])

        for c in range(6):
            d = DILS[c // 2]
            (wq, wqk), (wk_, wkk) = nxt
            if wres_todo:
                load_wres_one(W["od_out"], wres_todo.pop(0), stg)
            nxt = [getw(1024 + (c + 1) * 128), getw(1792 + (c + 1) * 128)] if c < 5 else [getw(2560)]
            for (wt, wkey, gcol, dst, nm) in ((wq, wqk, QG8, qn_s, "qn"), (wk_, wkk, V("kg"), kn_s, "kn")):
                row, rk = rows.next()
                for t in range(NT):
                    if t == min(3, NT - 1):
                        flush_casts()
                    bp = prr.next()
                    proj(bp, wt, wkey, t)
                    sq, sqk = sqb.next()
                    kb.op("act", lambda e: e.activation(out=sq, in_=banks[bp][:, :], func=AF.Square), reads=["B%d" % bp], writes=[sqk])
                    bs = srr.next()
                    kb.mm([lambda e: e.matmul(banks[bs][:, :], lhsT=blk_b[:, :], rhs=sq, start=True, stop=True)], reads=[sqk], writes=["B%d" % bs])
                    if qk_pend[0] is not None:
                        qk_b(*qk_pend[0])
                    qk_pend[0] = (bp, bs, row, rk, t, d, gcol)
                if qk_pend[0] is not None:
                    qk_b(*qk_pend[0])
                    qk_pend[0] = None
                kb.dma(rk, dst[c * 128:(c + 1) * 128, :], row, reads=[rk + "_%d" % t for t in range(NT)], writes=["%s_%d" % (nm, c)])
        for c in range(6):
            d = DILS[c // 2]
            (wv, wvk), = nxt
            if c < 5:
                nxt = [getw(2560 + (c + 1) * 128)]
            row, rk = rows.next()
            for t in range(NT):
                if t == min(3, NT - 1):
                    flush_casts()
                bp = prr.next()
                proj(bp, wv, wvk, t)
                if d == 1:
                    yv, pin = row[:, t * TT:(t + 1) * TT], banks[bp][:, :]
                else:
                    yv = row.rearrange("p (r m) -> p m r", r=d)[:, t * (TT // d):(t + 1) * (TT // d), :]
                    pin = banks[bp][:, :].rearrange("p (m r) -> p m r", r=d)
                kb.op("act", lambda e: e.activation(out=yv, in_=pin, func=AF.Copy),
                      reads=["B%d" % bp], writes=[rk + "_%d" % t])
            kb.dma(rk, v_s[c * 128:(c + 1) * 128, :], row, reads=[rk + "_%d" % t for t in range(NT)], writes=["v_%d" % c])
        flush_casts()
        kb.barrier()

    def phase_qa():
        arena.reset()
        arenaH.reset()
        qrow = arenaH.bf16(2 * S).rearrange("p (c s) -> p c s", c=2)
        krz = arenaH.bf16(4 * S).rearrange("p (h c s) -> p h c s", h=2, c=2)
        Vt = arenaH.bf16(NBLK * 256).rearrange("p (b c m) -> p b c m", b=NBLK, c=2)
        vrow = arena.bf16(2 * S).rearrange("p (c s) -> p c s", c=2)
        accN = arena.f32(2 * S).rearrange("p (c s) -> p c s", c=2)
        accD = arena.f32(2 * S).rearrange("p (c s) -> p c s", c=2)
        pts = Slots("pt", [arena.bf16(2 * 2 * 256).rearrange("p (c h n) -> p c h n", c=2, h=2) for _ in range(3)])
        yrow = qrow
        for hh in range(2):
            oh = 1 - hh
            kb.op("pool", lambda e: e.memset(krz[oh * 64:(oh + 1) * 64, hh, :, :], 0.0), writes=["krzz%d" % hh])
        srr = RR([0, 1, 2, 3])
        trr = RR([6, 7])
        qa_pend = [None]

        def qa_b(g, d, NB, r, kbk, gblk, pt, ptk):
            def pv(bank_id, col0, first):
                ba = banks[bank_id][:, :].rearrange("p (i n) -> p i n", i=4)
                fns = []
                for hh in range(2):
                    for c in range(2):
                        fns.append(lambda e, hh=hh, c=c: e.matmul(ba[hh * 64:(hh + 1) * 64, c, :], lhsT=Vt[:, gblk, c, hh * 64:(hh + 1) * 64],
                                                                  rhs=pt[:, c, hh, col0:col0 + 128], start=(first and c == 0), stop=False,
                                                                  skip_group_check=True))
                        fns.append(lambda e, hh=hh, c=c: e.matmul(ba[hh * 64:(hh + 1) * 64, 2 + c, :], lhsT=ones_b[:, 0:64],
                                                                  rhs=pt[:, c, hh, col0:col0 + 128], start=False, stop=(not first and c == 1),
                                                                  skip_group_check=True))
                kb.mm(fns, reads=["Vt", ptk + "c0", ptk + "c1"], writes=["B%d" % bank_id])
            bcur = 4 + (kbk % 2)
            pv(bcur, 0, first=(kbk == 0))
            ba = banks[bcur][:, :].rearrange("p (i n) -> p i n", i=4)
            if d == 1:
                dN = accN[:, :, kbk * 128:(kbk + 1) * 128]
                dD = accD[:, :, kbk * 128:(kbk + 1) * 128]
            else:
                dN = accN.rearrange("p c (m r) -> p c r m", r=d)[:, :, r, kbk * 128:(kbk + 1) * 128]
                dD = accD.rearrange("p c (m r) -> p c r m", r=d)[:, :, r, kbk * 128:(kbk + 1) * 128]
            for c in range(2):
                if g == 0:
                    kb.op("dve", lambda e: e.tensor_copy(out=dN[:, c, :], in_=ba[:, c, :]), reads=["B%d" % bcur], writes=["accN"])
                    kb.op("dve", lambda e: e.tensor_copy(out=dD[:, c, :], in_=ba[:, 2 + c, :]), reads=["B%d" % bcur], writes=["accD"])
                else:
                    kb.op("dve", lambda e: e.tensor_tensor(out=dN[:, c, :], in0=ba[:, c, :], in1=dN[:, c, :], op=ALU.add), reads=["B%d" % bcur, "accN"], writes=["accN"])
                    kb.op("dve", lambda e: e.tensor_tensor(out=dD[:, c, :], in0=ba[:, 2 + c, :], in1=dD[:, c, :], op=ALU.add), reads=["B%d" % bcur, "accD"], writes=["accD"])
            if kbk + 1 < NB:
                pv(4 + ((kbk + 1) % 2), 128, first=True)

        for g in range(3):
            d = DILS[g]
            L = S // d
            NB = L // 128
            kb.dma("qrow", qrow, fm(qn_s[g * 256:(g + 1) * 256, :]), reads=["qn_all"], writes=["qrow"])
            kb.dma("vrow", vrow, fm(v_s[g * 256:(g + 1) * 256, :]), reads=["v_all"], writes=["vrow"])
            for hh in range(2):
                kb.dma("krz%d" % hh, krz[hh * 64:(hh + 1) * 64, hh, :, :],
                       fm(kn_s[g * 256:(g + 1) * 256, :])[hh * 64:(hh + 1) * 64, :, :], reads=["kn_all"], writes=["krz"])
            for c in range(2):
                for b4 in range(NBLK // 4):
                    bid = trr.next()
                    bb = banks[bid][:, :].bitcast(BF16)
                    kb.mm([lambda e, q=q: e.transpose(bb[:, q * 128:(q + 1) * 128], vrow[:, c, (b4 * 4 + q) * 128:(b4 * 4 + q + 1) * 128], ident_b[:, :]) for q in range(4)],
                          reads=["vrow"], writes=["B%d" % bid])
                    kb.op("act", lambda e: e.activation(out=Vt[:, b4 * 4:(b4 + 1) * 4, c, :], in_=bb[:, 0:512].rearrange("p (q m) -> p q m", q=4), func=AF.Copy),
                          reads=["B%d" % bid], writes=["Vt"])
            for r in range(d):
                for kbk in range(NB):
                    base = r * L + kbk * 128
                    gblk = r * NB + kbk
                    N = 256 if kbk < NB - 1 else 128
                    pt, ptk = pts.next()
                    for c in range(2):
                        bid = srr.next()
                        bs = banks[bid][:, :].rearrange("p (h n) -> p h n", h=2)
                        kb.mm([lambda e, hh=hh: e.matmul(bs[:, hh, 0:N], lhsT=krz[:, hh, c, base:base + 128], rhs=qrow[:, c, base:base + N], start=True, stop=True)
                               for hh in range(2)], reads=["krz", "krzz0", "krzz1", "qrow"], writes=["B%d" % bid])
                        kb.op("act", lambda e: e.activation(out=pt[:, c, :, 0:N], in_=bs[:, :, 0:N], func=AF.Exp), reads=["B%d" % bid], writes=[ptk + "c%d" % c])
                        kb.op("dve", lambda e: e.tensor_tensor(out=pt[:, c, :, 0:N], in0=pt[:, c, :, 0:N], in1=mask_b[:, :, 0:N], op=ALU.mult),
                              reads=[ptk + "c%d" % c, "mask_b"], writes=[ptk + "c%d" % c])
                    if qa_pend[0] is not None:
                        qa_b(*qa_pend[0])
                    qa_pend[0] = (g, d, NB, r, kbk, gblk, pt, ptk)
            if qa_pend[0] is not None:
                qa_b(*qa_pend[0])
                qa_pend[0] = None
        HS = S // 2
        for c in range(2):
            for hf in range(2):
                sl = slice(hf * HS, (hf + 1) * HS)
                kd = "accDp%d%d" % (c, hf)
                kb.op("act", lambda e: e.activation(out=accD[:, c, sl], in_=accD[:, c, sl], func=AF.Ln), reads=["accD"], writes=[kd])
                kb.op("act", lambda e: e.activation(out=accD[:, c, sl], in_=accD[:, c, sl], func=AF.Exp, scale=-1.0), reads=[kd], writes=[kd])
                kb.op("dve", lambda e: e.tensor_tensor(out=yrow[:, c, sl], in0=accN[:, c, sl], in1=accD[:, c, sl], op=ALU.mult),
                      reads=["accN", kd, "qrow"], writes=["yr%d%d" % (c, hf)])
            kb.dma("yrow%d" % c, u1_s[512 + c * 128:512 + (c + 1) * 128, :], yrow[:, c, :], reads=["yr%d0" % c, "yr%d1" % c], writes=["u1att%d" % c])
        flush_casts()
        kb.barrier()

    def phase_q3():
        arena.reset()
        uts = Slots("ut", [arena.bf16(6 * TT).rearrange("p (c s) -> p c s", c=6) for _ in range(2)])

        def load_u(t):
            ut, uk = uts.next()
            kb.dma(uk, ut, fm(u1_s, t), reads=["u1all"], writes=[uk])
            return ut, uk

        phase_col(6, load_u, lambda h: h, "ffng1")
        flush_casts()
        kb.barrier()

    plist = [("p0", phase_p0), ("p2", phase_p2), ("p3", phase_p3), ("f1_0", lambda: phase_f1(0)),
             ("f2_0", lambda: phase_f2(final=(n_layers == 1)))]
    if n_layers == 2:
        plist += [("q2", phase_q2), ("qa", phase_qa), ("q3", phase_q3),
                  ("f1_1", lambda: phase_f1(1)), ("f2_1", lambda: phase_f2(final=True))]
    for name, fn in plist:
        with nc.named_scope(name):
            fn()
        if stop_after == name:
            break
    kb.barrier()
    return nc


def host_inputs(inp, b):
    vecs, _ = pack_vecs(inp)
    m = {
        "x": np.ascontiguousarray(np.asarray(inp["x"][b], np.float32)),
        "vecs": vecs,
        "bdA": blockdiag(inp["od_lru_wa"][0]),
        "bdX": blockdiag(inp["od_lru_wx"][0]),
        "ev_w_in": np.ascontiguousarray(inp["ev_w_in"][0], dtype=np.float32),
        "ev_w_out": np.ascontiguousarray(inp["ev_w_out"][0], dtype=np.float32),
        "od_w_in": np.ascontiguousarray(inp["od_w_in"][0], dtype=np.float32),
        "od_w_out": np.ascontiguousarray(inp["od_w_out"][0], dtype=np.float32),
    }
    for l in range(2):
        m["ffn_w_gate%d" % l] = np.ascontiguousarray(inp["ffn_w_gate"][l], dtype=np.float32)
        m["ffn_w_up%d" % l] = np.ascontiguousarray(inp["ffn_w_up"][l], dtype=np.float32)
        m["ffn_w_down%d" % l] = np.ascontiguousarray(inp["ffn_w_down"][l], dtype=np.float32)
    return m


def kernel(**inputs):
    inp = {k: np.asarray(v) for k, v in inputs.items()}
    B, S, _ = inp["x"].shape
    nc = build(S)
    in_maps = [host_inputs(inp, b) for b in range(B)]
    res = run_bass_kernel_spmd(nc, in_maps, core_ids=list(range(B)))
    out = np.stack([np.asarray(r["out"], np.float32) for r in res.results], axis=0)
    return out
```

```python
import os
import numpy as np
import concourse.bass as bass
import concourse.mybir as mybir
from concourse.bass_utils import run_bass_kernel_spmd

F32 = mybir.dt.float32
BF16 = mybir.dt.bfloat16
AF = mybir.ActivationFunctionType
ALU = mybir.AluOpType

D = 1024
DFF = 2816
NFF = DFF // 128
EPS = 1e-6
TT = 512
DILS = (1, 4, 16)
N_PE_TAPS = 8
POOL_TAPS = 9


class KB:
    def __init__(self, nc):
        self.nc = nc
        self.eng = {}
        for name, h in (("pe", nc.tensor), ("act", nc.scalar), ("dve", nc.vector), ("pool", nc.gpsimd), ("sp", nc.sync)):
            sem = nc.alloc_semaphore("sem_" + name)
            self.eng[name] = dict(h=h, sem=sem, cnt=0, seen={})
        self.lastw = {}
        self.readers = {}
        self.dma_sems = {}

    def _deps(self, reads, writes):
        deps = []
        for r in reads:
            if r in self.lastw:
                deps.append(self.lastw[r])
        for w in writes:
            if w in self.lastw:
                deps.append(self.lastw[w])
            deps.extend(self.readers.get(w, ()))
        return deps

    def _wait(self, me, deps):
        e = self.eng[me]
        best = {}
        for (sem, val, key) in deps:
            if best.get(key, (None, 0))[1] < val:
                best[key] = (sem, val)
        for key, (sem, val) in best.items():
            if e["seen"].get(key, 0) < val:
                e["h"].wait_ge(sem, val)
                e["seen"][key] = val

    def _commit(self, ticket, reads, writes):
        for r in reads:
            self.readers.setdefault(r, []).append(ticket)
        for w in writes:
            self.lastw[w] = ticket
            self.readers[w] = []

    def op(self, me, fn, reads=(), writes=()):
        e = self.eng[me]
        self._wait(me, self._deps(reads, writes))
        inst = fn(e["h"])
        e["cnt"] += 1
        inst.then_inc(e["sem"], 1)
        t = (e["sem"], e["cnt"], me)
        self._commit(t, reads, writes)
        return t

    def mm(self, fns, reads=(), writes=()):
        e = self.eng["pe"]
        self._wait("pe", self._deps(reads, writes))
        inst = None
        for fn in fns:
            inst = fn(e["h"])
        e["cnt"] += 1
        inst.then_inc(e["sem"], 1)
        t = (e["sem"], e["cnt"], "pe")
        self._commit(t, reads, writes)
        return t

    def dma(self, semkey, out, in_, reads=(), writes=(), q="sp"):
        e = self.eng[q]
        if q == "pool":
            semkey = semkey + "_sw"
        if semkey not in self.dma_sems:
            self.dma_sems[semkey] = [self.nc.alloc_semaphore("dsem_" + semkey), 0]
        s = self.dma_sems[semkey]
        self._wait(q, self._deps(reads, writes))
        inst = e["h"].dma_start(out=out, in_=in_)
        s[1] += 16
        inst.then_inc(s[0], 16)
        t = (s[0], s[1], "dma_" + semkey)
        self._commit(t, reads, writes)
        return t

    def barrier(self):
        for me, e in self.eng.items():
            for other, o in self.eng.items():
                if other != me and o["cnt"] > e["seen"].get(other, 0):
                    e["h"].wait_ge(o["sem"], o["cnt"])
                    e["seen"][other] = o["cnt"]
            for k, (sem, val) in self.dma_sems.items():
                key = "dma_" + k
                if val > e["seen"].get(key, 0):
                    e["h"].wait_ge(sem, val)
                    e["seen"][key] = val
        self.lastw = {}
        self.readers = {}


class Arena:
    def __init__(self, ap_f32):
        self.ap = ap_f32
        self.n = ap_f32.shape[1]
        self.off = 0

    def reset(self):
        self.off = 0

    def f32(self, n):
        assert self.off + n <= self.n, ("arena overflow", self.off, n, self.n)
        v = self.ap[:, self.off:self.off + n]
        self.off += n
        return v

    def bf16(self, n):
        m = (n + 1) // 2
        return self.f32(m).bitcast(BF16)[:, 0:n]


def _col(v):
    v = np.asarray(v, np.float32).reshape(-1)
    return np.ascontiguousarray(v.reshape(-1, 128).T)


def pack_vecs(inp):
    cols = []
    idx = {}

    def add(name, arr2d):
        idx[name] = sum(c.shape[1] for c in cols)
        cols.append(np.ascontiguousarray(arr2d, dtype=np.float32))

    for l in range(2):
        add("mixg%d" % l, _col(inp["mix_norm_g"][l]))
    for l in range(2):
        add("ffng%d" % l, _col(inp["ffn_norm_g"][l]))
    for k in range(3):
        add("scw%d" % k, _col(inp["ev_sc_conv_w"][0, k]))
    for k in range(31):
        add("cfw%d" % k, _col(inp["ev_cf_conv_w"][0, k]))
    add("cfb", _col(inp["ev_cf_conv_b"][0]))
    add("lng", _col(inp["ev_cf_ln_g"][0]))
    add("lnb", _col(inp["ev_cf_ln_b"][0]))
    for k in range(4):
        add("lruw%d" % k, _col(inp["od_lru_conv_w"][0, k]))
    add("lrub", _col(inp["od_lru_conv_b"][0]))
    add("ba", _col(inp["od_lru_ba"][0]))
    add("bx", _col(inp["od_lru_bx"][0]))
    add("lam", _col(inp["od_lru_lam"][0]))
    add("qg", np.tile(np.asarray(inp["od_q_norm_g"][0], np.float32), 2).reshape(128, 1))
    add("kg", np.tile(np.asarray(inp["od_k_norm_g"][0], np.float32), 2).reshape(128, 1))
    for l in range(2):
        for k in range(3):
            add("fcw%d_%d" % (l, k), _col(inp["ffn_conv_w"][l, k]))
        add("fcb%d" % l, _col(inp["ffn_conv_b"][l]))
    return np.ascontiguousarray(np.concatenate(cols, axis=1)), idx


def blockdiag(w):
    w = np.asarray(w, np.float32)
    bd = np.zeros((128, 4, 128), np.float32)
    for c in range(4):
        for hl in range(2):
            bd[hl * 64:(hl + 1) * 64, c, hl * 64:(hl + 1) * 64] = w[2 * c + hl]
    return np.ascontiguousarray(bd.reshape(128, 512))


_VIDX = None


def vec_index():
    global _VIDX
    if _VIDX is None:
        fake = {
            "mix_norm_g": np.zeros((2, 1024)), "ffn_norm_g": np.zeros((2, 1024)),
            "ev_sc_conv_w": np.zeros((1, 3, 512)), "ev_cf_conv_w": np.zeros((1, 31, 512)),
            "ev_cf_conv_b": np.zeros((1, 512)), "ev_cf_ln_g": np.zeros((1, 512)), "ev_cf_ln_b": np.zeros((1, 512)),
            "od_lru_conv_w": np.zeros((1, 4, 512)), "od_lru_conv_b": np.zeros((1, 512)),
            "od_lru_ba": np.zeros((1, 512)), "od_lru_bx": np.zeros((1, 512)), "od_lru_lam": np.zeros((1, 512)),
            "od_q_norm_g": np.zeros((1, 64)), "od_k_norm_g": np.zeros((1, 64)),
            "ffn_conv_w": np.zeros((2, 3, 2816)), "ffn_conv_b": np.zeros((2, 2816)),
        }
        v, idx = pack_vecs(fake)
        _VIDX = (idx, v.shape[1])
    return _VIDX


def build(S, n_layers=2, debug=False, stop_after=None):
    NT = S // TT
    NBLK = S // 128
    VI, NV = vec_index()
    nc = bass.Bass("TRN2", target_bir_lowering=False)
    kb = KB(nc)

    def din(name, shape, dt=F32):
        return nc.dram_tensor(name, shape, dt, kind="ExternalInput").ap()

    skind = "ExternalOutput" if debug else "Internal"

    def dscr(name, shape, dt):
        return nc.dram_tensor(name, shape, dt, kind=skind).ap()

    x_in = din("x", [S, D])
    vecs_d = din("vecs", [128, NV])
    bdA_d = din("bdA", [128, 512])
    bdX_d = din("bdX", [128, 512])
    W = {
        "ev_in": din("ev_w_in", [D, 2560]), "ev_out": din("ev_w_out", [1024, D]),
        "od_in": din("od_w_in", [D, 3328]), "od_out": din("od_w_out", [768, D]),
        "g0": din("ffn_w_gate0", [D, DFF]), "u0": din("ffn_w_up0", [D, DFF]), "d0": din("ffn_w_down0", [DFF, D]),
        "g1": din("ffn_w_gate1", [D, DFF]), "u1": din("ffn_w_up1", [D, DFF]), "d1": din("ffn_w_down1", [DFF, D]),
    }
    out_d = nc.dram_tensor("out", [S, D], F32, kind="ExternalOutput").ap()

    xT_s = dscr("xT_s", [D, S], F32)
    u0_s = dscr("u0_s", [512, S], BF16)
    cf_s = dscr("cf_s", [512, S], F32)
    u1_s = dscr("u1_s", [768, S], BF16)
    act_s = dscr("act_s", [DFF, S], BF16)
    qn_s = dscr("qn_s", [768, S], BF16)
    kn_s = dscr("kn_s", [768, S], BF16)
    v_s = dscr("v_s", [768, S], BF16)

    def fm(ap, t=None):
        v = ap.rearrange("(c p) s -> p c s", p=128)
        if t is not None:
            v = v[:, :, t * TT:(t + 1) * TT]
        return v

    ident_f = nc.alloc_sbuf_tensor("ident_f", [128, 128], F32)
    ident_b = nc.alloc_sbuf_tensor("ident_b", [128, 128], BF16)
    ones_b = nc.alloc_sbuf_tensor("ones_b", [128, 128], BF16)
    onesLN = nc.alloc_sbuf_tensor("onesLN", [128, 128], F32)
    blk_b = nc.alloc_sbuf_tensor("blk_b", [128, 128], BF16)
    mask_b = nc.alloc_sbuf_tensor("mask_b", [128, 2, 256], BF16)
    vecs = nc.alloc_sbuf_tensor("vecs_sb", [128, NV], F32)
    dv = nc.alloc_sbuf_tensor("dv", [128, 16], F32)
    bdA = nc.alloc_sbuf_tensor("bdA_sb", [128, 4, 128], BF16)
    bdX = nc.alloc_sbuf_tensor("bdX_sb", [128, 4, 128], BF16)
    hTraw = nc.alloc_sbuf_tensor("hTraw", [128, 4 * S], F32)
    hT = hTraw[:, :].bitcast(BF16).rearrange("p (c s) -> p c s", c=8)
    arenaH = Arena(hTraw[:, :])
    wres = nc.alloc_sbuf_tensor("wres", [128, NFF * 1024], BF16)
    AR = (nc.sbuf_bytes_remaining - 2048) // 4
    arena_t = nc.alloc_sbuf_tensor("arena", [128, AR], F32)
    arena = Arena(arena_t[:, :])
    banks = [nc.alloc_psum_tensor("bank%d" % i, [128, 512], F32) for i in range(8)]

    def V(name, i=0):
        o = VI[name] + i
        return vecs[:, o:o + 1]

    EPSC = dv[:, 0:1]
    QG8 = dv[:, 1:2]

    def SP8(i):
        return dv[:, 2 + i:3 + i]

    def SP16(i):
        return dv[:, 6 + i:7 + i]

    class RR:
        def __init__(self, ids):
            self.ids = ids
            self.i = 0

        def next(self):
            b = self.ids[self.i % len(self.ids)]
            self.i += 1
            return b

    class Slots:
        def __init__(self, name, aps):
            self.name = name
            self.aps = aps
            self.i = -1

        def next(self):
            self.i += 1
            k = self.i % len(self.aps)
            return self.aps[k], "%s%d" % (self.name, k)

    kb.dma("vecs", vecs[:, :], vecs_d, writes=["vecs"])
    kb.op("pool", lambda e: e.memset(ident_f[:, :], 1.0), writes=["ident_f"])
    kb.op("pool", lambda e: e.affine_select(out=ident_f[:, :], in_=ident_f[:, :], pattern=[[-1, 128]],
                                             compare_op=ALU.is_equal, fill=0.0, base=0, channel_multiplier=1),
          reads=["ident_f"], writes=["ident_f"])
    kb.op("pool", lambda e: e.tensor_copy(out=ident_b[:, :], in_=ident_f[:, :]), reads=["ident_f"], writes=["ident_b"])
    kb.op("pool", lambda e: e.memset(ones_b[:, :], 1.0), writes=["ones_b"])
    kb.op("pool", lambda e: e.memset(onesLN[:, :], 1.0 / 512.0), writes=["onesLN"])
    kb.op("pool", lambda e: e.memset(blk_b[:, :], 0.0), writes=["blk_b"])
    kb.op("pool", lambda e: e.memset(blk_b[0:64, 0:64], 1.0), reads=["blk_b"], writes=["blk_b"])
    kb.op("pool", lambda e: e.memset(blk_b[64:128, 64:128], 1.0), reads=["blk_b"], writes=["blk_b"])
    kb.op("pool", lambda e: e.memset(mask_b[:, :, :], 1.0), writes=["mask_b"])
    for hh in range(2):
        kb.op("pool", lambda e: e.affine_select(out=mask_b[:, hh, 0:128], in_=mask_b[:, hh, 0:128], pattern=[[1, 128]],
                                                 compare_op=ALU.is_ge, fill=0.0, base=0, channel_multiplier=-1),
              reads=["mask_b"], writes=["mask_b"])
        kb.op("pool", lambda e: e.affine_select(out=mask_b[:, hh, 128:256], in_=mask_b[:, hh, 128:256], pattern=[[-1, 128]],
                                                 compare_op=ALU.is_ge, fill=0.0, base=0, channel_multiplier=1),
              reads=["mask_b"], writes=["mask_b"])
    kb.op("dve", lambda e: e.memset(dv[:, :], EPS), writes=["dv"])
    kb.op("dve", lambda e: e.tensor_scalar(out=QG8, in0=V("qg"), scalar1=0.125, scalar2=None, op0=ALU.mult),
          reads=["vecs", "dv"], writes=["dv"])
    lam4 = vecs[:, VI["lam"]:VI["lam"] + 4]
    kb.op("act", lambda e: e.activation(out=dv[:, 10:14], in_=lam4, func=AF.Exp, scale=-1.0), reads=["vecs", "dv"], writes=["dv"])
    kb.op("act", lambda e: e.activation(out=dv[:, 10:14], in_=dv[:, 10:14], func=AF.Ln, bias=1.0, scale=1.0), reads=["dv"], writes=["dv"])
    kb.op("dve", lambda e: e.tensor_scalar(out=dv[:, 2:6], in0=dv[:, 10:14], scalar1=-8.0, scalar2=None, op0=ALU.mult),
          reads=["dv"], writes=["dv"])
    kb.op("dve", lambda e: e.tensor_scalar(out=dv[:, 6:10], in0=dv[:, 10:14], scalar1=-16.0, scalar2=None, op0=ALU.mult),
          reads=["dv"], writes=["dv"])
    arena.reset()
    st0 = arena.f32(1024)
    kb.dma("st0", st0[:, 0:512], bdA_d, writes=["st0"])
    kb.op("pool", lambda e: e.tensor_copy(out=bdA[:, :, :], in_=st0[:, 0:512].rearrange("p (c m) -> p c m", c=4)), reads=["st0"], writes=["bdA"])
    kb.dma("st0b", st0[:, 512:1024], bdX_d, writes=["st0b"])
    kb.op("pool", lambda e: e.tensor_copy(out=bdX[:, :, :], in_=st0[:, 512:1024].rearrange("p (c m) -> p c m", c=4)), reads=["st0b"], writes=["bdX"])
    kb.barrier()

    deferred_casts = []

    def flush_casts(spread=False):
        engs = ["pool", "dve", "act"]
        n = 0
        while deferred_casts:
            deferred_casts.pop(0)(engs[n % 3] if spread else "pool")
            n += 1

    def cast_op(eng, dst, src, reads, writes):
        if eng == "act":
            kb.op("act", lambda e: e.activation(out=dst, in_=src, func=AF.Copy), reads=reads, writes=writes)
        else:
            kb.op(eng, lambda e: e.tensor_copy(out=dst, in_=src), reads=reads, writes=writes)

    def load_w_chunk(Wap, col0, dst, dkey, stg):
        if len(deferred_casts) >= len(stg.aps):
            flush_casts()
        sap, skey = stg.next()
        src = Wap.rearrange("(kc p) m -> p kc m", p=128)[:, :, col0:col0 + 128]
        kb.dma(skey, sap[:, 0:1024].rearrange("p (kc m) -> p kc m", kc=8), src, writes=[skey])
        deferred_casts.append(lambda eng="pool": cast_op(eng, dst, sap[:, 0:1024].rearrange("p (kc m) -> p kc m", kc=8), [skey], [dkey]))

    def load_wres_one(Wap, kc, stg):
        if len(deferred_casts) >= len(stg.aps):
            flush_casts()
        sap, skey = stg.next()
        kb.dma(skey, sap[:, 0:1024], Wap[kc * 128:(kc + 1) * 128, :], writes=[skey])
        deferred_casts.append(lambda eng="pool": cast_op(eng, wres[:, kc * 1024:(kc + 1) * 1024], sap[:, 0:1024], [skey], ["wres"]))

    def proj(bank_id, wt, wkey, t, extra_reads=()):
        b = banks[bank_id]
        fns = []
        for kc in range(8):
            fns.append(lambda e, kc=kc: e.matmul(b[:, :], lhsT=wt[:, kc, :], rhs=hT[:, kc, t * TT:(t + 1) * TT],
                                                 start=(kc == 0), stop=(kc == 7)))
        kb.mm(fns, reads=[wkey, "hT%d" % t] + list(extra_reads), writes=["B%d" % bank_id])

    def norm_tile(xt, xkey, t, gname, sq, rs, rinv, statrr):
        kb.op("act", lambda e: e.activation(out=sq[:, :, :], in_=xt[:, :, :], func=AF.Square), reads=[xkey], writes=["sq"])
        bid = statrr.next()
        b = banks[bid]
        kb.mm([lambda e, c=c: e.matmul(b[:, :], lhsT=ones_b[:, :], rhs=sq[:, c, :], start=(c == 0), stop=(c == 7)) for c in range(8)],
              reads=["sq"], writes=["B%d" % bid])
        kb.op("act", lambda e: e.activation(out=rinv, in_=b[:, :], func=AF.Ln, bias=EPSC, scale=1.0 / D), reads=["B%d" % bid], writes=["rinv"])
        kb.op("act", lambda e: e.activation(out=rinv, in_=rinv, func=AF.Exp, scale=-0.5), reads=["rinv"], writes=["rinv"])
        for c in range(8):
            kb.op("dve", lambda e: e.scalar_tensor_tensor(out=hT[:, c, t * TT:(t + 1) * TT], in0=xt[:, c, :], scalar=V(gname, c),
                                                          in1=rinv, op0=ALU.mult, op1=ALU.mult),
                  reads=[xkey, "rinv"], writes=["hT%d" % t])

    def phase_p0():
        arena.reset()
        xin = Slots("xin", [arena.f32(4 * D).rearrange("p (s c) -> p s c", s=4) for _ in range(2)])
        xts = Slots("xt", [arena.f32(8 * TT).rearrange("p (c s) -> p c s", c=8) for _ in range(2)])
        sq = arena.bf16(8 * TT).rearrange("p (c s) -> p c s", c=8)
        rs = arena.f32(TT)
        rinv = arena.f32(TT)
        trr = RR([0, 1, 2, 3])
        srr = RR([4, 5])
        for t in range(NT):
            xi, xik = xin.next()
            kb.dma(xik, xi, x_in[t * TT:(t + 1) * TT, :].rearrange("(s p) c -> p s c", p=128), writes=[xik])
            xt, xk = xts.next()
            for c in range(8):
                bid = trr.next()
                b = banks[bid]
                kb.mm([lambda e, s=s: e.transpose(b[:, s * 128:(s + 1) * 128], xi[:, s, c * 128:(c + 1) * 128], ident_f[:, :]) for s in range(4)],
                      reads=[xik], writes=["B%d" % bid])
                kb.op("act", lambda e: e.activation(out=xt[:, c, :], in_=b[:, :], func=AF.Copy), reads=["B%d" % bid], writes=[xk])
            kb.dma(xk, fm(xT_s, t), xt, reads=[xk], writes=["xTs%d" % t], q="pool")
            norm_tile(xt, xk, t, "mixg0", sq, rs, rinv, srr)
        flush_casts()
        kb.barrier()

    def phase_norm(gname):
        arena.reset()
        xts = Slots("xt", [arena.f32(8 * TT).rearrange("p (c s) -> p c s", c=8) for _ in range(2)])
        sq = arena.bf16(8 * TT).rearrange("p (c s) -> p c s", c=8)
        rs = arena.f32(TT)
        rinv = arena.f32(TT)
        srr = RR([4, 5])
        for t in range(NT):
            xt, xk = xts.next()
            kb.dma(xk, xt, fm(xT_s, t), reads=["xTs%d" % t], writes=[xk])
            norm_tile(xt, xk, t, gname, sq, rs, rinv, srr)
        flush_casts()
        kb.barrier()

    def phase_p2():
        arena.reset()
        stg = Slots("stg", [arena.f32(1024) for _ in range(4)])
        ws = Slots("ws", [arena.bf16(1024).rearrange("p (kc m) -> p kc m", kc=8) for _ in range(9)])
        wres_todo = list(range(8))
        pbuf = Slots("pb", [arena.f32(2 + TT) for _ in range(2)])
        ubuf = Slots("ub", [arena.f32(30 + TT) for _ in range(2)])
        tmpa = Slots("ta", [arena.f32(TT) for _ in range(3)])
        tmpb = Slots("tb", [arena.f32(TT) for _ in range(3)])
        tmpc = Slots("tc", [arena.f32(TT) for _ in range(2)])
        tmpd = Slots("td", [arena.f32(TT) for _ in range(4)])
        yo = Slots("yo", [arena.bf16(TT) for _ in range(3)])
        co = Slots("co", [arena.f32(TT) for _ in range(3)])
        prr = RR([0, 1, 2, 3])
        crr = RR([4, 5, 6, 7])
        dgs = Slots("dg", [arena.f32(N_PE_TAPS * 128).rearrange("p (n m) -> p n m", n=N_PE_TAPS) for _ in range(2)])
        Wi = W["ev_in"]

        def getw(col0):
            wt, wk = ws.next()
            load_w_chunk(Wi, col0, wt, wk, stg)
            return wt, wk

        prr_sc = RR([0, 1, 2, 3, 4, 5, 6, 7])
        nxt = [getw(512 + 0), getw(1024 + 0), getw(0)]
        flush_casts(spread=True)
        for i in range(4):
            (wc, wck), (wx, wxk), (wb, wbk) = nxt
            load_wres_one(W["ev_out"], wres_todo.pop(0), stg)
            if i < 3:
                nxt = [getw(512 + (i + 1) * 128), getw(1024 + (i + 1) * 128), getw((i + 1) * 128)]
            else:
                nxt = [getw(1536), getw(2048)]
            prev = None
            for t in range(NT):
                if t == min(3, NT - 1):
                    flush_casts()
                bc, bx, bb = prr_sc.next(), prr_sc.next(), prr_sc.next()
                proj(bc, wc, wck, t)
                proj(bx, wx, wxk, t)
                proj(bb, wb, wbk, t)
                csb, ck = tmpa.next()
                kb.op("act", lambda e: e.activation(out=csb, in_=banks[bc][:, :], func=AF.Copy), reads=["B%d" % bc], writes=[ck])
                pb, pk = pbuf.next()
                if prev is None:
                    kb.op("dve", lambda e: e.memset(pb[:, 0:2], 0.0), writes=[pk + "h"])
                else:
                    kb.op("dve", lambda e: e.tensor_copy(out=pb[:, 0:2], in_=prev[0][:, TT:TT + 2]), reads=[prev[1]], writes=[pk + "h"])
                kb.op("dve", lambda e: e.tensor_tensor(out=pb[:, 2:2 + TT], in0=banks[bx][:, :], in1=csb, op=ALU.mult),
                      reads=["B%d" % bx, ck], writes=[pk])
                acc, ak = tmpb.next()
                kb.op("dve", lambda e: e.tensor_scalar(out=acc, in0=pb[:, 2:2 + TT], scalar1=V("scw2", i), scalar2=None, op0=ALU.mult),
                      reads=[pk], writes=[ak])
                kb.op("dve", lambda e: e.scalar_tensor_tensor(out=acc, in0=pb[:, 1:1 + TT], scalar=V("scw1", i), in1=acc, op0=ALU.mult, op1=ALU.add),
                      reads=[pk, pk + "h", ak], writes=[ak])
                kb.op("dve", lambda e: e.scalar_tensor_tensor(out=acc, in0=pb[:, 0:TT], scalar=V("scw0", i), in1=acc, op0=ALU.mult, op1=ALU.add),
                      reads=[pk, pk + "h", ak], writes=[ak])
                y, yk = yo.next()
                kb.op("dve", lambda e: e.tensor_tensor(out=y, in0=banks[bb][:, :], in1=acc, op=ALU.mult), reads=["B%d" % bb, ak], writes=[yk])
                kb.dma(yk, u0_s[i * 128:(i + 1) * 128, t * TT:(t + 1) * TT], y, reads=[yk], writes=["u0_%d_%d" % (i, t)])
                prev = (pb, pk)
        cf_pend = [None]
        PE_TAPS = list(range(0, N_PE_TAPS))
        DVE_TAPS = list(range(N_PE_TAPS, 30 - POOL_TAPS))
        POOL_T = list(range(30 - POOL_TAPS, 30))

        def cf_b(acc, ak, acc2, a2k, pacc, pak, i, t):
            c_, ck_ = co.next()
            kb.op("dve", lambda e: e.scalar_tensor_tensor(out=c_, in0=acc2, scalar=V("cfb", i), in1=pacc, op0=ALU.add, op1=ALU.add),
                  reads=[a2k, pak], writes=[ck_])
            kb.op("dve", lambda e: e.tensor_tensor(out=c_, in0=acc, in1=c_, op=ALU.add), reads=[ak, ck_], writes=[ck_])
            kb.dma(ck_, cf_s[i * 128:(i + 1) * 128, t * TT:(t + 1) * TT], c_, reads=[ck_], writes=["cf_%d_%d" % (i, t)])

        for i in range(4):
            (wa, wak), (wg, wgk) = nxt
            load_wres_one(W["ev_out"], wres_todo.pop(0), stg)
            if i < 3:
                nxt = [getw(1536 + (i + 1) * 128), getw(2048 + (i + 1) * 128)]
            dg, dgk = dgs.next()
            for n, k in enumerate(PE_TAPS):
                kb.op("pool", lambda e: e.tensor_scalar(out=dg[:, n, :], in0=ident_f[:, :], scalar1=V("cfw%d" % k, i), scalar2=0.0,
                                                        op0=ALU.mult, op1=ALU.add), reads=["ident_f"], writes=[dgk])
            prev = None
            for t in range(NT):
                if t == min(3, NT - 1):
                    flush_casts()
                ba_, bg_ = prr.next(), prr.next()
                proj(ba_, wa, wak, t)
                proj(bg_, wg, wgk, t)
                sg, sk = tmpa.next()
                kb.op("act", lambda e: e.activation(out=sg, in_=banks[bg_][:, :], func=AF.Sigmoid), reads=["B%d" % bg_], writes=[sk])
                ub, uk = ubuf.next()
                if prev is None:
                    kb.op("dve", lambda e: e.memset(ub[:, 0:30], 0.0), writes=[uk + "h"])
                else:
                    kb.op("dve", lambda e: e.tensor_copy(out=ub[:, 0:30], in_=prev[0][:, TT:TT + 30]), reads=[prev[1]], writes=[uk + "h"])
                kb.op("dve", lambda e: e.tensor_tensor(out=ub[:, 30:30 + TT], in0=banks[ba_][:, :], in1=sg, op=ALU.mult),
                      reads=["B%d" % ba_, sk], writes=[uk])
                cb2 = crr.next()
                acc2, a2k = banks[cb2][:, :], "B%d" % cb2
                kb.mm([lambda e, n=n, k=k: e.matmul(acc2, lhsT=dg[:, n, :], rhs=ub[:, k:k + TT], start=(n == 0), stop=(n == len(PE_TAPS) - 1))
                       for n, k in enumerate(PE_TAPS)], reads=[uk, uk + "h", dgk], writes=[a2k])
                cb = crr.next()
                acc, ak = banks[cb][:, :], "B%d" % cb
                kb.op("dve", lambda e: e.tensor_scalar(out=acc, in0=ub[:, 30:30 + TT], scalar1=V("cfw30", i), scalar2=None,
                                                       op0=ALU.mult), reads=[uk], writes=[ak])
                for k in DVE_TAPS:
                    kb.op("dve", lambda e: e.scalar_tensor_tensor(out=acc, in0=ub[:, k:k + TT], scalar=V("cfw%d" % k, i), in1=acc,
                                                                  op0=ALU.mult, op1=ALU.add), reads=[uk, uk + "h", ak], writes=[ak])
                pacc, pak = tmpc.next()
                for n, k in enumerate(POOL_T):
                    if n == 0:
                        dst, dk = pacc, pak
                    else:
                        dst, dk = tmpd.next()
                    kb.op("act", lambda e: e.activation(out=dst, in_=ub[:, k:k + TT], func=AF.Identity, scale=V("cfw%d" % k, i)),
                          reads=[uk, uk + "h"], writes=[dk])
                    if n > 0:
                        kb.op("pool", lambda e: e.tensor_tensor(out=pacc, in0=pacc, in1=dst, op=ALU.add), reads=[pak, dk], writes=[pak])
                if cf_pend[0] is not None:
                    cf_b(*cf_pend[0])
                cf_pend[0] = (acc, ak, acc2, a2k, pacc, pak, i, t)
                prev = (ub, uk)
        cf_b(*cf_pend[0])
        flush_casts()
        kb.barrier()

    def phase_col(KC, load_fn, prep_fn, gname, final=False):
        xts = Slots("xt", [arena.f32(8 * TT).rearrange("p (c s) -> p c s", c=8) for _ in range(2)])
        if gname is not None:
            sq = arena.bf16(8 * TT).rearrange("p (c s) -> p c s", c=8)
            rs = None
            rinv = arena.f32(TT)
        if final:
            arenaH.reset()
            ots = Slots("ot", [arenaH.f32(4 * D).rearrange("p (s c) -> p s c", s=4) for _ in range(2)])
        orr = RR([0, 1, 2, 3])
        srr = RR([4, 5])
        trr = RR([6, 7])
        def loadx(t):
            xt, xk = xts.next()
            kb.dma(xk, xt, fm(xT_s, t), reads=["xTs%d" % t], writes=[xk])
            return (xt, xk)

        HU = {}
        HX = {}
        P = {}
        for t0 in range(min(2, NT)):
            HU[t0] = load_fn(t0)
            HX[t0] = loadx(t0)
        P[0] = prep_fn(HU[0])
        for t in range(NT):
            xt, xk = HX[t]
            if t + 1 < NT:
                P[t + 1] = prep_fn(HU[t + 1])
            ut, uk = P[t]
            for m in range(8):
                bid = orr.next()
                b = banks[bid]
                kb.mm([lambda e, kc=kc: e.matmul(b[:, :], lhsT=wres[:, kc * 1024 + m * 128: kc * 1024 + (m + 1) * 128], rhs=ut[:, kc, :],
                                                 start=(kc == 0), stop=(kc == KC - 1)) for kc in range(KC)],
                      reads=["wres"] + (uk if isinstance(uk, list) else [uk]), writes=["B%d" % bid])
                kb.op("dve", lambda e: e.tensor_tensor(out=xt[:, m, :], in0=b[:, :], in1=xt[:, m, :], op=ALU.add), reads=["B%d" % bid, xk], writes=[xk])
            if t + 2 < NT:
                HU[t + 2] = load_fn(t + 2)
            if not final:
                kb.dma(xk, fm(xT_s, t), xt, reads=[xk], writes=["xTs%d" % t], q="pool")
                if gname is not None:
                    norm_tile(xt, xk, t, gname, sq, rs, rinv, srr)
            else:
                ot, ok = ots.next()
                for s in range(4):
                    for half in range(2):
                        bid = trr.next()
                        b = banks[bid]
                        kb.mm([lambda e, q=q: e.transpose(b[:, q * 128:(q + 1) * 128], xt[:, half * 4 + q, s * 128:(s + 1) * 128], ident_f[:, :]) for q in range(4)],
                              reads=[xk], writes=["B%d" % bid])
                        kb.op("act", lambda e: e.activation(out=ot[:, s, half * 512:(half + 1) * 512], in_=b[:, :], func=AF.Copy),
                              reads=["B%d" % bid], writes=[ok])
                kb.dma(ok, out_d[t * TT:(t + 1) * TT, :].rearrange("(s p) c -> p s c", p=128), ot, reads=[ok], writes=["out%d" % t], q="pool")
            if t + 2 < NT:
                HX[t + 2] = loadx(t + 2)

    def phase_p3():
        arena.reset()
        uts = Slots("ut", [arena.bf16(8 * TT).rearrange("p (c s) -> p c s", c=8) for _ in range(2)])
        cfts = Slots("cft", [arena.f32(4 * TT).rearrange("p (c s) -> p c s", c=4) for _ in range(2)])
        sq4 = arena.f32(4 * TT).rearrange("p (c s) -> p c s", c=4)
        mean_sb = arena.f32(TT)
        m2 = arena.f32(TT)
        var = m2
        sd = m2
        rinv2 = m2
        tmps = Slots("lt", [arena.f32(TT) for _ in range(2)])
        lrr = RR([6, 7])

        def load_fn(t):
            ut, uk = uts.next()
            kb.dma(uk, ut[:, 0:4, :], fm(u0_s, t), reads=["u0_%d_%d" % (i, t) for i in range(4)], writes=[uk + "lo"])
            cft, ck = cfts.next()
            kb.dma(ck, cft, fm(cf_s, t), reads=["cf_%d_%d" % (i, t) for i in range(4)], writes=[ck])
            return (ut, uk, cft, ck)

        def prep_fn(h):
            ut, uk, cft, ck = h
            bm = lrr.next()
            kb.mm([lambda e, i=i: e.matmul(banks[bm][:, :], lhsT=onesLN[:, :], rhs=cft[:, i, :], start=(i == 0), stop=(i == 3)) for i in range(4)],
                  reads=[ck], writes=["B%d" % bm])
            kb.op("act", lambda e: e.activation(out=sq4[:, :, :], in_=cft[:, :, :], func=AF.Square), reads=[ck], writes=["sq4"])
            be = lrr.next()
            kb.mm([lambda e, i=i: e.matmul(banks[be][:, :], lhsT=onesLN[:, :], rhs=sq4[:, i, :], start=(i == 0), stop=(i == 3)) for i in range(4)],
                  reads=["sq4"], writes=["B%d" % be])
            kb.op("act", lambda e: e.activation(out=mean_sb, in_=banks[bm][:, :], func=AF.Copy), reads=["B%d" % bm], writes=["mean"])
            kb.op("dve", lambda e: e.tensor_tensor(out=m2, in0=mean_sb, in1=mean_sb, op=ALU.mult), reads=["mean"], writes=["m2"])
            kb.op("dve", lambda e: e.tensor_tensor(out=var, in0=banks[be][:, :], in1=m2, op=ALU.subtract), reads=["B%d" % be, "m2"], writes=["m2"])
            kb.op("dve", lambda e: e.tensor_scalar(out=var, in0=var, scalar1=0.0, scalar2=None, op0=ALU.max), reads=["m2"], writes=["m2"])
            kb.op("act", lambda e: e.activation(out=sd, in_=var, func=AF.Ln, bias=EPSC, scale=1.0), reads=["m2"], writes=["m2"])
            kb.op("act", lambda e: e.activation(out=rinv2, in_=sd, func=AF.Exp, scale=-0.5), reads=["m2"], writes=["m2"])
            for i in range(4):
                tp, tk = tmps.next()
                kb.op("dve", lambda e: e.tensor_tensor(out=tp, in0=cft[:, i, :], in1=mean_sb, op=ALU.subtract), reads=[ck, "mean"], writes=[tk])
                kb.op("dve", lambda e: e.tensor_tensor(out=tp, in0=tp, in1=rinv2, op=ALU.mult), reads=[tk, "m2"], writes=[tk])
                kb.op("act", lambda e: e.activation(out=ut[:, 4 + i, :], in_=tp, func=AF.Silu, bias=V("lnb", i), scale=V("lng", i)),
                      reads=[tk], writes=[uk])
            return ut, [uk, uk + "lo"]

        phase_col(8, load_fn, prep_fn, "ffng0")
        flush_casts()
        kb.barrier()

    def phase_f1(l):
        arena.reset()
        stg = Slots("stg", [arena.f32(1024) for _ in range(3)])
        ws = Slots("ws", [arena.bf16(1024).rearrange("p (kc m) -> p kc m", kc=8) for _ in range(6)])
        gbuf = Slots("gb", [arena.f32(2 + TT) for _ in range(3)])
        tmpb = Slots("tb", [arena.f32(TT) for _ in range(3)])
        tmps = Slots("tsl", [arena.f32(TT) for _ in range(3)])
        ao = Slots("ao", [arena.bf16(TT) for _ in range(4)])
        prr = RR([0, 1, 2, 3, 4, 5, 6, 7])
        Wg, Wu = W["g%d" % l], W["u%d" % l]

        def getw(Wap, col0):
            wt, wk = ws.next()
            load_w_chunk(Wap, col0, wt, wk, stg)
            return wt, wk

        ups = Slots("up", [arena.bf16(TT) for _ in range(3)])

        def stage_b(acc, ak, up, upk, j, t):
            sl, sk = tmps.next()
            kb.op("act", lambda e: e.activation(out=sl, in_=acc, func=AF.Silu), reads=[ak], writes=[sk])
            a_, ak_ = ao.next()
            kb.op("pool", lambda e: e.tensor_tensor(out=a_, in0=up, in1=sl, op=ALU.mult), reads=[upk, sk], writes=[ak_])
            kb.dma(ak_, act_s[j * 128:(j + 1) * 128, t * TT:(t + 1) * TT], a_, reads=[ak_], writes=["act_%d_%d" % (j, t)])

        pend = None
        nxt = [getw(Wg, 0), getw(Wu, 0)]
        flush_casts(spread=True)
        for j in range(NFF):
            (wg, wgk), (wu, wuk) = nxt
            if j + 1 < NFF:
                nxt = [getw(Wg, (j + 1) * 128), getw(Wu, (j + 1) * 128)]
            load_wres_one(W["d%d" % l], j, stg)
            prev = None
            for t in range(NT):
                if t == min(3, NT - 1):
                    flush_casts()
                bg_, bu_ = prr.next(), prr.next()
                proj(bg_, wg, wgk, t)
                proj(bu_, wu, wuk, t)
                gb, gk = gbuf.next()
                if prev is None:
                    kb.op("dve", lambda e: e.memset(gb[:, 0:2], 0.0), writes=[gk + "h"])
                else:
                    kb.op("dve", lambda e: e.tensor_copy(out=gb[:, 0:2], in_=prev[0][:, TT:TT + 2]), reads=[prev[1]], writes=[gk + "h"])
                kb.op("act", lambda e: e.activation(out=gb[:, 2:2 + TT], in_=banks[bg_][:, :], func=AF.Copy), reads=["B%d" % bg_], writes=[gk])
                acc, ak = tmpb.next()
                kb.op("act", lambda e: e.activation(out=acc, in_=banks[bg_][:, :], func=AF.Identity, bias=V("fcb%d" % l, j), scale=V("fcw%d_2" % l, j)),
                      reads=["B%d" % bg_], writes=[ak])
                up, upk = ups.next()
                kb.op("act", lambda e: e.activation(out=up, in_=banks[bu_][:, :], func=AF.Copy), reads=["B%d" % bu_], writes=[upk])
                kb.op("dve", lambda e: e.scalar_tensor_tensor(out=acc, in0=gb[:, 1:1 + TT], scalar=V("fcw%d_1" % l, j), in1=acc, op0=ALU.mult, op1=ALU.add),
                      reads=[gk, gk + "h", ak], writes=[ak])
                kb.op("dve", lambda e: e.scalar_tensor_tensor(out=acc, in0=gb[:, 0:TT], scalar=V("fcw%d_0" % l, j), in1=acc, op0=ALU.mult, op1=ALU.add),
                      reads=[gk, gk + "h", ak], writes=[ak])
                if pend is not None:
                    stage_b(*pend)
                pend = (acc, ak, up, upk, j, t)
                prev = (gb, gk)
        stage_b(*pend)
        flush_casts()
        kb.barrier()

    def phase_f2(final):
        arena.reset()
        ats = Slots("at", [arena.bf16(NFF * TT).rearrange("p (c s) -> p c s", c=NFF) for _ in range(2)])

        def load_u(t):
            at, ak = ats.next()
            kb.dma(ak, at, fm(act_s, t), reads=["act_%d_%d" % (j, t) for j in range(NFF)], writes=[ak])
            return at, ak

        phase_col(NFF, load_u, lambda h: h, None if final else "mixg1", final=final)
        flush_casts()
        kb.barrier()

    def phase_q2():
        arena.reset()
        stg = Slots("stg", [arena.f32(1024) for _ in range(3)])
        ws = Slots("ws", [arena.bf16(1024).rearrange("p (kc m) -> p kc m", kc=8) for _ in range(5)])
        wres_todo = list(range(6))
        lru_mark = arena.off
        xbuf = Slots("xb", [arena.f32(3 + TT) for _ in range(3)])
        nsl = {"xc": 4, "a": 4, "bt": 4, "mu": 4, "r": 2, "ig": 2, "a2": 2}
        T = {n: Slots(n, [arena.f32(TT) for _ in range(k)]) for n, k in nsl.items()}
        T["gs"] = Slots("gs", [arena.bf16(TT) for _ in range(6)])
        hb = Slots("hb", [arena.f32(TT) for _ in range(3)])
        xcb = Slots("xcb", [arena.bf16(TT) for _ in range(4)])
        yo = Slots("yo", [arena.bf16(TT) for _ in range(3)])
        prr = RR([0, 1, 2, 3])
        srr = RR([4, 5, 6, 7])
        Wi = W["od_in"]

        def getw(col0):
            wt, wk = ws.next()
            load_w_chunk(Wi, col0, wt, wk, stg)
            return wt, wk

        def lru_a(cx):
            i, t = cx["i"], cx["t"]
            bx_, bg_ = prr.next(), prr.next()
            cx["bg"] = bg_
            proj(bx_, cx["wx"], cx["wxk"], t)
            proj(bg_, cx["wg"], cx["wgk"], t)
            xb, xk = xbuf.next()
            prev = cx["prev"]
            if prev is None:
                kb.op("dve", lambda e: e.memset(xb[:, 0:3], 0.0), writes=[xk + "h"])
            else:
                kb.op("dve", lambda e: e.tensor_copy(out=xb[:, 0:3], in_=prev[0][:, TT:TT + 3]), reads=[prev[1]], writes=[xk + "h"])
            kb.op("act", lambda e: e.activation(out=xb[:, 3:3 + TT], in_=banks[bx_][:, :], func=AF.Copy), reads=["B%d" % bx_], writes=[xk])
            cx["xb"] = (xb, xk)
            gs, gsk = T["gs"].next()
            kb.op("act", lambda e: e.activation(out=gs, in_=banks[bg_][:, :], func=AF.Gelu_apprx_tanh), reads=["B%d" % bg_], writes=[gsk])
            cx["gs"] = (gs, gsk)
            xc, xck = T["xc"].next()
            kb.op("act", lambda e: e.activation(out=xc, in_=banks[bx_][:, :], func=AF.Identity, bias=V("lrub", i), scale=V("lruw3", i)),
                  reads=["B%d" % bx_], writes=[xck])
            for k in (2, 1, 0):
                kb.op("dve", lambda e: e.scalar_tensor_tensor(out=xc, in0=xb[:, k:k + TT], scalar=V("lruw%d" % k, i), in1=xc,
                                                              op0=ALU.mult, op1=ALU.add), reads=[xk, xk + "h", xck], writes=[xck])
            cx["xc"] = (xc, xck)
            xcbf, xcbk = xcb.next()
            kb.op("dve", lambda e: e.tensor_copy(out=xcbf, in_=xc), reads=[xck], writes=[xcbk])
            cx["xcbf"] = (xcbf, xcbk)

        def lru_b(cxs):
            st = []
            for cx in cxs:
                i = cx["i"]
                xcbf, xcbk = cx["xcbf"]
                br_, bi_ = srr.next(), srr.next()
                kb.mm([lambda e, i=i, br_=br_, xcbf=xcbf: e.matmul(banks[br_][:, :], lhsT=bdA[:, i, :], rhs=xcbf, start=True, stop=True)],
                      reads=["bdA", xcbk], writes=["B%d" % br_])
                kb.mm([lambda e, i=i, bi_=bi_, xcbf=xcbf: e.matmul(banks[bi_][:, :], lhsT=bdX[:, i, :], rhs=xcbf, start=True, stop=True)],
                      reads=["bdX", xcbk], writes=["B%d" % bi_])
                st.append(dict(cx=cx, i=i, br=br_, bi=bi_))
            for s_ in st:
                i = s_["i"]
                r, rk = T["r"].next()
                ig, igk = T["ig"].next()
                s_["r"], s_["ig"] = (r, rk), (ig, igk)
                kb.op("act", lambda e: e.activation(out=r, in_=banks[s_["br"]][:, :], func=AF.Sigmoid, bias=V("ba", i)), reads=["B%d" % s_["br"]], writes=[rk])
                kb.op("act", lambda e: e.activation(out=ig, in_=banks[s_["bi"]][:, :], func=AF.Sigmoid, bias=V("bx", i)), reads=["B%d" % s_["bi"]], writes=[igk])
            for s_ in st:
                i = s_["i"]
                r, rk = s_["r"]
                a, ak = T["a"].next()
                a2, a2k = T["a2"].next()
                s_["a2"] = (a2, a2k)
                kb.op("act", lambda e: e.activation(out=a, in_=r, func=AF.Exp, scale=SP8(i)), reads=[rk], writes=[ak])
                kb.op("act", lambda e: e.activation(out=a2, in_=r, func=AF.Exp, scale=SP16(i)), reads=[rk], writes=[a2k])
                s_["cx"]["a"] = (a, ak)
            for s_ in st:
                a2, a2k = s_["a2"]
                ig, igk = s_["ig"]
                xc, xck = s_["cx"]["xc"]
                kb.op("dve", lambda e: e.tensor_scalar(out=a2, in0=a2, scalar1=1.0, scalar2=0.0, op0=ALU.subtract, op1=ALU.min), reads=[a2k], writes=[a2k])
                bt, btk = T["bt"].next()
                kb.op("dve", lambda e: e.tensor_tensor(out=bt, in0=ig, in1=xc, op=ALU.mult), reads=[igk, xck], writes=[btk])
                s_["cx"]["bt"] = (bt, btk)
            for s_ in st:
                a2, a2k = s_["a2"]
                mu, muk = T["mu"].next()
                kb.op("act", lambda e: e.activation(out=mu, in_=a2, func=AF.Sqrt, scale=-1.0), reads=[a2k], writes=[muk])
                s_["cx"]["mu"] = (mu, muk)

        def lru_c(cx):
            i, t = cx["i"], cx["t"]
            a, ak = cx["a"]
            bt, btk = cx["bt"]
            mu, muk = cx["mu"]
            gs, gsk = cx["gs"]
            kb.op("dve", lambda e: e.tensor_tensor(out=bt, in0=bt, in1=mu, op=ALU.mult), reads=[btk, muk], writes=[btk])
            h, hk = hb.next()
            hprev = lru_state["hprev"] if t > 0 else None
            if hprev is None:
                kb.op("dve", lambda e: e.tensor_tensor_scan(out=h, data0=a, data1=bt, initial=0.0, op0=ALU.mult, op1=ALU.add),
                      reads=[ak, btk], writes=[hk])
            else:
                kb.op("dve", lambda e: e.tensor_tensor_scan(out=h, data0=a, data1=bt, initial=hprev[0][:, TT - 1:TT], op0=ALU.mult, op1=ALU.add),
                      reads=[ak, btk, hprev[1]], writes=[hk])
            y, yk = yo.next()
            kb.op("pool", lambda e: e.tensor_tensor(out=y, in0=h, in1=gs, op=ALU.mult), reads=[hk, gsk], writes=[yk])
            kb.dma(yk, u1_s[i * 128:(i + 1) * 128, t * TT:(t + 1) * TT], y, reads=[yk], writes=["u1_%d_%d" % (i, t)])
            lru_state["hprev"] = (h, hk)

        lru_state = {"hprev": None}
        pipe = []
        nxt = [getw(0), getw(512)]
        flush_casts(spread=True)
        nB = 0
        nC = 0
        for i in range(4):
            (wx, wxk), (wg, wgk) = nxt
            load_wres_one(W["od_out"], wres_todo.pop(0), stg)
            nxt = [getw((i + 1) * 128), getw(512 + (i + 1) * 128)] if i < 3 else [getw(1024), getw(1792)]
            prev = None
            for t in range(NT):
                if t == min(3, NT - 1):
                    flush_casts()
                cx = dict(i=i, t=t, wx=wx, wxk=wxk, wg=wg, wgk=wgk, prev=prev)
                lru_a(cx)
                prev = cx["xb"]
                pipe.append(cx)
                if len(pipe) % 2 == 0:
                    if len(pipe) - nB >= 4:
                        lru_b(pipe[nB:nB + 2])
                        nB += 2
                    while nB - nC > 2:
                        lru_c(pipe[nC])
                        nC += 1
        while nB < len(pipe):
            lru_b(pipe[nB:nB + 2])
            nB += 2
        while nC < len(pipe):
            lru_c(pipe[nC])
            nC += 1
        flush_casts()
        kb.barrier()
        arena.off = lru_mark
        sqb = Slots("sqb", [arena.bf16(TT) for _ in range(2)])
        rqs = Slots("rq", [arena.f32(TT) for _ in range(2)])
        rows = Slots("row", [arena.bf16(S) for _ in range(3)])
        qk_pend = [None]

        def qk_b(bp, bs, row, rk, t, d, gcol):
            rq, rqk = rqs.next()
            kb.op("act", lambda e: e.activation(out=rq, in_=banks[bs][:, :], func=AF.Ln, bias=EPSC, scale=1.0 / 64.0), reads=["B%d" % bs], writes=[rqk])
            kb.op("act", lambda e: e.activation(out=rq, in_=rq, func=AF.Exp, scale=-0.5), reads=[rqk], writes=[rqk])
            if d == 1:
                yv, pin, rin = row[:, t * TT:(t + 1) * TT], banks[bp][:, :], rq
            else:
                yv = row.rearrange("p (r m) -> p m r", r=d)[:, t * (TT // d):(t + 1) * (TT // d), :]
                pin = banks[bp][:, :].rearrange("p (m r) -> p m r", r=d)
                rin = rq.rearrange("p (m r) -> p m r", r=d)
            kb.op("dve", lambda e: e.scalar_tensor_tensor(out=yv, in0=pin, scalar=gcol, in1=rin, op0=ALU.mult, op1=ALU.add if False else ALU.mult),
                  reads=["B%d" % bp, rqk], writes=[rk + "_%d" % t])

        for c in range(6):
            d = DILS[c // 2]
            (wq, wqk), (wk_, wkk) = nxt
            if wres_todo:
                load_wres_one(W["od_out"], wres_todo.pop(0), stg)
            nxt = [getw(1024 + (c + 1) * 128), getw(1792 + (c + 1) * 128)] if c < 5 else [getw(2560)]
            for (wt, wkey, gcol, dst, nm) in ((wq, wqk, QG8, qn_s, "qn"), (wk_, wkk, V("kg"), kn_s, "kn")):
                row, rk = rows.next()
                for t in range(NT):
                    if t == min(3, NT - 1):
                        flush_casts()
                    bp = prr.next()
                    proj(bp, wt, wkey, t)
                    sq, sqk = sqb.next()
                    kb.op("act", lambda e: e.activation(out=sq, in_=banks[bp][:, :], func=AF.Square), reads=["B%d" % bp], writes=[sqk])
                    bs = srr.next()
                    kb.mm([lambda e: e.matmul(banks[bs][:, :], lhsT=blk_b[:, :], rhs=sq, start=True, stop=True)], reads=[sqk], writes=["B%d" % bs])
                    if qk_pend[0] is not None:
                        qk_b(*qk_pend[0])
                    qk_pend[0] = (bp, bs, row, rk, t, d, gcol)
                if qk_pend[0] is not None:
                    qk_b(*qk_pend[0])
                    qk_pend[0] = None
                kb.dma(rk, dst[c * 128:(c + 1) * 128, :], row, reads=[rk + "_%d" % t for t in range(NT)], writes=["%s_%d" % (nm, c)])
        for c in range(6):
            d = DILS[c // 2]
            (wv, wvk), = nxt
            if c < 5:
                nxt = [getw(2560 + (c + 1) * 128)]
            row, rk = rows.next()
            for t in range(NT):
                if t == min(3, NT - 1):
                    flush_casts()
                bp = prr.next()
                proj(bp, wv, wvk, t)
                if d == 1:
                    yv, pin = row[:, t * TT:(t + 1) * TT], banks[bp][:, :]
                else:
                    yv = row.rearrange("p (r m) -> p m r", r=d)[:, t * (TT // d):(t + 1) * (TT // d), :]
                    pin = banks[bp][:, :].rearrange("p (m r) -> p m r", r=d)
                kb.op("act", lambda e: e.activation(out=yv, in_=pin, func=AF.Copy),
                      reads=["B%d" % bp], writes=[rk + "_%d" % t])
            kb.dma(rk, v_s[c * 128:(c + 1) * 128, :], row, reads=[rk + "_%d" % t for t in range(NT)], writes=["v_%d" % c])
        flush_casts()
        kb.barrier()

    def phase_qa():
        arena.reset()
        arenaH.reset()
        qrow = arenaH.bf16(2 * S).rearrange("p (c s) -> p c s", c=2)
        krz = arenaH.bf16(4 * S).rearrange("p (h c s) -> p h c s", h=2, c=2)
        Vt = arenaH.bf16(NBLK * 256).rearrange("p (b c m) -> p b c m", b=NBLK, c=2)
        vrow = arena.bf16(2 * S).rearrange("p (c s) -> p c s", c=2)
        accN = arena.f32(2 * S).rearrange("p (c s) -> p c s", c=2)
        accD = arena.f32(2 * S).rearrange("p (c s) -> p c s", c=2)
        pts = Slots("pt", [arena.bf16(2 * 2 * 256).rearrange("p (c h n) -> p c h n", c=2, h=2) for _ in range(3)])
        yrow = qrow
        for hh in range(2):
            oh = 1 - hh
            kb.op("pool", lambda e: e.memset(krz[oh * 64:(oh + 1) * 64, hh, :, :], 0.0), writes=["krzz%d" % hh])
        srr = RR([0, 1, 2, 3])
        trr = RR([6, 7])
        qa_pend = [None]

        def qa_b(g, d, NB, r, kbk, gblk, pt, ptk):
            def pv(bank_id, col0, first):
                ba = banks[bank_id][:, :].rearrange("p (i n) -> p i n", i=4)
                fns = []
                for hh in range(2):
                    for c in range(2):
                        fns.append(lambda e, hh=hh, c=c: e.matmul(ba[hh * 64:(hh + 1) * 64, c, :], lhsT=Vt[:, gblk, c, hh * 64:(hh + 1) * 64],
                                                                  rhs=pt[:, c, hh, col0:col0 + 128], start=(first and c == 0), stop=False,
                                                                  skip_group_check=True))
                        fns.append(lambda e, hh=hh, c=c: e.matmul(ba[hh * 64:(hh + 1) * 64, 2 + c, :], lhsT=ones_b[:, 0:64],
                                                                  rhs=pt[:, c, hh, col0:col0 + 128], start=False, stop=(not first and c == 1),
                                                                  skip_group_check=True))
                kb.mm(fns, reads=["Vt", ptk + "c0", ptk + "c1"], writes=["B%d" % bank_id])
            bcur = 4 + (kbk % 2)
            pv(bcur, 0, first=(kbk == 0))
            ba = banks[bcur][:, :].rearrange("p (i n) -> p i n", i=4)
            if d == 1:
                dN = accN[:, :, kbk * 128:(kbk + 1) * 128]
                dD = accD[:, :, kbk * 128:(kbk + 1) * 128]
            else:
                dN = accN.rearrange("p c (m r) -> p c r m", r=d)[:, :, r, kbk * 128:(kbk + 1) * 128]
                dD = accD.rearrange("p c (m r) -> p c r m", r=d)[:, :, r, kbk * 128:(kbk + 1) * 128]
            for c in range(2):
                if g == 0:
                    kb.op("dve", lambda e: e.tensor_copy(out=dN[:, c, :], in_=ba[:, c, :]), reads=["B%d" % bcur], writes=["accN"])
                    kb.op("dve", lambda e: e.tensor_copy(out=dD[:, c, :], in_=ba[:, 2 + c, :]), reads=["B%d" % bcur], writes=["accD"])
                else:
                    kb.op("dve", lambda e: e.tensor_tensor(out=dN[:, c, :], in0=ba[:, c, :], in1=dN[:, c, :], op=ALU.add), reads=["B%d" % bcur, "accN"], writes=["accN"])
                    kb.op("dve", lambda e: e.tensor_tensor(out=dD[:, c, :], in0=ba[:, 2 + c, :], in1=dD[:, c, :], op=ALU.add), reads=["B%d" % bcur, "accD"], writes=["accD"])
            if kbk + 1 < NB:
                pv(4 + ((kbk + 1) % 2), 128, first=True)

        for g in range(3):
            d = DILS[g]
            L = S // d
            NB = L // 128
            kb.dma("qrow", qrow, fm(qn_s[g * 256:(g + 1) * 256, :]), reads=["qn_all"], writes=["qrow"])
            kb.dma("vrow", vrow, fm(v_s[g * 256:(g + 1) * 256, :]), reads=["v_all"], writes=["vrow"])
            for hh in range(2):
                kb.dma("krz%d" % hh, krz[hh * 64:(hh + 1) * 64, hh, :, :],
                       fm(kn_s[g * 256:(g + 1) * 256, :])[hh * 64:(hh + 1) * 64, :, :], reads=["kn_all"], writes=["krz"])
            for c in range(2):
                for b4 in range(NBLK // 4):
                    bid = trr.next()
                    bb = banks[bid][:, :].bitcast(BF16)
                    kb.mm([lambda e, q=q: e.transpose(bb[:, q * 128:(q + 1) * 128], vrow[:, c, (b4 * 4 + q) * 128:(b4 * 4 + q + 1) * 128], ident_b[:, :]) for q in range(4)],
                          reads=["vrow"], writes=["B%d" % bid])
                    kb.op("act", lambda e: e.activation(out=Vt[:, b4 * 4:(b4 + 1) * 4, c, :], in_=bb[:, 0:512].rearrange("p (q m) -> p q m", q=4), func=AF.Copy),
                          reads=["B%d" % bid], writes=["Vt"])
            for r in range(d):
                for kbk in range(NB):
                    base = r * L + kbk * 128
                    gblk = r * NB + kbk
                    N = 256 if kbk < NB - 1 else 128
                    pt, ptk = pts.next()
                    for c in range(2):
                        bid = srr.next()
                        bs = banks[bid][:, :].rearrange("p (h n) -> p h n", h=2)
                        kb.mm([lambda e, hh=hh: e.matmul(bs[:, hh, 0:N], lhsT=krz[:, hh, c, base:base + 128], rhs=qrow[:, c, base:base + N], start=True, stop=True)
                               for hh in range(2)], reads=["krz", "krzz0", "krzz1", "qrow"], writes=["B%d" % bid])
                        kb.op("act", lambda e: e.activation(out=pt[:, c, :, 0:N], in_=bs[:, :, 0:N], func=AF.Exp), reads=["B%d" % bid], writes=[ptk + "c%d" % c])
                        kb.op("dve", lambda e: e.tensor_tensor(out=pt[:, c, :, 0:N], in0=pt[:, c, :, 0:N], in1=mask_b[:, :, 0:N], op=ALU.mult),
                              reads=[ptk + "c%d" % c, "mask_b"], writes=[ptk + "c%d" % c])
                    if qa_pend[0] is not None:
                        qa_b(*qa_pend[0])
                    qa_pend[0] = (g, d, NB, r, kbk, gblk, pt, ptk)
            if qa_pend[0] is not None:
                qa_b(*qa_pend[0])
                qa_pend[0] = None
        HS = S // 2
        for c in range(2):
            for hf in range(2):
                sl = slice(hf * HS, (hf + 1) * HS)
                kd = "accDp%d%d" % (c, hf)
                kb.op("act", lambda e: e.activation(out=accD[:, c, sl], in_=accD[:, c, sl], func=AF.Ln), reads=["accD"], writes=[kd])
                kb.op("act", lambda e: e.activation(out=accD[:, c, sl], in_=accD[:, c, sl], func=AF.Exp, scale=-1.0), reads=[kd], writes=[kd])
                kb.op("dve", lambda e: e.tensor_tensor(out=yrow[:, c, sl], in0=accN[:, c, sl], in1=accD[:, c, sl], op=ALU.mult),
                      reads=["accN", kd, "qrow"], writes=["yr%d%d" % (c, hf)])
            kb.dma("yrow%d" % c, u1_s[512 + c * 128:512 + (c + 1) * 128, :], yrow[:, c, :], reads=["yr%d0" % c, "yr%d1" % c], writes=["u1att%d" % c])
        flush_casts()
        kb.barrier()

    def phase_q3():
        arena.reset()
        uts = Slots("ut", [arena.bf16(6 * TT).rearrange("p (c s) -> p c s", c=6) for _ in range(2)])

        def load_u(t):
            ut, uk = uts.next()
            kb.dma(uk, ut, fm(u1_s, t), reads=["u1all"], writes=[uk])
            return ut, uk

        phase_col(6, load_u, lambda h: h, "ffng1")
        flush_casts()
        kb.barrier()

    plist = [("p0", phase_p0), ("p2", phase_p2), ("p3", phase_p3), ("f1_0", lambda: phase_f1(0)),
             ("f2_0", lambda: phase_f2(final=(n_layers == 1)))]
    if n_layers == 2:
        plist += [("q2", phase_q2), ("qa", phase_qa), ("q3", phase_q3),
                  ("f1_1", lambda: phase_f1(1)), ("f2_1", lambda: phase_f2(final=True))]
    for name, fn in plist:
        with nc.named_scope(name):
            fn()
        if stop_after == name:
            break
    kb.barrier()
    return nc


def host_inputs(inp, b):
    vecs, _ = pack_vecs(inp)
    m = {
        "x": np.ascontiguousarray(np.asarray(inp["x"][b], np.float32)),
        "vecs": vecs,
        "bdA": blockdiag(inp["od_lru_wa"][0]),
        "bdX": blockdiag(inp["od_lru_wx"][0]),
        "ev_w_in": np.ascontiguousarray(inp["ev_w_in"][0], dtype=np.float32),
        "ev_w_out": np.ascontiguousarray(inp["ev_w_out"][0], dtype=np.float32),
        "od_w_in": np.ascontiguousarray(inp["od_w_in"][0], dtype=np.float32),
        "od_w_out": np.ascontiguousarray(inp["od_w_out"][0], dtype=np.float32),
    }
    for l in range(2):
        m["ffn_w_gate%d" % l] = np.ascontiguousarray(inp["ffn_w_gate"][l], dtype=np.float32)
        m["ffn_w_up%d" % l] = np.ascontiguousarray(inp["ffn_w_up"][l], dtype=np.float32)
        m["ffn_w_down%d" % l] = np.ascontiguousarray(inp["ffn_w_down"][l], dtype=np.float32)
    return m


def kernel(**inputs):
    inp = {k: np.asarray(v) for k, v in inputs.items()}
    B, S, _ = inp["x"].shape
    nc = build(S)
    in_maps = [host_inputs(inp, b) for b in range(B)]
    res = run_bass_kernel_spmd(nc, in_maps, core_ids=list(range(B)))
    out = np.stack([np.asarray(r["out"], np.float32) for r in res.results], axis=0)
    return out
```

```python
import os
import numpy as np
import concourse.bass as bass
import concourse.mybir as mybir
from concourse.bass_utils import run_bass_kernel_spmd

F32 = mybir.dt.float32
BF16 = mybir.dt.bfloat16
AF = mybir.ActivationFunctionType
ALU = mybir.AluOpType

D = 1024
DFF = 2816
NFF = DFF // 128
EPS = 1e-6
TT = 512
DILS = (1, 4, 16)
N_PE_TAPS = 8
POOL_TAPS = 9


class KB:
    def __init__(self, nc):
        self.nc = nc
        self.eng = {}
        for name, h in (("pe", nc.tensor), ("act", nc.scalar), ("dve", nc.vector), ("pool", nc.gpsimd), ("sp", nc.sync)):
            sem = nc.alloc_semaphore("sem_" + name)
            self.eng[name] = dict(h=h, sem=sem, cnt=0, seen={})
        self.lastw = {}
        self.readers = {}
        self.dma_sems = {}

    def _deps(self, reads, writes):
        deps = []
        for r in reads:
            if r in self.lastw:
                deps.append(self.lastw[r])
        for w in writes:
            if w in self.lastw:
                deps.append(self.lastw[w])
            deps.extend(self.readers.get(w, ()))
        return deps

    def _wait(self, me, deps):
        e = self.eng[me]
        best = {}
        for (sem, val, key) in deps:
            if best.get(key, (None, 0))[1] < val:
                best[key] = (sem, val)
        for key, (sem, val) in best.items():
            if e["seen"].get(key, 0) < val:
                e["h"].wait_ge(sem, val)
                e["seen"][key] = val

    def _commit(self, ticket, reads, writes):
        for r in reads:
            self.readers.setdefault(r, []).append(ticket)
        for w in writes:
            self.lastw[w] = ticket
            self.readers[w] = []

    def op(self, me, fn, reads=(), writes=()):
        e = self.eng[me]
        self._wait(me, self._deps(reads, writes))
        inst = fn(e["h"])
        e["cnt"] += 1
        inst.then_inc(e["sem"], 1)
        t = (e["sem"], e["cnt"], me)
        self._commit(t, reads, writes)
        return t

    def mm(self, fns, reads=(), writes=()):
        e = self.eng["pe"]
        self._wait("pe", self._deps(reads, writes))
        inst = None
        for fn in fns:
            inst = fn(e["h"])
        e["cnt"] += 1
        inst.then_inc(e["sem"], 1)
        t = (e["sem"], e["cnt"], "pe")
        self._commit(t, reads, writes)
        return t

    def dma(self, semkey, out, in_, reads=(), writes=(), q="sp"):
        e = self.eng[q]
        if q == "pool":
            semkey = semkey + "_sw"
        if semkey not in self.dma_sems:
            self.dma_sems[semkey] = [self.nc.alloc_semaphore("dsem_" + semkey), 0]
        s = self.dma_sems[semkey]
        self._wait(q, self._deps(reads, writes))
        inst = e["h"].dma_start(out=out, in_=in_)
        s[1] += 16
        inst.then_inc(s[0], 16)
        t = (s[0], s[1], "dma_" + semkey)
        self._commit(t, reads, writes)
        return t

    def barrier(self):
        for me, e in self.eng.items():
            for other, o in self.eng.items():
                if other != me and o["cnt"] > e["seen"].get(other, 0):
                    e["h"].wait_ge(o["sem"], o["cnt"])
                    e["seen"][other] = o["cnt"]
            for k, (sem, val) in self.dma_sems.items():
                key = "dma_" + k
                if val > e["seen"].get(key, 0):
                    e["h"].wait_ge(sem, val)
                    e["seen"][key] = val
        self.lastw = {}
        self.readers = {}


class Arena:
    def __init__(self, ap_f32):
        self.ap = ap_f32
        self.n = ap_f32.shape[1]
        self.off = 0

    def reset(self):
        self.off = 0

    def f32(self, n):
        assert self.off + n <= self.n, ("arena overflow", self.off, n, self.n)
        v = self.ap[:, self.off:self.off + n]
        self.off += n
        return v

    def bf16(self, n):
        m = (n + 1) // 2
        return self.f32(m).bitcast(BF16)[:, 0:n]


def _col(v):
    v = np.asarray(v, np.float32).reshape(-1)
    return np.ascontiguousarray(v.reshape(-1, 128).T)


def pack_vecs(inp):
    cols = []
    idx = {}

    def add(name, arr2d):
        idx[name] = sum(c.shape[1] for c in cols)
        cols.append(np.ascontiguousarray(arr2d, dtype=np.float32))

    for l in range(2):
        add("mixg%d" % l, _col(inp["mix_norm_g"][l]))
    for l in range(2):
        add("ffng%d" % l, _col(inp["ffn_norm_g"][l]))
    for k in range(3):
        add("scw%d" % k, _col(inp["ev_sc_conv_w"][0, k]))
    for k in range(31):
        add("cfw%d" % k, _col(inp["ev_cf_conv_w"][0, k]))
    add("cfb", _col(inp["ev_cf_conv_b"][0]))
    add("lng", _col(inp["ev_cf_ln_g"][0]))
    add("lnb", _col(inp["ev_cf_ln_b"][0]))
    for k in range(4):
        add("lruw%d" % k, _col(inp["od_lru_conv_w"][0, k]))
    add("lrub", _col(inp["od_lru_conv_b"][0]))
    add("ba", _col(inp["od_lru_ba"][0]))
    add("bx", _col(inp["od_lru_bx"][0]))
    add("lam", _col(inp["od_lru_lam"][0]))
    add("qg", np.tile(np.asarray(inp["od_q_norm_g"][0], np.float32), 2).reshape(128, 1))
    add("kg", np.tile(np.asarray(inp["od_k_norm_g"][0], np.float32), 2).reshape(128, 1))
    for l in range(2):
        for k in range(3):
            add("fcw%d_%d" % (l, k), _col(inp["ffn_conv_w"][l, k]))
        add("fcb%d" % l, _col(inp["ffn_conv_b"][l]))
    return np.ascontiguousarray(np.concatenate(cols, axis=1)), idx


def blockdiag(w):
    w = np.asarray(w, np.float32)
    bd = np.zeros((128, 4, 128), np.float32)
    for c in range(4):
        for hl in range(2):
            bd[hl * 64:(hl + 1) * 64, c, hl * 64:(hl + 1) * 64] = w[2 * c + hl]
    return np.ascontiguousarray(bd.reshape(128, 512))


_VIDX = None


def vec_index():
    global _VIDX
    if _VIDX is None:
        fake = {
            "mix_norm_g": np.zeros((2, 1024)), "ffn_norm_g": np.zeros((2, 1024)),
            "ev_sc_conv_w": np.zeros((1, 3, 512)), "ev_cf_conv_w": np.zeros((1, 31, 512)),
            "ev_cf_conv_b": np.zeros((1, 512)), "ev_cf_ln_g": np.zeros((1, 512)), "ev_cf_ln_b": np.zeros((1, 512)),
            "od_lru_conv_w": np.zeros((1, 4, 512)), "od_lru_conv_b": np.zeros((1, 512)),
            "od_lru_ba": np.zeros((1, 512)), "od_lru_bx": np.zeros((1, 512)), "od_lru_lam": np.zeros((1, 512)),
            "od_q_norm_g": np.zeros((1, 64)), "od_k_norm_g": np.zeros((1, 64)),
            "ffn_conv_w": np.zeros((2, 3, 2816)), "ffn_conv_b": np.zeros((2, 2816)),
        }
        v, idx = pack_vecs(fake)
        _VIDX = (idx, v.shape[1])
    return _VIDX


def build(S, n_layers=2, debug=False, stop_after=None):
    NT = S // TT
    NBLK = S // 128
    VI, NV = vec_index()
    nc = bass.Bass("TRN2", target_bir_lowering=False)
    kb = KB(nc)

    def din(name, shape, dt=F32):
        return nc.dram_tensor(name, shape, dt, kind="ExternalInput").ap()

    skind = "ExternalOutput" if debug else "Internal"

    def dscr(name, shape, dt):
        return nc.dram_tensor(name, shape, dt, kind=skind).ap()

    x_in = din("x", [S, D])
    vecs_d = din("vecs", [128, NV])
    bdA_d = din("bdA", [128, 512])
    bdX_d = din("bdX", [128, 512])
    W = {
        "ev_in": din("ev_w_in", [D, 2560]), "ev_out": din("ev_w_out", [1024, D]),
        "od_in": din("od_w_in", [D, 3328]), "od_out": din("od_w_out", [768, D]),
        "g0": din("ffn_w_gate0", [D, DFF]), "u0": din("ffn_w_up0", [D, DFF]), "d0": din("ffn_w_down0", [DFF, D]),
        "g1": din("ffn_w_gate1", [D, DFF]), "u1": din("ffn_w_up1", [D, DFF]), "d1": din("ffn_w_down1", [DFF, D]),
    }
    out_d = nc.dram_tensor("out", [S, D], F32, kind="ExternalOutput").ap()

    xT_s = dscr("xT_s", [D, S], F32)
    u0_s = dscr("u0_s", [512, S], BF16)
    cf_s = dscr("cf_s", [512, S], F32)
    u1_s = dscr("u1_s", [768, S], BF16)
    act_s = dscr("act_s", [DFF, S], BF16)
    qn_s = dscr("qn_s", [768, S], BF16)
    kn_s = dscr("kn_s", [768, S], BF16)
    v_s = dscr("v_s", [768, S], BF16)

    def fm(ap, t=None):
        v = ap.rearrange("(c p) s -> p c s", p=128)
        if t is not None:
            v = v[:, :, t * TT:(t + 1) * TT]
        return v

    ident_f = nc.alloc_sbuf_tensor("ident_f", [128, 128], F32)
    ident_b = nc.alloc_sbuf_tensor("ident_b", [128, 128], BF16)
    ones_b = nc.alloc_sbuf_tensor("ones_b", [128, 128], BF16)
    onesLN = nc.alloc_sbuf_tensor("onesLN", [128, 128], F32)
    blk_b = nc.alloc_sbuf_tensor("blk_b", [128, 128], BF16)
    mask_b = nc.alloc_sbuf_tensor("mask_b", [128, 2, 256], BF16)
    vecs = nc.alloc_sbuf_tensor("vecs_sb", [128, NV], F32)
    dv = nc.alloc_sbuf_tensor("dv", [128, 16], F32)
    bdA = nc.alloc_sbuf_tensor("bdA_sb", [128, 4, 128], BF16)
    bdX = nc.alloc_sbuf_tensor("bdX_sb", [128, 4, 128], BF16)
    hTraw = nc.alloc_sbuf_tensor("hTraw", [128, 4 * S], F32)
    hT = hTraw[:, :].bitcast(BF16).rearrange("p (c s) -> p c s", c=8)
    arenaH = Arena(hTraw[:, :])
    wres = nc.alloc_sbuf_tensor("wres", [128, NFF * 1024], BF16)
    AR = (nc.sbuf_bytes_remaining - 2048) // 4
    arena_t = nc.alloc_sbuf_tensor("arena", [128, AR], F32)
    arena = Arena(arena_t[:, :])
    banks = [nc.alloc_psum_tensor("bank%d" % i, [128, 512], F32) for i in range(8)]

    def V(name, i=0):
        o = VI[name] + i
        return vecs[:, o:o + 1]

    EPSC = dv[:, 0:1]
    QG8 = dv[:, 1:2]

    def SP8(i):
        return dv[:, 2 + i:3 + i]

    def SP16(i):
        return dv[:, 6 + i:7 + i]

    class RR:
        def __init__(self, ids):
            self.ids = ids
            self.i = 0

        def next(self):
            b = self.ids[self.i % len(self.ids)]
            self.i += 1
            return b

    class Slots:
        def __init__(self, name, aps):
            self.name = name
            self.aps = aps
            self.i = -1

        def next(self):
            self.i += 1
            k = self.i % len(self.aps)
            return self.aps[k], "%s%d" % (self.name, k)

    kb.dma("vecs", vecs[:, :], vecs_d, writes=["vecs"])
    kb.op("pool", lambda e: e.memset(ident_f[:, :], 1.0), writes=["ident_f"])
    kb.op("pool", lambda e: e.affine_select(out=ident_f[:, :], in_=ident_f[:, :], pattern=[[-1, 128]],
                                             compare_op=ALU.is_equal, fill=0.0, base=0, channel_multiplier=1),
          reads=["ident_f"], writes=["ident_f"])
    kb.op("pool", lambda e: e.tensor_copy(out=ident_b[:, :], in_=ident_f[:, :]), reads=["ident_f"], writes=["ident_b"])
    kb.op("pool", lambda e: e.memset(ones_b[:, :], 1.0), writes=["ones_b"])
    kb.op("pool", lambda e: e.memset(onesLN[:, :], 1.0 / 512.0), writes=["onesLN"])
    kb.op("pool", lambda e: e.memset(blk_b[:, :], 0.0), writes=["blk_b"])
    kb.op("pool", lambda e: e.memset(blk_b[0:64, 0:64], 1.0), reads=["blk_b"], writes=["blk_b"])
    kb.op("pool", lambda e: e.memset(blk_b[64:128, 64:128], 1.0), reads=["blk_b"], writes=["blk_b"])
    kb.op("pool", lambda e: e.memset(mask_b[:, :, :], 1.0), writes=["mask_b"])
    for hh in range(2):
        kb.op("pool", lambda e: e.affine_select(out=mask_b[:, hh, 0:128], in_=mask_b[:, hh, 0:128], pattern=[[1, 128]],
                                                 compare_op=ALU.is_ge, fill=0.0, base=0, channel_multiplier=-1),
              reads=["mask_b"], writes=["mask_b"])
        kb.op("pool", lambda e: e.affine_select(out=mask_b[:, hh, 128:256], in_=mask_b[:, hh, 128:256], pattern=[[-1, 128]],
                                                 compare_op=ALU.is_ge, fill=0.0, base=0, channel_multiplier=1),
              reads=["mask_b"], writes=["mask_b"])
    kb.op("dve", lambda e: e.memset(dv[:, :], EPS), writes=["dv"])
    kb.op("dve", lambda e: e.tensor_scalar(out=QG8, in0=V("qg"), scalar1=0.125, scalar2=None, op0=ALU.mult),
          reads=["vecs", "dv"], writes=["dv"])
    lam4 = vecs[:, VI["lam"]:VI["lam"] + 4]
    kb.op("act", lambda e: e.activation(out=dv[:, 10:14], in_=lam4, func=AF.Exp, scale=-1.0), reads=["vecs", "dv"], writes=["dv"])
    kb.op("act", lambda e: e.activation(out=dv[:, 10:14], in_=dv[:, 10:14], func=AF.Ln, bias=1.0, scale=1.0), reads=["dv"], writes=["dv"])
    kb.op("dve", lambda e: e.tensor_scalar(out=dv[:, 2:6], in0=dv[:, 10:14], scalar1=-8.0, scalar2=None, op0=ALU.mult),
          reads=["dv"], writes=["dv"])
    kb.op("dve", lambda e: e.tensor_scalar(out=dv[:, 6:10], in0=dv[:, 10:14], scalar1=-16.0, scalar2=None, op0=ALU.mult),
          reads=["dv"], writes=["dv"])
    arena.reset()
    st0 = arena.f32(1024)
    kb.dma("st0", st0[:, 0:512], bdA_d, writes=["st0"])
    kb.op("pool", lambda e: e.tensor_copy(out=bdA[:, :, :], in_=st0[:, 0:512].rearrange("p (c m) -> p c m", c=4)), reads=["st0"], writes=["bdA"])
    kb.dma("st0b", st0[:, 512:1024], bdX_d, writes=["st0b"])
    kb.op("pool", lambda e: e.tensor_copy(out=bdX[:, :, :], in_=st0[:, 512:1024].rearrange("p (c m) -> p c m", c=4)), reads=["st0b"], writes=["bdX"])
    kb.barrier()

    deferred_casts = []

    def flush_casts():
        while deferred_casts:
            deferred_casts.pop(0)()

    def load_w_chunk(Wap, col0, dst, dkey, stg):
        if len(deferred_casts) >= len(stg.aps):
            flush_casts()
        sap, skey = stg.next()
        src = Wap.rearrange("(kc p) m -> p kc m", p=128)[:, :, col0:col0 + 128]
        kb.dma(skey, sap[:, 0:1024].rearrange("p (kc m) -> p kc m", kc=8), src, writes=[skey])
        deferred_casts.append(lambda: kb.op("pool", lambda e: e.tensor_copy(out=dst, in_=sap[:, 0:1024].rearrange("p (kc m) -> p kc m", kc=8)),
                                            reads=[skey], writes=[dkey]))

    def load_wres_one(Wap, kc, stg):
        if len(deferred_casts) >= len(stg.aps):
            flush_casts()
        sap, skey = stg.next()
        kb.dma(skey, sap[:, 0:1024], Wap[kc * 128:(kc + 1) * 128, :], writes=[skey])
        deferred_casts.append(lambda: kb.op("pool", lambda e: e.tensor_copy(out=wres[:, kc * 1024:(kc + 1) * 1024], in_=sap[:, 0:1024]),
                                            reads=[skey], writes=["wres"]))

    def proj(bank_id, wt, wkey, t, extra_reads=()):
        b = banks[bank_id]
        fns = []
        for kc in range(8):
            fns.append(lambda e, kc=kc: e.matmul(b[:, :], lhsT=wt[:, kc, :], rhs=hT[:, kc, t * TT:(t + 1) * TT],
                                                 start=(kc == 0), stop=(kc == 7)))
        kb.mm(fns, reads=[wkey, "hT%d" % t] + list(extra_reads), writes=["B%d" % bank_id])

    def norm_tile(xt, xkey, t, gname, sq, rs, rinv, statrr):
        kb.op("act", lambda e: e.activation(out=sq[:, :, :], in_=xt[:, :, :], func=AF.Square), reads=[xkey], writes=["sq"])
        bid = statrr.next()
        b = banks[bid]
        kb.mm([lambda e, c=c: e.matmul(b[:, :], lhsT=ones_b[:, :], rhs=sq[:, c, :], start=(c == 0), stop=(c == 7)) for c in range(8)],
              reads=["sq"], writes=["B%d" % bid])
        kb.op("act", lambda e: e.activation(out=rinv, in_=b[:, :], func=AF.Ln, bias=EPSC, scale=1.0 / D), reads=["B%d" % bid], writes=["rinv"])
        kb.op("act", lambda e: e.activation(out=rinv, in_=rinv, func=AF.Exp, scale=-0.5), reads=["rinv"], writes=["rinv"])
        for c in range(8):
            kb.op("dve", lambda e: e.scalar_tensor_tensor(out=hT[:, c, t * TT:(t + 1) * TT], in0=xt[:, c, :], scalar=V(gname, c),
                                                          in1=rinv, op0=ALU.mult, op1=ALU.mult),
                  reads=[xkey, "rinv"], writes=["hT%d" % t])

    def phase_p0():
        arena.reset()
        xin = Slots("xin", [arena.f32(4 * D).rearrange("p (s c) -> p s c", s=4) for _ in range(2)])
        xts = Slots("xt", [arena.f32(8 * TT).rearrange("p (c s) -> p c s", c=8) for _ in range(2)])
        sq = arena.bf16(8 * TT).rearrange("p (c s) -> p c s", c=8)
        rs = arena.f32(TT)
        rinv = arena.f32(TT)
        trr = RR([0, 1, 2, 3])
        srr = RR([4, 5])
        for t in range(NT):
            xi, xik = xin.next()
            kb.dma(xik, xi, x_in[t * TT:(t + 1) * TT, :].rearrange("(s p) c -> p s c", p=128), writes=[xik])
            xt, xk = xts.next()
            for c in range(8):
                bid = trr.next()
                b = banks[bid]
                kb.mm([lambda e, s=s: e.transpose(b[:, s * 128:(s + 1) * 128], xi[:, s, c * 128:(c + 1) * 128], ident_f[:, :]) for s in range(4)],
                      reads=[xik], writes=["B%d" % bid])
                kb.op("act", lambda e: e.activation(out=xt[:, c, :], in_=b[:, :], func=AF.Copy), reads=["B%d" % bid], writes=[xk])
            kb.dma(xk, fm(xT_s, t), xt, reads=[xk], writes=["xTs%d" % t], q="pool")
            norm_tile(xt, xk, t, "mixg0", sq, rs, rinv, srr)
        flush_casts()
        kb.barrier()

    def phase_norm(gname):
        arena.reset()
        xts = Slots("xt", [arena.f32(8 * TT).rearrange("p (c s) -> p c s", c=8) for _ in range(2)])
        sq = arena.bf16(8 * TT).rearrange("p (c s) -> p c s", c=8)
        rs = arena.f32(TT)
        rinv = arena.f32(TT)
        srr = RR([4, 5])
        for t in range(NT):
            xt, xk = xts.next()
            kb.dma(xk, xt, fm(xT_s, t), reads=["xTs%d" % t], writes=[xk])
            norm_tile(xt, xk, t, gname, sq, rs, rinv, srr)
        flush_casts()
        kb.barrier()

    def phase_p2():
        arena.reset()
        stg = Slots("stg", [arena.f32(1024) for _ in range(4)])
        ws = Slots("ws", [arena.bf16(1024).rearrange("p (kc m) -> p kc m", kc=8) for _ in range(9)])
        wres_todo = list(range(8))
        pbuf = Slots("pb", [arena.f32(2 + TT) for _ in range(2)])
        ubuf = Slots("ub", [arena.f32(30 + TT) for _ in range(2)])
        tmpa = Slots("ta", [arena.f32(TT) for _ in range(3)])
        tmpb = Slots("tb", [arena.f32(TT) for _ in range(3)])
        tmpc = Slots("tc", [arena.f32(TT) for _ in range(2)])
        tmpd = Slots("td", [arena.f32(TT) for _ in range(4)])
        yo = Slots("yo", [arena.bf16(TT) for _ in range(3)])
        co = Slots("co", [arena.f32(TT) for _ in range(3)])
        prr = RR([0, 1, 2, 3])
        crr = RR([4, 5, 6, 7])
        dgs = Slots("dg", [arena.f32(N_PE_TAPS * 128).rearrange("p (n m) -> p n m", n=N_PE_TAPS) for _ in range(2)])
        Wi = W["ev_in"]

        def getw(col0):
            wt, wk = ws.next()
            load_w_chunk(Wi, col0, wt, wk, stg)
            return wt, wk

        prr_sc = RR([0, 1, 2, 3, 4, 5, 6, 7])
        nxt = [getw(512 + 0), getw(1024 + 0), getw(0)]
        flush_casts()
        for i in range(4):
            (wc, wck), (wx, wxk), (wb, wbk) = nxt
            load_wres_one(W["ev_out"], wres_todo.pop(0), stg)
            if i < 3:
                nxt = [getw(512 + (i + 1) * 128), getw(1024 + (i + 1) * 128), getw((i + 1) * 128)]
            else:
                nxt = [getw(1536), getw(2048)]
            prev = None
            for t in range(NT):
                if t == min(3, NT - 1):
                    flush_casts()
                bc, bx, bb = prr_sc.next(), prr_sc.next(), prr_sc.next()
                proj(bc, wc, wck, t)
                proj(bx, wx, wxk, t)
                proj(bb, wb, wbk, t)
                csb, ck = tmpa.next()
                kb.op("act", lambda e: e.activation(out=csb, in_=banks[bc][:, :], func=AF.Copy), reads=["B%d" % bc], writes=[ck])
                pb, pk = pbuf.next()
                if prev is None:
                    kb.op("dve", lambda e: e.memset(pb[:, 0:2], 0.0), writes=[pk + "h"])
                else:
                    kb.op("dve", lambda e: e.tensor_copy(out=pb[:, 0:2], in_=prev[0][:, TT:TT + 2]), reads=[prev[1]], writes=[pk + "h"])
                kb.op("dve", lambda e: e.tensor_tensor(out=pb[:, 2:2 + TT], in0=banks[bx][:, :], in1=csb, op=ALU.mult),
                      reads=["B%d" % bx, ck], writes=[pk])
                acc, ak = tmpb.next()
                kb.op("dve", lambda e: e.tensor_scalar(out=acc, in0=pb[:, 2:2 + TT], scalar1=V("scw2", i), scalar2=None, op0=ALU.mult),
                      reads=[pk], writes=[ak])
                kb.op("dve", lambda e: e.scalar_tensor_tensor(out=acc, in0=pb[:, 1:1 + TT], scalar=V("scw1", i), in1=acc, op0=ALU.mult, op1=ALU.add),
                      reads=[pk, pk + "h", ak], writes=[ak])
                kb.op("dve", lambda e: e.scalar_tensor_tensor(out=acc, in0=pb[:, 0:TT], scalar=V("scw0", i), in1=acc, op0=ALU.mult, op1=ALU.add),
                      reads=[pk, pk + "h", ak], writes=[ak])
                y, yk = yo.next()
                kb.op("dve", lambda e: e.tensor_tensor(out=y, in0=banks[bb][:, :], in1=acc, op=ALU.mult), reads=["B%d" % bb, ak], writes=[yk])
                kb.dma(yk, u0_s[i * 128:(i + 1) * 128, t * TT:(t + 1) * TT], y, reads=[yk], writes=["u0_%d_%d" % (i, t)])
                prev = (pb, pk)
        cf_pend = [None]
        PE_TAPS = list(range(0, N_PE_TAPS))
        DVE_TAPS = list(range(N_PE_TAPS, 30 - POOL_TAPS))
        POOL_T = list(range(30 - POOL_TAPS, 30))

        def cf_b(acc, ak, acc2, a2k, pacc, pak, i, t):
            c_, ck_ = co.next()
            kb.op("dve", lambda e: e.scalar_tensor_tensor(out=c_, in0=acc2, scalar=V("cfb", i), in1=pacc, op0=ALU.add, op1=ALU.add),
                  reads=[a2k, pak], writes=[ck_])
            kb.op("dve", lambda e: e.tensor_tensor(out=c_, in0=acc, in1=c_, op=ALU.add), reads=[ak, ck_], writes=[ck_])
            kb.dma(ck_, cf_s[i * 128:(i + 1) * 128, t * TT:(t + 1) * TT], c_, reads=[ck_], writes=["cf_%d_%d" % (i, t)])

        for i in range(4):
            (wa, wak), (wg, wgk) = nxt
            load_wres_one(W["ev_out"], wres_todo.pop(0), stg)
            if i < 3:
                nxt = [getw(1536 + (i + 1) * 128), getw(2048 + (i + 1) * 128)]
            dg, dgk = dgs.next()
            for n, k in enumerate(PE_TAPS):
                kb.op("pool", lambda e: e.tensor_scalar(out=dg[:, n, :], in0=ident_f[:, :], scalar1=V("cfw%d" % k, i), scalar2=0.0,
                                                        op0=ALU.mult, op1=ALU.add), reads=["ident_f"], writes=[dgk])
            prev = None
            for t in range(NT):
                if t == min(3, NT - 1):
                    flush_casts()
                ba_, bg_ = prr.next(), prr.next()
                proj(ba_, wa, wak, t)
                proj(bg_, wg, wgk, t)
                sg, sk = tmpa.next()
                kb.op("act", lambda e: e.activation(out=sg, in_=banks[bg_][:, :], func=AF.Sigmoid), reads=["B%d" % bg_], writes=[sk])
                ub, uk = ubuf.next()
                if prev is None:
                    kb.op("dve", lambda e: e.memset(ub[:, 0:30], 0.0), writes=[uk + "h"])
                else:
                    kb.op("dve", lambda e: e.tensor_copy(out=ub[:, 0:30], in_=prev[0][:, TT:TT + 30]), reads=[prev[1]], writes=[uk + "h"])
                kb.op("dve", lambda e: e.tensor_tensor(out=ub[:, 30:30 + TT], in0=banks[ba_][:, :], in1=sg, op=ALU.mult),
                      reads=["B%d" % ba_, sk], writes=[uk])
                cb2 = crr.next()
                acc2, a2k = banks[cb2][:, :], "B%d" % cb2
                kb.mm([lambda e, n=n, k=k: e.matmul(acc2, lhsT=dg[:, n, :], rhs=ub[:, k:k + TT], start=(n == 0), stop=(n == len(PE_TAPS) - 1))
                       for n, k in enumerate(PE_TAPS)], reads=[uk, uk + "h", dgk], writes=[a2k])
                cb = crr.next()
                acc, ak = banks[cb][:, :], "B%d" % cb
                kb.op("dve", lambda e: e.tensor_scalar(out=acc, in0=ub[:, 30:30 + TT], scalar1=V("cfw30", i), scalar2=None,
                                                       op0=ALU.mult), reads=[uk], writes=[ak])
                for k in DVE_TAPS:
                    kb.op("dve", lambda e: e.scalar_tensor_tensor(out=acc, in0=ub[:, k:k + TT], scalar=V("cfw%d" % k, i), in1=acc,
                                                                  op0=ALU.mult, op1=ALU.add), reads=[uk, uk + "h", ak], writes=[ak])
                pacc, pak = tmpc.next()
                for n, k in enumerate(POOL_T):
                    if n == 0:
                        dst, dk = pacc, pak
                    else:
                        dst, dk = tmpd.next()
                    kb.op("act", lambda e: e.activation(out=dst, in_=ub[:, k:k + TT], func=AF.Identity, scale=V("cfw%d" % k, i)),
                          reads=[uk, uk + "h"], writes=[dk])
                    if n > 0:
                        kb.op("pool", lambda e: e.tensor_tensor(out=pacc, in0=pacc, in1=dst, op=ALU.add), reads=[pak, dk], writes=[pak])
                if cf_pend[0] is not None:
                    cf_b(*cf_pend[0])
                cf_pend[0] = (acc, ak, acc2, a2k, pacc, pak, i, t)
                prev = (ub, uk)
        cf_b(*cf_pend[0])
        flush_casts()
        kb.barrier()

    def phase_col(KC, load_fn, prep_fn, gname, final=False):
        xts = Slots("xt", [arena.f32(8 * TT).rearrange("p (c s) -> p c s", c=8) for _ in range(2)])
        if gname is not None:
            sq = arena.bf16(8 * TT).rearrange("p (c s) -> p c s", c=8)
            rs = None
            rinv = arena.f32(TT)
        if final:
            arenaH.reset()
            ots = Slots("ot", [arenaH.f32(4 * D).rearrange("p (s c) -> p s c", s=4) for _ in range(2)])
        orr = RR([0, 1, 2, 3])
        srr = RR([4, 5])
        trr = RR([6, 7])
        def loadx(t):
            xt, xk = xts.next()
            kb.dma(xk, xt, fm(xT_s, t), reads=["xTs%d" % t], writes=[xk])
            return (xt, xk)

        HU = {}
        HX = {}
        P = {}
        for t0 in range(min(2, NT)):
            HU[t0] = load_fn(t0)
            HX[t0] = loadx(t0)
        P[0] = prep_fn(HU[0])
        for t in range(NT):
            xt, xk = HX[t]
            if t + 1 < NT:
                P[t + 1] = prep_fn(HU[t + 1])
            ut, uk = P[t]
            for m in range(8):
                bid = orr.next()
                b = banks[bid]
                kb.mm([lambda e, kc=kc: e.matmul(b[:, :], lhsT=wres[:, kc * 1024 + m * 128: kc * 1024 + (m + 1) * 128], rhs=ut[:, kc, :],
                                                 start=(kc == 0), stop=(kc == KC - 1)) for kc in range(KC)],
                      reads=["wres"] + (uk if isinstance(uk, list) else [uk]), writes=["B%d" % bid])
                kb.op("dve", lambda e: e.tensor_tensor(out=xt[:, m, :], in0=b[:, :], in1=xt[:, m, :], op=ALU.add), reads=["B%d" % bid, xk], writes=[xk])
            if t + 2 < NT:
                HU[t + 2] = load_fn(t + 2)
            if not final:
                kb.dma(xk, fm(xT_s, t), xt, reads=[xk], writes=["xTs%d" % t], q="pool")
                if gname is not None:
                    norm_tile(xt, xk, t, gname, sq, rs, rinv, srr)
            else:
                ot, ok = ots.next()
                for s in range(4):
                    for half in range(2):
                        bid = trr.next()
                        b = banks[bid]
                        kb.mm([lambda e, q=q: e.transpose(b[:, q * 128:(q + 1) * 128], xt[:, half * 4 + q, s * 128:(s + 1) * 128], ident_f[:, :]) for q in range(4)],
                              reads=[xk], writes=["B%d" % bid])
                        kb.op("act", lambda e: e.activation(out=ot[:, s, half * 512:(half + 1) * 512], in_=b[:, :], func=AF.Copy),
                              reads=["B%d" % bid], writes=[ok])
                kb.dma(ok, out_d[t * TT:(t + 1) * TT, :].rearrange("(s p) c -> p s c", p=128), ot, reads=[ok], writes=["out%d" % t], q="pool")
            if t + 2 < NT:
                HX[t + 2] = loadx(t + 2)

    def phase_p3():
        arena.reset()
        uts = Slots("ut", [arena.bf16(8 * TT).rearrange("p (c s) -> p c s", c=8) for _ in range(2)])
        cfts = Slots("cft", [arena.f32(4 * TT).rearrange("p (c s) -> p c s", c=4) for _ in range(2)])
        sq4 = arena.bf16(4 * TT).rearrange("p (c s) -> p c s", c=4)
        mean_sb = arena.f32(TT)
        m2 = arena.f32(TT)
        var = m2
        sd = m2
        rinv2 = m2
        tmps = Slots("lt", [arena.f32(TT) for _ in range(2)])
        lrr = RR([6, 7])

        def load_fn(t):
            ut, uk = uts.next()
            kb.dma(uk, ut[:, 0:4, :], fm(u0_s, t), reads=["u0_%d_%d" % (i, t) for i in range(4)], writes=[uk + "lo"])
            cft, ck = cfts.next()
            kb.dma(ck, cft, fm(cf_s, t), reads=["cf_%d_%d" % (i, t) for i in range(4)], writes=[ck])
            return (ut, uk, cft, ck)

        def prep_fn(h):
            ut, uk, cft, ck = h
            bm = lrr.next()
            kb.mm([lambda e, i=i: e.matmul(banks[bm][:, :], lhsT=onesLN[:, :], rhs=cft[:, i, :], start=(i == 0), stop=(i == 3)) for i in range(4)],
                  reads=[ck], writes=["B%d" % bm])
            kb.op("act", lambda e: e.activation(out=sq4[:, :, :], in_=cft[:, :, :], func=AF.Square), reads=[ck], writes=["sq4"])
            be = lrr.next()
            kb.mm([lambda e, i=i: e.matmul(banks[be][:, :], lhsT=ones_b[:, :], rhs=sq4[:, i, :], start=(i == 0), stop=(i == 3)) for i in range(4)],
                  reads=["sq4"], writes=["B%d" % be])
            kb.op("act", lambda e: e.activation(out=mean_sb, in_=banks[bm][:, :], func=AF.Copy), reads=["B%d" % bm], writes=["mean"])
            kb.op("dve", lambda e: e.tensor_tensor(out=m2, in0=mean_sb, in1=mean_sb, op=ALU.mult), reads=["mean"], writes=["m2"])
            kb.op("dve", lambda e: e.scalar_tensor_tensor(out=var, in0=banks[be][:, :], scalar=1.0 / 512.0, in1=m2, op0=ALU.mult, op1=ALU.subtract),
                  reads=["B%d" % be, "m2"], writes=["m2"])
            kb.op("dve", lambda e: e.tensor_scalar(out=var, in0=var, scalar1=0.0, scalar2=None, op0=ALU.max), reads=["m2"], writes=["m2"])
            kb.op("act", lambda e: e.activation(out=sd, in_=var, func=AF.Ln, bias=EPSC, scale=1.0), reads=["m2"], writes=["m2"])
            kb.op("act", lambda e: e.activation(out=rinv2, in_=sd, func=AF.Exp, scale=-0.5), reads=["m2"], writes=["m2"])
            for i in range(4):
                tp, tk = tmps.next()
                kb.op("dve", lambda e: e.tensor_tensor(out=tp, in0=cft[:, i, :], in1=mean_sb, op=ALU.subtract), reads=[ck, "mean"], writes=[tk])
                kb.op("dve", lambda e: e.tensor_tensor(out=tp, in0=tp, in1=rinv2, op=ALU.mult), reads=[tk, "m2"], writes=[tk])
                kb.op("act", lambda e: e.activation(out=ut[:, 4 + i, :], in_=tp, func=AF.Silu, bias=V("lnb", i), scale=V("lng", i)),
                      reads=[tk], writes=[uk])
            return ut, [uk, uk + "lo"]

        phase_col(8, load_fn, prep_fn, "ffng0")
        flush_casts()
        kb.barrier()

    def phase_f1(l):
        arena.reset()
        stg = Slots("stg", [arena.f32(1024) for _ in range(3)])
        ws = Slots("ws", [arena.bf16(1024).rearrange("p (kc m) -> p kc m", kc=8) for _ in range(6)])
        gbuf = Slots("gb", [arena.f32(2 + TT) for _ in range(3)])
        tmpb = Slots("tb", [arena.f32(TT) for _ in range(3)])
        tmps = Slots("tsl", [arena.f32(TT) for _ in range(3)])
        ao = Slots("ao", [arena.bf16(TT) for _ in range(4)])
        prr = RR([0, 1, 2, 3, 4, 5, 6, 7])
        Wg, Wu = W["g%d" % l], W["u%d" % l]

        def getw(Wap, col0):
            wt, wk = ws.next()
            load_w_chunk(Wap, col0, wt, wk, stg)
            return wt, wk

        ups = Slots("up", [arena.bf16(TT) for _ in range(3)])

        def stage_b(acc, ak, up, upk, j, t):
            sl, sk = tmps.next()
            kb.op("act", lambda e: e.activation(out=sl, in_=acc, func=AF.Silu), reads=[ak], writes=[sk])
            a_, ak_ = ao.next()
            kb.op("pool", lambda e: e.tensor_tensor(out=a_, in0=up, in1=sl, op=ALU.mult), reads=[upk, sk], writes=[ak_])
            kb.dma(ak_, act_s[j * 128:(j + 1) * 128, t * TT:(t + 1) * TT], a_, reads=[ak_], writes=["act_%d_%d" % (j, t)])

        pend = None
        nxt = [getw(Wg, 0), getw(Wu, 0)]
        flush_casts()
        for j in range(NFF):
            (wg, wgk), (wu, wuk) = nxt
            if j + 1 < NFF:
                nxt = [getw(Wg, (j + 1) * 128), getw(Wu, (j + 1) * 128)]
            load_wres_one(W["d%d" % l], j, stg)
            prev = None
            for t in range(NT):
                if t == min(3, NT - 1):
                    flush_casts()
                bg_, bu_ = prr.next(), prr.next()
                proj(bg_, wg, wgk, t)
                proj(bu_, wu, wuk, t)
                gb, gk = gbuf.next()
                if prev is None:
                    kb.op("dve", lambda e: e.memset(gb[:, 0:2], 0.0), writes=[gk + "h"])
                else:
                    kb.op("dve", lambda e: e.tensor_copy(out=gb[:, 0:2], in_=prev[0][:, TT:TT + 2]), reads=[prev[1]], writes=[gk + "h"])
                kb.op("act", lambda e: e.activation(out=gb[:, 2:2 + TT], in_=banks[bg_][:, :], func=AF.Copy), reads=["B%d" % bg_], writes=[gk])
                acc, ak = tmpb.next()
                kb.op("act", lambda e: e.activation(out=acc, in_=banks[bg_][:, :], func=AF.Identity, bias=V("fcb%d" % l, j), scale=V("fcw%d_2" % l, j)),
                      reads=["B%d" % bg_], writes=[ak])
                up, upk = ups.next()
                kb.op("act", lambda e: e.activation(out=up, in_=banks[bu_][:, :], func=AF.Copy), reads=["B%d" % bu_], writes=[upk])
                kb.op("dve", lambda e: e.scalar_tensor_tensor(out=acc, in0=gb[:, 1:1 + TT], scalar=V("fcw%d_1" % l, j), in1=acc, op0=ALU.mult, op1=ALU.add),
                      reads=[gk, gk + "h", ak], writes=[ak])
                kb.op("dve", lambda e: e.scalar_tensor_tensor(out=acc, in0=gb[:, 0:TT], scalar=V("fcw%d_0" % l, j), in1=acc, op0=ALU.mult, op1=ALU.add),
                      reads=[gk, gk + "h", ak], writes=[ak])
                if pend is not None:
                    stage_b(*pend)
                pend = (acc, ak, up, upk, j, t)
                prev = (gb, gk)
        stage_b(*pend)
        flush_casts()
        kb.barrier()

    def phase_f2(final):
        arena.reset()
        ats = Slots("at", [arena.bf16(NFF * TT).rearrange("p (c s) -> p c s", c=NFF) for _ in range(2)])

        def load_u(t):
            at, ak = ats.next()
            kb.dma(ak, at, fm(act_s, t), reads=["act_%d_%d" % (j, t) for j in range(NFF)], writes=[ak])
            return at, ak

        phase_col(NFF, load_u, lambda h: h, None if final else "mixg1", final=final)
        flush_casts()
        kb.barrier()

    def phase_q2():
        arena.reset()
        stg = Slots("stg", [arena.f32(1024) for _ in range(3)])
        ws = Slots("ws", [arena.bf16(1024).rearrange("p (kc m) -> p kc m", kc=8) for _ in range(5)])
        wres_todo = list(range(6))
        lru_mark = arena.off
        xbuf = Slots("xb", [arena.f32(3 + TT) for _ in range(3)])
        nsl = {"xc": 4, "a": 4, "bt": 4, "mu": 4, "r": 2, "ig": 2, "a2": 2}
        T = {n: Slots(n, [arena.f32(TT) for _ in range(k)]) for n, k in nsl.items()}
        T["gs"] = Slots("gs", [arena.bf16(TT) for _ in range(6)])
        hb = Slots("hb", [arena.f32(TT) for _ in range(3)])
        xcb = Slots("xcb", [arena.bf16(TT) for _ in range(4)])
        yo = Slots("yo", [arena.bf16(TT) for _ in range(3)])
        prr = RR([0, 1, 2, 3])
        srr = RR([4, 5, 6, 7])
        Wi = W["od_in"]

        def getw(col0):
            wt, wk = ws.next()
            load_w_chunk(Wi, col0, wt, wk, stg)
            return wt, wk

        def lru_a(cx):
            i, t = cx["i"], cx["t"]
            bx_, bg_ = prr.next(), prr.next()
            cx["bg"] = bg_
            proj(bx_, cx["wx"], cx["wxk"], t)
            proj(bg_, cx["wg"], cx["wgk"], t)
            xb, xk = xbuf.next()
            prev = cx["prev"]
            if prev is None:
                kb.op("dve", lambda e: e.memset(xb[:, 0:3], 0.0), writes=[xk + "h"])
            else:
                kb.op("dve", lambda e: e.tensor_copy(out=xb[:, 0:3], in_=prev[0][:, TT:TT + 3]), reads=[prev[1]], writes=[xk + "h"])
            kb.op("act", lambda e: e.activation(out=xb[:, 3:3 + TT], in_=banks[bx_][:, :], func=AF.Copy), reads=["B%d" % bx_], writes=[xk])
            cx["xb"] = (xb, xk)
            gs, gsk = T["gs"].next()
            kb.op("act", lambda e: e.activation(out=gs, in_=banks[bg_][:, :], func=AF.Gelu_apprx_tanh), reads=["B%d" % bg_], writes=[gsk])
            cx["gs"] = (gs, gsk)
            xc, xck = T["xc"].next()
            kb.op("act", lambda e: e.activation(out=xc, in_=banks[bx_][:, :], func=AF.Identity, bias=V("lrub", i), scale=V("lruw3", i)),
                  reads=["B%d" % bx_], writes=[xck])
            for k in (2, 1, 0):
                kb.op("dve", lambda e: e.scalar_tensor_tensor(out=xc, in0=xb[:, k:k + TT], scalar=V("lruw%d" % k, i), in1=xc,
                                                              op0=ALU.mult, op1=ALU.add), reads=[xk, xk + "h", xck], writes=[xck])
            cx["xc"] = (xc, xck)
            xcbf, xcbk = xcb.next()
            kb.op("dve", lambda e: e.tensor_copy(out=xcbf, in_=xc), reads=[xck], writes=[xcbk])
            cx["xcbf"] = (xcbf, xcbk)

        def lru_b(cxs):
            st = []
            for cx in cxs:
                i = cx["i"]
                xcbf, xcbk = cx["xcbf"]
                br_, bi_ = srr.next(), srr.next()
                kb.mm([lambda e, i=i, br_=br_, xcbf=xcbf: e.matmul(banks[br_][:, :], lhsT=bdA[:, i, :], rhs=xcbf, start=True, stop=True)],
                      reads=["bdA", xcbk], writes=["B%d" % br_])
                kb.mm([lambda e, i=i, bi_=bi_, xcbf=xcbf: e.matmul(banks[bi_][:, :], lhsT=bdX[:, i, :], rhs=xcbf, start=True, stop=True)],
                      reads=["bdX", xcbk], writes=["B%d" % bi_])
                st.append(dict(cx=cx, i=i, br=br_, bi=bi_))
            for s_ in st:
                i = s_["i"]
                r, rk = T["r"].next()
                ig, igk = T["ig"].next()
                s_["r"], s_["ig"] = (r, rk), (ig, igk)
                kb.op("act", lambda e: e.activation(out=r, in_=banks[s_["br"]][:, :], func=AF.Sigmoid, bias=V("ba", i)), reads=["B%d" % s_["br"]], writes=[rk])
                kb.op("act", lambda e: e.activation(out=ig, in_=banks[s_["bi"]][:, :], func=AF.Sigmoid, bias=V("bx", i)), reads=["B%d" % s_["bi"]], writes=[igk])
            for s_ in st:
                i = s_["i"]
                r, rk = s_["r"]
                a, ak = T["a"].next()
                a2, a2k = T["a2"].next()
                s_["a2"] = (a2, a2k)
                kb.op("act", lambda e: e.activation(out=a, in_=r, func=AF.Exp, scale=SP8(i)), reads=[rk], writes=[ak])
                kb.op("act", lambda e: e.activation(out=a2, in_=r, func=AF.Exp, scale=SP16(i)), reads=[rk], writes=[a2k])
                s_["cx"]["a"] = (a, ak)
            for s_ in st:
                a2, a2k = s_["a2"]
                ig, igk = s_["ig"]
                xc, xck = s_["cx"]["xc"]
                kb.op("dve", lambda e: e.tensor_scalar(out=a2, in0=a2, scalar1=1.0, scalar2=0.0, op0=ALU.subtract, op1=ALU.min), reads=[a2k], writes=[a2k])
                bt, btk = T["bt"].next()
                kb.op("dve", lambda e: e.tensor_tensor(out=bt, in0=ig, in1=xc, op=ALU.mult), reads=[igk, xck], writes=[btk])
                s_["cx"]["bt"] = (bt, btk)
            for s_ in st:
                a2, a2k = s_["a2"]
                mu, muk = T["mu"].next()
                kb.op("act", lambda e: e.activation(out=mu, in_=a2, func=AF.Sqrt, scale=-1.0), reads=[a2k], writes=[muk])
                s_["cx"]["mu"] = (mu, muk)

        def lru_c(cx):
            i, t = cx["i"], cx["t"]
            a, ak = cx["a"]
            bt, btk = cx["bt"]
            mu, muk = cx["mu"]
            gs, gsk = cx["gs"]
            kb.op("dve", lambda e: e.tensor_tensor(out=bt, in0=bt, in1=mu, op=ALU.mult), reads=[btk, muk], writes=[btk])
            h, hk = hb.next()
            hprev = lru_state["hprev"] if t > 0 else None
            if hprev is None:
                kb.op("dve", lambda e: e.tensor_tensor_scan(out=h, data0=a, data1=bt, initial=0.0, op0=ALU.mult, op1=ALU.add),
                      reads=[ak, btk], writes=[hk])
            else:
                kb.op("dve", lambda e: e.tensor_tensor_scan(out=h, data0=a, data1=bt, initial=hprev[0][:, TT - 1:TT], op0=ALU.mult, op1=ALU.add),
                      reads=[ak, btk, hprev[1]], writes=[hk])
            y, yk = yo.next()
            kb.op("pool", lambda e: e.tensor_tensor(out=y, in0=h, in1=gs, op=ALU.mult), reads=[hk, gsk], writes=[yk])
            kb.dma(yk, u1_s[i * 128:(i + 1) * 128, t * TT:(t + 1) * TT], y, reads=[yk], writes=["u1_%d_%d" % (i, t)])
            lru_state["hprev"] = (h, hk)

        lru_state = {"hprev": None}
        pipe = []
        nxt = [getw(0), getw(512)]
        flush_casts()
        nB = 0
        nC = 0
        for i in range(4):
            (wx, wxk), (wg, wgk) = nxt
            load_wres_one(W["od_out"], wres_todo.pop(0), stg)
            nxt = [getw((i + 1) * 128), getw(512 + (i + 1) * 128)] if i < 3 else [getw(1024), getw(1792)]
            prev = None
            for t in range(NT):
                if t == min(3, NT - 1):
                    flush_casts()
                cx = dict(i=i, t=t, wx=wx, wxk=wxk, wg=wg, wgk=wgk, prev=prev)
                lru_a(cx)
                prev = cx["xb"]
                pipe.append(cx)
                if len(pipe) % 2 == 0:
                    if len(pipe) - nB >= 4:
                        lru_b(pipe[nB:nB + 2])
                        nB += 2
                    while nB - nC > 2:
                        lru_c(pipe[nC])
                        nC += 1
        while nB < len(pipe):
            lru_b(pipe[nB:nB + 2])
            nB += 2
        while nC < len(pipe):
            lru_c(pipe[nC])
            nC += 1
        flush_casts()
        kb.barrier()
        arena.off = lru_mark
        sqb = Slots("sqb", [arena.bf16(TT) for _ in range(2)])
        rqs = Slots("rq", [arena.f32(TT) for _ in range(2)])
        rows = Slots("row", [arena.bf16(S) for _ in range(3)])
        qk_pend = [None]

        def qk_b(bp, bs, row, rk, t, d, gcol):
            rq, rqk = rqs.next()
            kb.op("act", lambda e: e.activation(out=rq, in_=banks[bs][:, :], func=AF.Ln, bias=EPSC, scale=1.0 / 64.0), reads=["B%d" % bs], writes=[rqk])
            kb.op("act", lambda e: e.activation(out=rq, in_=rq, func=AF.Exp, scale=-0.5), reads=[rqk], writes=[rqk])
            if d == 1:
                yv, pin, rin = row[:, t * TT:(t + 1) * TT], banks[bp][:, :], rq
            else:
                yv = row.rearrange("p (r m) -> p m r", r=d)[:, t * (TT // d):(t + 1) * (TT // d), :]
                pin = banks[bp][:, :].rearrange("p (m r) -> p m r", r=d)
                rin = rq.rearrange("p (m r) -> p m r", r=d)
            kb.op("dve", lambda e: e.scalar_tensor_tensor(out=yv, in0=pin, scalar=gcol, in1=rin, op0=ALU.mult, op1=ALU.add if False else ALU.mult),
                  reads=["B%d" % bp, rqk], writes=[rk + "_%d" % t])

        for c in range(6):
            d = DILS[c // 2]
            (wq, wqk), (wk_, wkk) = nxt
            if wres_todo:
                load_wres_one(W["od_out"], wres_todo.pop(0), stg)
            nxt = [getw(1024 + (c + 1) * 128), getw(1792 + (c + 1) * 128)] if c < 5 else [getw(2560)]
            for (wt, wkey, gcol, dst, nm) in ((wq, wqk, QG8, qn_s, "qn"), (wk_, wkk, V("kg"), kn_s, "kn")):
                row, rk = rows.next()
                for t in range(NT):
                    if t == min(3, NT - 1):
                        flush_casts()
                    bp = prr.next()
                    proj(bp, wt, wkey, t)
                    sq, sqk = sqb.next()
                    kb.op("act", lambda e: e.activation(out=sq, in_=banks[bp][:, :], func=AF.Square), reads=["B%d" % bp], writes=[sqk])
                    bs = srr.next()
                    kb.mm([lambda e: e.matmul(banks[bs][:, :], lhsT=blk_b[:, :], rhs=sq, start=True, stop=True)], reads=[sqk], writes=["B%d" % bs])
                    if qk_pend[0] is not None:
                        qk_b(*qk_pend[0])
                    qk_pend[0] = (bp, bs, row, rk, t, d, gcol)
                if qk_pend[0] is not None:
                    qk_b(*qk_pend[0])
                    qk_pend[0] = None
                kb.dma(rk, dst[c * 128:(c + 1) * 128, :], row, reads=[rk + "_%d" % t for t in range(NT)], writes=["%s_%d" % (nm, c)])
        for c in range(6):
            d = DILS[c // 2]
            (wv, wvk), = nxt
            if c < 5:
                nxt = [getw(2560 + (c + 1) * 128)]
            row, rk = rows.next()
            for t in range(NT):
                if t == min(3, NT - 1):
                    flush_casts()
                bp = prr.next()
                proj(bp, wv, wvk, t)
                if d == 1:
                    yv, pin = row[:, t * TT:(t + 1) * TT], banks[bp][:, :]
                else:
                    yv = row.rearrange("p (r m) -> p m r", r=d)[:, t * (TT // d):(t + 1) * (TT // d), :]
                    pin = banks[bp][:, :].rearrange("p (m r) -> p m r", r=d)
                kb.op("act", lambda e: e.activation(out=yv, in_=pin, func=AF.Copy),
                      reads=["B%d" % bp], writes=[rk + "_%d" % t])
            kb.dma(rk, v_s[c * 128:(c + 1) * 128, :], row, reads=[rk + "_%d" % t for t in range(NT)], writes=["v_%d" % c])
        flush_casts()
        kb.barrier()

    def phase_qa():
        arena.reset()
        arenaH.reset()
        qrow = arenaH.bf16(2 * S).rearrange("p (c s) -> p c s", c=2)
        krz = arenaH.bf16(4 * S).rearrange("p (h c s) -> p h c s", h=2, c=2)
        Vt = arenaH.bf16(NBLK * 256).rearrange("p (b c m) -> p b c m", b=NBLK, c=2)
        vrow = arena.bf16(2 * S).rearrange("p (c s) -> p c s", c=2)
        accN = arena.f32(2 * S).rearrange("p (c s) -> p c s", c=2)
        accD = arena.f32(2 * S).rearrange("p (c s) -> p c s", c=2)
        pts = Slots("pt", [arena.bf16(2 * 2 * 256).rearrange("p (c h n) -> p c h n", c=2, h=2) for _ in range(3)])
        yrow = qrow
        for hh in range(2):
            oh = 1 - hh
            kb.op("pool", lambda e: e.memset(krz[oh * 64:(oh + 1) * 64, hh, :, :], 0.0), writes=["krzz%d" % hh])
        srr = RR([0, 1, 2, 3])
        trr = RR([6, 7])
        qa_pend = [None]

        def qa_b(g, d, NB, r, kbk, gblk, pt, ptk):
            def pv(bank_id, col0, first):
                ba = banks[bank_id][:, :].rearrange("p (i n) -> p i n", i=4)
                fns = []
                for hh in range(2):
                    for c in range(2):
                        fns.append(lambda e, hh=hh, c=c: e.matmul(ba[hh * 64:(hh + 1) * 64, c, :], lhsT=Vt[:, gblk, c, hh * 64:(hh + 1) * 64],
                                                                  rhs=pt[:, c, hh, col0:col0 + 128], start=(first and c == 0), stop=False,
                                                                  skip_group_check=True))
                        fns.append(lambda e, hh=hh, c=c: e.matmul(ba[hh * 64:(hh + 1) * 64, 2 + c, :], lhsT=ones_b[:, 0:64],
                                                                  rhs=pt[:, c, hh, col0:col0 + 128], start=False, stop=(not first and c == 1),
                                                                  skip_group_check=True))
                kb.mm(fns, reads=["Vt", ptk + "c0", ptk + "c1"], writes=["B%d" % bank_id])
            bcur = 4 + (kbk % 2)
            pv(bcur, 0, first=(kbk == 0))
            ba = banks[bcur][:, :].rearrange("p (i n) -> p i n", i=4)
            if d == 1:
                dN = accN[:, :, kbk * 128:(kbk + 1) * 128]
                dD = accD[:, :, kbk * 128:(kbk + 1) * 128]
            else:
                dN = accN.rearrange("p c (m r) -> p c r m", r=d)[:, :, r, kbk * 128:(kbk + 1) * 128]
                dD = accD.rearrange("p c (m r) -> p c r m", r=d)[:, :, r, kbk * 128:(kbk + 1) * 128]
            for c in range(2):
                if g == 0:
                    kb.op("dve", lambda e: e.tensor_copy(out=dN[:, c, :], in_=ba[:, c, :]), reads=["B%d" % bcur], writes=["accN"])
                    kb.op("dve", lambda e: e.tensor_copy(out=dD[:, c, :], in_=ba[:, 2 + c, :]), reads=["B%d" % bcur], writes=["accD"])
                else:
                    kb.op("dve", lambda e: e.tensor_tensor(out=dN[:, c, :], in0=ba[:, c, :], in1=dN[:, c, :], op=ALU.add), reads=["B%d" % bcur, "accN"], writes=["accN"])
                    kb.op("dve", lambda e: e.tensor_tensor(out=dD[:, c, :], in0=ba[:, 2 + c, :], in1=dD[:, c, :], op=ALU.add), reads=["B%d" % bcur, "accD"], writes=["accD"])
            if kbk + 1 < NB:
                pv(4 + ((kbk + 1) % 2), 128, first=True)

        for g in range(3):
            d = DILS[g]
            L = S // d
            NB = L // 128
            kb.dma("qrow", qrow, fm(qn_s[g * 256:(g + 1) * 256, :]), reads=["qn_all"], writes=["qrow"])
            kb.dma("vrow", vrow, fm(v_s[g * 256:(g + 1) * 256, :]), reads=["v_all"], writes=["vrow"])
            for hh in range(2):
                kb.dma("krz%d" % hh, krz[hh * 64:(hh + 1) * 64, hh, :, :],
                       fm(kn_s[g * 256:(g + 1) * 256, :])[hh * 64:(hh + 1) * 64, :, :], reads=["kn_all"], writes=["krz"])
            for c in range(2):
                for b4 in range(NBLK // 4):
                    bid = trr.next()
                    bb = banks[bid][:, :].bitcast(BF16)
                    kb.mm([lambda e, q=q: e.transpose(bb[:, q * 128:(q + 1) * 128], vrow[:, c, (b4 * 4 + q) * 128:(b4 * 4 + q + 1) * 128], ident_b[:, :]) for q in range(4)],
                          reads=["vrow"], writes=["B%d" % bid])
                    kb.op("act", lambda e: e.activation(out=Vt[:, b4 * 4:(b4 + 1) * 4, c, :], in_=bb[:, 0:512].rearrange("p (q m) -> p q m", q=4), func=AF.Copy),
                          reads=["B%d" % bid], writes=["Vt"])
            for r in range(d):
                for kbk in range(NB):
                    base = r * L + kbk * 128
                    gblk = r * NB + kbk
                    N = 256 if kbk < NB - 1 else 128
                    pt, ptk = pts.next()
                    for c in range(2):
                        bid = srr.next()
                        bs = banks[bid][:, :].rearrange("p (h n) -> p h n", h=2)
                        kb.mm([lambda e, hh=hh: e.matmul(bs[:, hh, 0:N], lhsT=krz[:, hh, c, base:base + 128], rhs=qrow[:, c, base:base + N], start=True, stop=True)
                               for hh in range(2)], reads=["krz", "krzz0", "krzz1", "qrow"], writes=["B%d" % bid])
                        kb.op("act", lambda e: e.activation(out=pt[:, c, :, 0:N], in_=bs[:, :, 0:N], func=AF.Exp), reads=["B%d" % bid], writes=[ptk + "c%d" % c])
                        kb.op("dve", lambda e: e.tensor_tensor(out=pt[:, c, :, 0:N], in0=pt[:, c, :, 0:N], in1=mask_b[:, :, 0:N], op=ALU.mult),
                              reads=[ptk + "c%d" % c, "mask_b"], writes=[ptk + "c%d" % c])
                    if qa_pend[0] is not None:
                        qa_b(*qa_pend[0])
                    qa_pend[0] = (g, d, NB, r, kbk, gblk, pt, ptk)
            if qa_pend[0] is not None:
                qa_b(*qa_pend[0])
                qa_pend[0] = None
        HS = S // 2
        for c in range(2):
            for hf in range(2):
                sl = slice(hf * HS, (hf + 1) * HS)
                kd = "accDp%d%d" % (c, hf)
                kb.op("act", lambda e: e.activation(out=accD[:, c, sl], in_=accD[:, c, sl], func=AF.Ln), reads=["accD"], writes=[kd])
                kb.op("act", lambda e: e.activation(out=accD[:, c, sl], in_=accD[:, c, sl], func=AF.Exp, scale=-1.0), reads=[kd], writes=[kd])
                kb.op("dve", lambda e: e.tensor_tensor(out=yrow[:, c, sl], in0=accN[:, c, sl], in1=accD[:, c, sl], op=ALU.mult),
                      reads=["accN", kd, "qrow"], writes=["yr%d%d" % (c, hf)])
            kb.dma("yrow%d" % c, u1_s[512 + c * 128:512 + (c + 1) * 128, :], yrow[:, c, :], reads=["yr%d0" % c, "yr%d1" % c], writes=["u1att%d" % c])
        flush_casts()
        kb.barrier()

    def phase_q3():
        arena.reset()
        uts = Slots("ut", [arena.bf16(6 * TT).rearrange("p (c s) -> p c s", c=6) for _ in range(2)])

        def load_u(t):
            ut, uk = uts.next()
            kb.dma(uk, ut, fm(u1_s, t), reads=["u1all"], writes=[uk])
            return ut, uk

        phase_col(6, load_u, lambda h: h, "ffng1")
        flush_casts()
        kb.barrier()

    plist = [("p0", phase_p0), ("p2", phase_p2), ("p3", phase_p3), ("f1_0", lambda: phase_f1(0)),
             ("f2_0", lambda: phase_f2(final=(n_layers == 1)))]
    if n_layers == 2:
        plist += [("q2", phase_q2), ("qa", phase_qa), ("q3", phase_q3),
                  ("f1_1", lambda: phase_f1(1)), ("f2_1", lambda: phase_f2(final=True))]
    for name, fn in plist:
        with nc.named_scope(name):
            fn()
        if stop_after == name:
            break
    kb.barrier()
    return nc


def host_inputs(inp, b):
    vecs, _ = pack_vecs(inp)
    m = {
        "x": np.ascontiguousarray(np.asarray(inp["x"][b], np.float32)),
        "vecs": vecs,
        "bdA": blockdiag(inp["od_lru_wa"][0]),
        "bdX": blockdiag(inp["od_lru_wx"][0]),
        "ev_w_in": np.ascontiguousarray(inp["ev_w_in"][0], dtype=np.float32),
        "ev_w_out": np.ascontiguousarray(inp["ev_w_out"][0], dtype=np.float32),
        "od_w_in": np.ascontiguousarray(inp["od_w_in"][0], dtype=np.float32),
        "od_w_out": np.ascontiguousarray(inp["od_w_out"][0], dtype=np.float32),
    }
    for l in range(2):
        m["ffn_w_gate%d" % l] = np.ascontiguousarray(inp["ffn_w_gate"][l], dtype=np.float32)
        m["ffn_w_up%d" % l] = np.ascontiguousarray(inp["ffn_w_up"][l], dtype=np.float32)
        m["ffn_w_down%d" % l] = np.ascontiguousarray(inp["ffn_w_down"][l], dtype=np.float32)
    return m


def kernel(**inputs):
    inp = {k: np.asarray(v) for k, v in inputs.items()}
    B, S, _ = inp["x"].shape
    nc = build(S)
    in_maps = [host_inputs(inp, b) for b in range(B)]
    res = run_bass_kernel_spmd(nc, in_maps, core_ids=list(range(B)))
    out = np.stack([np.asarray(r["out"], np.float32) for r in res.results], axis=0)
    return out
```
